# Optimizing a Trainium2 kernel written in Bass

```python
import jax, jax.numpy as jnp
from jax import lax
import numpy as np

D_MODEL = 1024
BATCH = 8
SEQ = 8192
DEPTH = 4

GRID_W = 64
CTX_LEN = 256
D_MIX = D_MODEL
HEAD_DIM = 64
NA_W = D_MIX // 2
NA_HEADS = NA_W // HEAD_DIM
NA_KH = 8
NA_KW = 16
LRU_W = D_MIX // 4
LRU_HEADS = 4
LRU_HD = LRU_W // LRU_HEADS
LRU_CONV = 4
LRU_C = 8.0
FNO_W = D_MIX - NA_W - LRU_W
FNO_GROUPS = 4
FNO_GD = FNO_W // FNO_GROUPS
IN_COLS = 3 * NA_W + 2 * LRU_W + FNO_W
D_FF = ((8 * D_MODEL // 3 + 255) // 256) * 256
N_SUB = 3
N_MOD = 3 * N_SUB
MACARON = 0.5
ALPHA = (2 * DEPTH) ** 0.25
BETA = (8 * DEPTH) ** -0.25
LN_EPS = 1e-5

kernel_name = "hymba_style_na_rglru_fourier_macaron_deepnorm_dit"


def layer_norm(z, g, b):
    mu = jnp.mean(z, axis=-1, keepdims=True)
    var = jnp.mean(jnp.square(z - mu), axis=-1, keepdims=True)
    return (z - mu) * lax.rsqrt(var + LN_EPS) * g + b


def residual_post_norm(x, y, g, b):
    z = ALPHA * x.astype(jnp.float32) + y.astype(jnp.float32)
    return layer_norm(z, g.astype(jnp.float32), b.astype(jnp.float32)).astype(x.dtype)


def modulate(xs, m, j):
    return xs * (1 + m[3 * j + 1]) + m[3 * j]


def swiglu(h, wg, wu, wd):
    return (jax.nn.silu(h @ wg) * (h @ wu)) @ wd


def ffn_sublayer(xs, m, j, wg, wu, wd, g, b):
    y = MACARON * swiglu(modulate(xs, m, j), wg, wu, wd)
    return residual_post_norm(xs, m[3 * j + 2] * y, g, b)


def split_heads(t):
    bsz, n, w = t.shape
    return t.reshape(bsz, n, w // HEAD_DIM, HEAD_DIM).transpose(0, 2, 1, 3)


def merge_heads(t):
    bsz, h, n, d = t.shape
    return t.transpose(0, 2, 1, 3).reshape(bsz, n, h * d)


def context_attention(qc, kc, vc):
    s = jnp.einsum('bhqd,bhkd->bhqk', qc, kc).astype(jnp.float32)
    p = jax.nn.softmax(s, axis=-1).astype(vc.dtype)
    return jnp.einsum('bhqk,bhkd->bhqd', p, vc)


def neighbourhood_attention(q, k, v, kc, vc, rpb):
    bsz, nh, s_len, hd = q.shape
    rows = s_len // GRID_W
    kh = min(NA_KH, rows)
    qg = q.reshape(bsz, nh, rows, GRID_W, hd)
    kg = k.reshape(bsz, nh, rows, GRID_W, hd)
    vg = v.reshape(bsz, nh, rows, GRID_W, hd)
    col0 = np.clip(np.arange(GRID_W) - NA_KW // 2, 0, GRID_W - NA_KW)
    cols = col0[:, None] + np.arange(NA_KW)[None, :]
    dc_idx = cols - np.arange(GRID_W)[:, None] + (NA_KW - 1)
    n_loc = kh * NA_KW

    def row_fn(r):
        rs = jnp.clip(r - kh // 2, 0, rows - kh)
        kb = lax.dynamic_slice_in_dim(kg, rs, kh, axis=2)
        vb = lax.dynamic_slice_in_dim(vg, rs, kh, axis=2)
        kw = kb[:, :, :, cols, :]
        vw = vb[:, :, :, cols, :]
        qr = lax.dynamic_index_in_dim(qg, r, axis=2, keepdims=False)
        s_loc = jnp.einsum('bhqd,bhiqjd->bhqij', qr, kw).astype(jnp.float32)
        dr_idx = rs + jnp.arange(kh) - r + (NA_KH - 1)
        bias = rpb[:, dr_idx][:, :, dc_idx].transpose(0, 2, 1, 3)
        s_loc = s_loc + bias[None].astype(jnp.float32)
        s_ctx = jnp.einsum('bhqd,bhkd->bhqk', qr, kc).astype(jnp.float32)
        s = jnp.concatenate([s_loc.reshape(bsz, nh, GRID_W, n_loc), s_ctx], axis=-1)
        p = jax.nn.softmax(s, axis=-1)
        p_loc = p[..., :n_loc].reshape(bsz, nh, GRID_W, kh, NA_KW).astype(vw.dtype)
        p_ctx = p[..., n_loc:].astype(vc.dtype)
        return (jnp.einsum('bhqij,bhiqjd->bhqd', p_loc, vw)
                + jnp.einsum('bhqk,bhkd->bhqd', p_ctx, vc))

    out = lax.map(row_fn, jnp.arange(rows))
    return out.transpose(1, 2, 0, 3, 4).reshape(bsz, nh, s_len, hd)


def conv_centred(x, w, b):
    n = x.shape[1]
    left = LRU_CONV // 2
    xp = jnp.pad(x, ((0, 0), (left, LRU_CONV - 1 - left), (0, 0)))
    y = xp[:, 0:n] * w[0]
    for t in range(1, LRU_CONV):
        y = y + xp[:, t:t + n] * w[t]
    return y + b


def rglru_coeffs(x, wa, ba, wx, bx, lam):
    bsz, n, _ = x.shape
    xb = x.reshape(bsz, n, LRU_HEADS, LRU_HD)
    r = jax.nn.sigmoid(jnp.einsum('bnhc,hcd->bnhd', xb, wa).reshape(bsz, n, LRU_W) + ba)
    i = jax.nn.sigmoid(jnp.einsum('bnhc,hcd->bnhd', xb, wx).reshape(bsz, n, LRU_W) + bx)
    log_a = -LRU_C * r * jax.nn.softplus(-lam)
    a = jnp.exp(log_a)
    bterm = jnp.sqrt(-jnp.expm1(2.0 * log_a)) * (i * x)
    return a, bterm


def linear_scan(a, b, h0, reverse):
    def combine(e1, e2):
        a1, b1 = e1
        a2, b2 = e2
        return a1 * a2, a2 * b1 + b2
    a_cum, b_cum = lax.associative_scan(combine, (a, b), axis=1, reverse=reverse)
    return b_cum + a_cum * h0[:, None, :]


def rglru_bidirectional(x_lat, x_ctx, wa, ba, wx, bx, lam):
    outs_lat, outs_ctx = [], []
    for d, reverse in enumerate((False, True)):
        f32 = jnp.float32
        pa = (wa[d].astype(f32), ba[d].astype(f32), wx[d].astype(f32), bx[d].astype(f32), lam[d].astype(f32))
        a_c, b_c = rglru_coeffs(x_ctx, *pa)
        h_c = linear_scan(a_c, b_c, jnp.zeros((x_ctx.shape[0], LRU_W), f32), reverse)
        h_end = h_c[:, 0] if reverse else h_c[:, -1]
        a_l, b_l = rglru_coeffs(x_lat, *pa)
        outs_lat.append(linear_scan(a_l, b_l, h_end, reverse))
        outs_ctx.append(h_c)
    return outs_lat[0] + outs_lat[1], outs_ctx[0] + outs_ctx[1]


def fourier_mix(f, fw):
    bsz, n, _ = f.shape
    fb = f.astype(jnp.float32).reshape(bsz, n, FNO_GROUPS, FNO_GD)
    y = jnp.fft.fft2(fb, axes=(1, 3), norm='ortho').real
    y = jnp.einsum('bngc,gcd->bngd', y, fw.astype(jnp.float32))
    return y.reshape(bsz, n, FNO_W)


def hybrid_mixer(h_lat, h_ctx, w_in, w_out, rpb, conv_w, conv_b, wa, ba, wx, bx, lam, fw, need_ctx):
    dt = h_lat.dtype
    cuts = [NA_W, 2 * NA_W, 3 * NA_W, 3 * NA_W + LRU_W, 3 * NA_W + 2 * LRU_W]
    q, k, v, xr, gr, f = jnp.split(h_lat @ w_in, cuts, axis=-1)
    qc, kc, vc, xrc, grc, fc = jnp.split(h_ctx @ w_in, cuts, axis=-1)
    scale = HEAD_DIM ** -0.5
    kch, vch = split_heads(kc), split_heads(vc)
    na_lat = merge_heads(neighbourhood_attention(split_heads(q) * scale, split_heads(k), split_heads(v), kch, vch, rpb))
    xl = conv_centred(xr, conv_w, conv_b).astype(jnp.float32)
    xc = conv_centred(xrc, conv_w, conv_b).astype(jnp.float32)
    hl, hc = rglru_bidirectional(xl, xc, wa, ba, wx, bx, lam)
    lru_lat = (hl * jax.nn.gelu(gr.astype(jnp.float32))).astype(dt)
    fno_lat = fourier_mix(f, fw).astype(dt)
    y_lat = jnp.concatenate([na_lat, lru_lat, fno_lat], axis=-1) @ w_out
    if not need_ctx:
        return y_lat, None
    na_ctx = merge_heads(context_attention(split_heads(qc) * scale, kch, vch))
    lru_ctx = (hc * jax.nn.gelu(grc.astype(jnp.float32))).astype(dt)
    fno_ctx = fourier_mix(fc, fw).astype(dt)
    y_ctx = jnp.concatenate([na_ctx, lru_ctx, fno_ctx], axis=-1) @ w_out
    return y_lat, y_ctx


def setup_inputs(seed: int = 0) -> dict:
    key = jax.random.key(seed)
    ks = jax.random.split(key, 32)
    f32 = jnp.float32

    def nrm(k, shape, s):
        return s * jax.random.normal(k, shape, f32)

    a0 = jax.random.uniform(ks[22], (DEPTH, 2, LRU_W), f32, 0.9, 0.999)
    s0 = a0 ** (1.0 / LRU_C)
    lru_lambda = jnp.log(s0) - jnp.log1p(-s0)
    return {
        "x": nrm(ks[0], (BATCH, SEQ, D_MODEL), 1.0),
        "c": nrm(ks[1], (BATCH, D_MODEL), 1.0),
        "ctx": nrm(ks[2], (BATCH, CTX_LEN, D_MODEL), 1.0),
        "c_ctx": nrm(ks[3], (D_MODEL,), 1.0),
        "w_ada": nrm(ks[4], (DEPTH, D_MODEL, N_MOD * D_MODEL), 0.5 * D_MODEL ** -0.5),
        "b_ada": nrm(ks[5], (DEPTH, N_MOD * D_MODEL), 0.01),
        "ln_g": 1.0 + nrm(ks[6], (DEPTH, N_SUB, D_MODEL), 0.02),
        "ln_b": nrm(ks[7], (DEPTH, N_SUB, D_MODEL), 0.02),
        "ff1_gate": nrm(ks[8], (DEPTH, D_MODEL, D_FF), D_MODEL ** -0.5),
        "ff1_up": nrm(ks[9], (DEPTH, D_MODEL, D_FF), D_MODEL ** -0.5),
        "ff1_down": nrm(ks[10], (DEPTH, D_FF, D_MODEL), BETA * D_FF ** -0.5),
        "ff2_gate": nrm(ks[11], (DEPTH, D_MODEL, D_FF), D_MODEL ** -0.5),
        "ff2_up": nrm(ks[12], (DEPTH, D_MODEL, D_FF), D_MODEL ** -0.5),
        "ff2_down": nrm(ks[13], (DEPTH, D_FF, D_MODEL), BETA * D_FF ** -0.5),
        "w_in": nrm(ks[14], (DEPTH, D_MODEL, IN_COLS), D_MODEL ** -0.5),
        "w_out": nrm(ks[15], (DEPTH, D_MIX, D_MODEL), BETA * D_MIX ** -0.5),
        "na_rpb": nrm(ks[16], (DEPTH, NA_HEADS, 2 * NA_KH - 1, 2 * NA_KW - 1), 0.02),
        "lru_conv_w": nrm(ks[17], (DEPTH, LRU_CONV, LRU_W), LRU_CONV ** -0.5),
        "lru_conv_b": nrm(ks[18], (DEPTH, LRU_W), 0.01),
        "lru_wa": nrm(ks[19], (DEPTH, 2, LRU_HEADS, LRU_HD, LRU_HD), LRU_HD ** -0.5),
        "lru_ba": nrm(ks[20], (DEPTH, 2, LRU_W), 0.01),
        "lru_wx": nrm(ks[21], (DEPTH, 2, LRU_HEADS, LRU_HD, LRU_HD), LRU_HD ** -0.5),
        "lru_bx": nrm(ks[23], (DEPTH, 2, LRU_W), 0.01),
        "lru_lambda": lru_lambda,
        "fno_w": nrm(ks[24], (DEPTH, FNO_GROUPS, FNO_GD, FNO_GD), FNO_GD ** -0.5),
    }


def reference(x, c, ctx, c_ctx, w_ada, b_ada, ln_g, ln_b, ff1_gate, ff1_up, ff1_down, ff2_gate, ff2_up, ff2_down,
              w_in, w_out, na_rpb, lru_conv_w, lru_conv_b, lru_wa, lru_ba, lru_wx, lru_bx, lru_lambda, fno_w):
    bsz = x.shape[0]
    x_lat, x_ctx = x, ctx
    for l in range(DEPTH):
        last = l == DEPTH - 1
        m_lat = (jax.nn.silu(c) @ w_ada[l] + b_ada[l]).reshape(bsz, N_MOD, D_MODEL).transpose(1, 0, 2)[:, :, None, :]
        m_ctx = (jax.nn.silu(c_ctx) @ w_ada[l] + b_ada[l]).reshape(N_MOD, D_MODEL)
        x_lat = ffn_sublayer(x_lat, m_lat, 0, ff1_gate[l], ff1_up[l], ff1_down[l], ln_g[l, 0], ln_b[l, 0])
        x_ctx = ffn_sublayer(x_ctx, m_ctx, 0, ff1_gate[l], ff1_up[l], ff1_down[l], ln_g[l, 0], ln_b[l, 0])
        y_lat, y_ctx = hybrid_mixer(modulate(x_lat, m_lat, 1), modulate(x_ctx, m_ctx, 1),
                                    w_in[l], w_out[l], na_rpb[l], lru_conv_w[l], lru_conv_b[l],
                                    lru_wa[l], lru_ba[l], lru_wx[l], lru_bx[l], lru_lambda[l], fno_w[l],
                                    not last)
        x_lat = residual_post_norm(x_lat, m_lat[5] * y_lat, ln_g[l, 1], ln_b[l, 1])
        x_lat = ffn_sublayer(x_lat, m_lat, 2, ff2_gate[l], ff2_up[l], ff2_down[l], ln_g[l, 2], ln_b[l, 2])
        if not last:
            x_ctx = residual_post_norm(x_ctx, m_ctx[5] * y_ctx, ln_g[l, 1], ln_b[l, 1])
            x_ctx = ffn_sublayer(x_ctx, m_ctx, 2, ff2_gate[l], ff2_up[l], ff2_down[l], ln_g[l, 2], ln_b[l, 2])
    return x_lat
```

```python
import contextlib
import numpy as np
import concourse.bass as bass
import concourse.mybir as mybir
from concourse.bass_utils import run_bass_kernel_spmd

F32 = mybir.dt.float32
BF16 = mybir.dt.bfloat16
I32 = mybir.dt.int32
AF = mybir.ActivationFunctionType
ALU = mybir.AluOpType

D = 1024
DEPTH = 4
NCTX = 256
NLAT = 8192
T = NCTX + NLAT
DFF = 2816
NJ = DFF // 128
ALPHA = (2 * DEPTH) ** 0.25
EPS_P = 1e-5 / (ALPHA * ALPHA)
NEG = -30000.0

COMPUTE = ("pe", "act", "dve", "pool")


class Prog:
    def __init__(self, nc):
        self.nc = nc
        self.ops = []
        self.barriers = set()
        self.sb_base = 16512
        self.sb_top = 229344
        self.sb_off = self.sb_base
        self.sb_max = 0
        self._n = 0

    def sb(self, name, shape, dtype):
        sz = int(np.prod(shape[1:])) * mybir.dt.size(dtype)
        off = (self.sb_off + 63) // 64 * 64
        self._n += 1
        assert off + sz <= self.sb_top, (name, off + sz, self.sb_top)
        t = self.nc.alloc_sbuf_tensor_at(f"{name}_{self._n}", list(shape), dtype, offset=off)
        self.sb_off = off + sz
        self.sb_max = max(self.sb_max, self.sb_off)
        return t

    def mark(self):
        return self.sb_off

    def release(self, m):
        self.sb_off = m

    def op(self, eng, fn, r=(), w=(), stream=None):
        self.ops.append([eng, fn, tuple(r), tuple(w), stream])

    def dma(self, eng, out, in_, r=(), w=(), stream=None, **kw):
        assert stream is not None
        self.op(eng, lambda e: e.dma_start(out=out, in_=in_, **kw), r, w, stream)

    def barrier(self):
        self.barriers.add(len(self.ops))

    def emit(self):
        nc = self.nc
        ops = self.ops
        n = len(ops)
        last_w, readers = {}, {}
        deps = [None] * n
        eng_idx = [0] * n
        eng_count, last_on_eng, last_on_stream, pending = {}, {}, {}, {}
        for i, (eng, fn, r, w, stream) in enumerate(ops):
            if i in self.barriers:
                bd = set(last_on_eng.values()) | set(last_on_stream.values())
                for e in COMPUTE + ("sp",):
                    pending.setdefault(e, set()).update(bd)
            d = set()
            if eng in pending:
                d |= pending.pop(eng)
            for k in r:
                if k in last_w:
                    d.add(last_w[k])
            for k in w:
                if k in last_w:
                    d.add(last_w[k])
                d.update(readers.get(k, ()))
            if stream is not None and stream in last_on_stream:
                d.add(last_on_stream[stream])
            d.discard(i)
            best = {}
            d2 = set()
            for jx in d:
                if ops[jx][4] is None:
                    ej = ops[jx][0]
                    if ej not in best or jx > best[ej]:
                        best[ej] = jx
                else:
                    d2.add(jx)
            d2.update(best.values())
            deps[i] = d2
            for k in r:
                readers.setdefault(k, []).append(i)
            for k in w:
                last_w[k] = i
                readers[k] = []
            eng_idx[i] = eng_count.get(eng, 0)
            eng_count[eng] = eng_idx[i] + 1
            if stream is None:
                last_on_eng[eng] = i
            else:
                last_on_stream[stream] = i
        need_sig = [False] * n
        for i in range(n):
            eng = ops[i][0]
            keep = set()
            for j in deps[i]:
                ej, sj = ops[j][0], ops[j][4]
                if sj is None and ej == eng:
                    if eng == "pe":
                        continue
                    if ops[i][4] is None and eng_idx[i] - eng_idx[j] >= 3:
                        continue
                keep.add(j)
                need_sig[j] = True
            deps[i] = keep
        sigval = [0] * n
        cnt = {}
        for i in range(n):
            eng, _, _, _, stream = ops[i]
            if stream is not None:
                key = ("dma", stream)
                cnt[key] = cnt.get(key, 0) + 16
                sigval[i] = cnt[key]
            elif need_sig[i]:
                key = ("eng", eng)
                cnt[key] = cnt.get(key, 0) + 1
                sigval[i] = cnt[key]
        waits = [None] * n
        seen = {}
        for i in range(n):
            eng = ops[i][0]
            need = {}
            for j in deps[i]:
                ej, sj = ops[j][0], ops[j][4]
                key = ("dma", sj) if sj is not None else ("eng", ej)
                need[key] = max(need.get(key, 0), sigval[j])
            wl = []
            for key, v in need.items():
                if seen.get((eng, key), 0) >= v:
                    continue
                seen[(eng, key)] = v
                wl.append((key, v))
            waits[i] = wl
        keys = list(cnt.keys())
        self.n_sems = len(keys)
        self.cnt = cnt
        self.plan = (deps, sigval, need_sig, waits)
        with contextlib.ExitStack() as es:
            sems = {}
            for k in keys:
                sems[k] = es.enter_context(nc.semaphore(f"s{len(sems)}"))
            block = es.enter_context(nc.Block())
            per_eng = {}
            for i, o in enumerate(ops):
                per_eng.setdefault(o[0], []).append(i)

            def run(eng_name, e):
                for i in per_eng.get(eng_name, []):
                    _, fn, _, _, stream = ops[i]
                    for key, v in waits[i]:
                        e.wait_ge(sems[key], v)
                    ins = fn(e)
                    if stream is not None:
                        ins.then_inc(sems[("dma", stream)], 16)
                    elif need_sig[i]:
                        ins.then_inc(sems[("eng", eng_name)], 1)
                if eng_name == "sp":
                    for k in keys:
                        e.wait_ge(sems[k], cnt[k])

            @block.sync
            def _(e):
                run("sp", e)

            @block.tensor
            def _(e):
                run("pe", e)

            @block.scalar
            def _(e):
                run("act", e)

            @block.vector
            def _(e):
                run("dve", e)

            @block.gpsimd
            def _(e):
                run("pool", e)
        return nc


class Ctx:
    pass


def PS(i):
    return ("ps", i)


def declare(nc, C, n_layers, dbg):
    def inp(name, shape, dt=F32):
        return nc.dram_tensor(name, list(shape), dt, kind="ExternalInput").ap()

    C.xin = inp("xin", [D, T])
    C.cvec = inp("cvec", [128, 16])
    C.bada = inp("bada", [128, DEPTH * 144])
    C.lng = inp("lng", [128, 96])
    C.lnb = inp("lnb", [128, 96])
    C.ident = inp("ident", [128, 128])
    C.w_ada = inp("w_ada", [DEPTH, D, 9 * D])
    C.ffg = [inp("ff1_gate", [DEPTH, D, DFF]), inp("ff2_gate", [DEPTH, D, DFF])]
    C.ffu = [inp("ff1_up", [DEPTH, D, DFF]), inp("ff2_up", [DEPTH, D, DFF])]
    C.ffd = [inp("ff1_down", [DEPTH, DFF, D]), inp("ff2_down", [DEPTH, DFF, D])]
    C.w_in = inp("w_in", [DEPTH, D, 2304])
    C.w_out = inp("w_out", [DEPTH, D, D])
    C.lruv = inp("lruv", [128, DEPTH * 2 * 11])
    C.lru_wa = inp("lru_wa", [DEPTH, 2, 4, 64, 64])
    C.lru_wx = inp("lru_wx", [DEPTH, 2, 4, 64, 64])
    C.fno_w = inp("fno_w", [DEPTH, 4, 64, 64])
    C.c64bd = inp("c64bd", [128, 128])
    C.s64bd = inp("s64bd", [128, 128])
    C.cw128 = inp("cw128", [128, 128])
    C.sw128 = inp("sw128", [128, 128])
    C.nsw128 = inp("nsw128", [128, 128])
    C.wtC = inp("wtC", [128, 128 * 64])
    C.t256 = inp("t256", [128, 2 * 2 * 256])
    C.rpbT = inp("rpbT", [DEPTH, 8, 128, 1920])
    C.cmask = inp("cmask", [2, 128, 1920])
    C.rmask = inp("rmask", [2, 128, 8 * 512])
    C.out = nc.dram_tensor("out", [D, NLAT], F32, kind="ExternalOutput").ap()
    sk = "ExternalOutput" if dbg else "Internal"
    C.QK = nc.dram_tensor("QK", [1024, T], BF16, kind=sk).ap()
    C.XG = nc.dram_tensor("XG", [512, T], F32, kind=sk).ap()
    C.VA = nc.dram_tensor("VA", [T, 1024], BF16, kind=sk).ap()
    C.G = nc.dram_tensor("G", [T, 512], BF16, kind=sk).ap()
    C.AB = nc.dram_tensor("AB", [2, 128, 64, 256], BF16, kind=sk).ap()
    C.CAT = nc.dram_tensor("CAT", [1024, T], BF16, kind=sk).ap()
    C.XT = nc.dram_tensor("XT", [D, T], F32, kind="Internal").ap()
    if dbg:
        C.dbg = nc.dram_tensor("dbg", [D, T], F32, kind="ExternalOutput").ap()
        C.dbg2 = nc.dram_tensor("dbg2", [128, DEPTH * 144], F32, kind="ExternalOutput").ap()
    C.psb = [nc.alloc_psum_tensor(f"psb{i}", [128, 512], F32) for i in range(8)]


def tiles_all():
    tl = [(0, NCTX, 1)]
    for t in range(NLAT // 512):
        tl.append((NCTX + 512 * t, 512, 0))
    return tl


def prologue(P, C, n_layers):
    nc = P.nc
    C.identb = P.sb("identb", [128, 128], BF16)
    C.onesb = P.sb("onesb", [128, 128], BF16)
    C.M = P.sb("M", [128, DEPTH * 144], F32)
    C.S1P = P.sb("S1P", [128, DEPTH * 48], F32)
    C.GS = P.sb("GS", [128, DEPTH * 48], F32)
    C.LNG = P.sb("LNG", [128, 96], F32)
    C.LNB = P.sb("LNB", [128, 96], F32)
    C.epsc = P.sb("epsc", [128, 1], F32)
    C.zc = P.sb("zc", [128, 1], F32)
    P.op("dve", lambda e: e.memset(C.zc[:], 0.0), w=["zc"])
    P.dma("pool", C.identb[:], C.ident, w=["identb"], stream="c0")
    P.dma("sp", C.LNG[:], C.lng, w=["LNG"], stream="c1")
    P.dma("sp", C.LNB[:], C.lnb, w=["LNB"], stream="c2")
    P.op("dve", lambda e: e.memset(C.onesb[:], 1.0 / D), w=["onesb"])
    P.op("dve", lambda e: e.memset(C.epsc[:], EPS_P), w=["epsc"])
    C.scsb = P.sb("scsb", [128, 16], BF16)
    mk = P.mark()
    cs = P.sb("cs", [128, 16], F32)
    P.dma("sp", cs[:], C.cvec, w=["cs"], stream="c3")
    P.op("act", lambda e: e.activation(out=C.scsb[:], in_=cs[:], func=AF.Silu), r=["cs"], w=["scsb"])
    bufs = adaln_alloc(P)
    for j9 in range(10):
        adaln_step(P, C, 0, bufs, j9, 7)
    P.barrier()
    P.release(mk)


def adaln_alloc(P):
    wa = [P.sb(f"wa{i}", [128, 8, D], BF16) for i in range(2)]
    bad = P.sb("bad", [128, 144], F32)
    return wa, bad


def adaln_step(P, C, l, bufs, j9, bank):
    wa, bad = bufs
    ps = C.psb[bank]
    if j9 < 9:
        buf = wa[j9 % 2]
        bk = ("wa", j9 % 2)
        if j9 == 0:
            P.dma("sp", bad[:], C.bada[:, l * 144:(l + 1) * 144], w=["bad"], stream="c4")
        src = C.w_ada[l, :, j9 * D:(j9 + 1) * D].rearrange("(k p) n -> p k n", p=128)
        P.dma("pool", buf[:], src, w=[bk], stream=f"wa{j9 % 2}")
        for mo in range(8):
            col = (j9 * 8 + mo) * 2
            for k in range(8):
                P.op("pe", lambda e, k=k, mo=mo, col=col: e.matmul(
                    ps[:, col:col + 2], lhsT=buf[:, k, mo * 128:(mo + 1) * 128], rhs=C.scsb[:, 2 * k:2 * k + 2],
                    start=(k == 0), stop=(k == 7)), r=[bk, "scsb"], w=[PS(bank)])
        return
    P.op("dve", lambda e: e.tensor_tensor(out=C.M[:, l * 144:(l + 1) * 144], in0=ps[:, 0:144], in1=bad[:, :],
                                          op=ALU.add), r=["bad"], w=[PS(bank), "M"])
    for j in range(3):
        gc = (0.5 if j != 1 else 1.0) / ALPHA
        P.op("dve", lambda e, j=j: e.tensor_scalar(
            out=C.S1P[:, (l * 3 + j) * 16:(l * 3 + j + 1) * 16],
            in0=C.M[:, l * 144 + (3 * j + 1) * 16: l * 144 + (3 * j + 2) * 16],
            scalar1=1.0, scalar2=None, op0=ALU.add), r=["M"], w=["S1P"])
        P.op("dve", lambda e, j=j, gc=gc: e.tensor_scalar(
            out=C.GS[:, (l * 3 + j) * 16:(l * 3 + j + 1) * 16],
            in0=C.M[:, l * 144 + (3 * j + 2) * 16: l * 144 + (3 * j + 3) * 16],
            scalar1=gc, scalar2=None, op0=ALU.mult), r=["M"], w=["GS"])


def mod_aps(C, l, j, m, s):
    i = ((l * 3 + j) * 8 + m) * 2 + s
    sh = l * 144 + ((3 * j) * 8 + m) * 2 + s
    return C.S1P[:, i:i + 1], C.M[:, sh:sh + 1], C.GS[:, i:i + 1]


def ln_tail(P, C, l, j, xb, xkey, n, ps_mean, ps_ex2, st, pfx, inter=None, every=False):
    msq = st
    kmean, kex2 = PS(ps_mean), PS(ps_ex2)
    pm, pe2 = C.psb[ps_mean], C.psb[ps_ex2]
    inter = list(inter) if inter else []

    def nxt():
        if inter:
            inter.pop(0)()

    nxt()
    P.op("act", lambda e: e.activation(out=msq[:, :n], in_=pm[:, :n], func=AF.Square), w=[kmean, pfx + "msq"])
    P.op("dve", lambda e: e.tensor_tensor(out=msq[:, :n], in0=pe2[:, :n], in1=msq[:, :n], op=ALU.subtract),
         w=[kex2, pfx + "msq"])
    P.op("act", lambda e: e.activation(out=msq[:, :n], in_=msq[:, :n], func=AF.Sqrt, bias=C.epsc[:, 0:1]),
         r=["epsc"], w=[pfx + "msq"])
    P.op("dve", lambda e: e.reciprocal(out=msq[:, :n], in_=msq[:, :n]), w=[pfx + "msq"])
    nxt()
    for m in range(8):
        g = C.LNG[:, (l * 3 + j) * 8 + m:(l * 3 + j) * 8 + m + 1]
        b = C.LNB[:, (l * 3 + j) * 8 + m:(l * 3 + j) * 8 + m + 1]
        P.op("dve", (lambda m=m: lambda e: e.tensor_tensor(out=xb[:, m, :n], in0=xb[:, m, :n], in1=pm[:, :n],
                                                           op=ALU.subtract))(), w=[kmean, (xkey, m)])
        P.op("dve", (lambda m=m: lambda e: e.tensor_tensor(out=xb[:, m, :n], in0=xb[:, m, :n], in1=msq[:, :n],
                                                           op=ALU.mult))(), r=[pfx + "msq"], w=[(xkey, m)])
        P.op("act", (lambda m=m, g=g, b=b: lambda e: e.activation(out=xb[:, m, :n], in_=xb[:, m, :n],
                                                                  func=AF.Identity, scale=g, bias=b))(),
             r=["LNG", "LNB"], w=[(xkey, m)])
        if every or m % 2 == 1:
            nxt()
    while inter:
        nxt()


def ffn_phase(P, C, l, which, src, dst_fn, tiles):
    j = 0 if which == 0 else 2
    P.barrier()
    mk = P.mark()
    wg = P.sb("wg", [128, 8, DFF], BF16)
    wu = P.sb("wu", [128, 8, DFF], BF16)
    wd = P.sb("wd", [128, NJ, D], BF16)
    Wg = C.ffg[which][l].rearrange("(k p) n -> p k n", p=128)
    Wu = C.ffu[which][l].rearrange("(k p) n -> p k n", p=128)
    Wd = C.ffd[which][l].rearrange("(k p) n -> p k n", p=128)
    for k in range(8):
        P.dma("pool", wg[:, k, :], Wg[:, k, :], w=[("wg", k)], stream=f"w{k % 4}")
        P.dma("pool", wu[:, k, :], Wu[:, k, :], w=[("wu", k)], stream=f"w{4 + k % 4}")
    for k in range(0, NJ, 2):
        P.dma("pool", wd[:, k:k + 2, :], Wd[:, k:k + 2, :], w=[("wd", k), ("wd", k + 1)], stream=f"w{8 + (k // 2) % 4}")
    xb = [P.sb(f"xb{i}", [128, 8, 512], F32) for i in range(2)]
    h = P.sb("h", [128, 8, 512], BF16)
    a = P.sb("a", [128, NJ, 512], BF16)
    sg = [P.sb(f"sg{i}", [128, 512], BF16) for i in range(2)]
    zb = [P.sb(f"zb{i}", [128, 512], BF16) for i in range(2)]
    zq = [P.sb(f"zq{i}", [128, 512], BF16) for i in range(2)]
    st = P.sb("msq", [128, 512], F32)
    srcv = src.rearrange("(m p) t -> p m t", p=128)

    def load(ti):
        t0, n, s = tiles[ti]
        P.dma("sp", xb[ti % 2][:, :, :n], srcv[:, :, t0:t0 + n], w=[(f"xb{ti % 2}", m) for m in range(8)],
              stream=f"xl{ti % 2}")

    def modulate(ti):
        t0, n, s = tiles[ti]
        X = xb[ti % 2]
        xk = f"xb{ti % 2}"
        for m in range(8):
            s1p, sh, gs = mod_aps(C, l, j, m, s)
            P.op("act", lambda e, m=m, s1p=s1p, sh=sh: e.activation(
                out=h[:, m, :n], in_=X[:, m, :n], func=AF.Identity, scale=s1p, bias=sh),
                r=[(xk, m), "S1P", "M"], w=[("h", m)])

    KPRE = 9

    def gate_up(ti, jlo, jhi):
        t0, n, s = tiles[ti]
        for jj in range(jlo, jhi):
            pg, pu = jj % 2, 2 + jj % 2
            for k in range(8):
                P.op("pe", lambda e, k=k, jj=jj, pg=pg: e.matmul(
                    C.psb[pg][:, :n], lhsT=wg[:, k, jj * 128:(jj + 1) * 128], rhs=h[:, k, :n],
                    start=(k == 0), stop=(k == 7)), r=[("wg", k), ("h", k)], w=[PS(pg)])
            for k in range(8):
                P.op("pe", lambda e, k=k, jj=jj, pu=pu: e.matmul(
                    C.psb[pu][:, :n], lhsT=wu[:, k, jj * 128:(jj + 1) * 128], rhs=h[:, k, :n],
                    start=(k == 0), stop=(k == 7)), r=[("wu", k), ("h", k)], w=[PS(pu)])
            P.op("act", lambda e, jj=jj, pg=pg: e.activation(
                out=sg[jj % 2][:, :n], in_=C.psb[pg][:, :n], func=AF.Silu), w=[PS(pg), ("sg", jj % 2)])
            P.op("dve", lambda e, jj=jj, pu=pu: e.tensor_tensor(
                out=a[:, jj, :n], in0=C.psb[pu][:, :n], in1=sg[jj % 2][:, :n], op=ALU.mult),
                r=[("sg", jj % 2)], w=[PS(pu), ("a", jj)])

    def tile_body(ti, t0, n, s):
        X = xb[ti % 2]
        xk = f"xb{ti % 2}"
        gate_up(ti, 0 if ti == 0 else KPRE, NJ)
        if ti + 1 < len(tiles):
            modulate(ti + 1)

        def y_mms(m, py):
            for jj in range(NJ):
                P.op("pe", lambda e, jj=jj: e.matmul(
                    C.psb[py][:, :n], lhsT=wd[:, jj, m * 128:(m + 1) * 128], rhs=a[:, jj, :n],
                    start=(jj == 0), stop=(jj == NJ - 1)), r=[("wd", jj), ("a", jj)], w=[PS(py)])

        inter = []
        if ti + 1 < len(tiles):
            inter = [(lambda q=q: gate_up(ti + 1, q, q + 1)) for q in range(KPRE)]
        resid_ln_part(P, C, l, j, X, xk, n, s, zb, zq, st, y_mms, inter)
        dstv = dst_fn(ti)
        if dstv is not None:
            P.dma("sp", dstv, X[:, :, :n], r=[(xk, m) for m in range(8)], stream=f"xs{ti % 2}")

    load(0)
    modulate(0)
    for ti, (t0, n, s) in enumerate(tiles):
        if ti + 1 < len(tiles):
            load(ti + 1)
        tile_body(ti, t0, n, s)
    P.release(mk)


def resid_ln_part(P, C, l, j, X, xk, n, s, zb, zq, msq, y_mms, before_ln=None):
    def stats(m):
        P.op("pe", lambda e: e.matmul(C.psb[6][:, :n], lhsT=C.onesb[:, :], rhs=zb[m % 2][:, :n],
                                      start=(m == 0), stop=(m == 7)), r=["onesb", ("zb", m % 2)], w=[PS(6)])
        P.op("pe", lambda e: e.matmul(C.psb[7][:, :n], lhsT=C.onesb[:, :], rhs=zq[m % 2][:, :n],
                                      start=(m == 0), stop=(m == 7)), r=["onesb", ("zq", m % 2)], w=[PS(7)])

    for m in range(8):
        py = 4 + m % 2
        s1p, sh, gs = mod_aps(C, l, j, m, s)
        y_mms(m, py)

        def ep(m=m, py=py, gs=gs):
            P.op("dve", lambda e: e.scalar_tensor_tensor(
                out=X[:, m, :n], in0=C.psb[py][:, :n], scalar=gs, in1=X[:, m, :n], op0=ALU.mult, op1=ALU.add),
                r=["GS"], w=[PS(py), (xk, m)])
            P.op("act", lambda e: e.activation(out=zb[m % 2][:, :n], in_=X[:, m, :n], func=AF.Identity),
                 r=[(xk, m)], w=[("zb", m % 2)])
            P.op("act", lambda e: e.activation(out=zq[m % 2][:, :n], in_=X[:, m, :n], func=AF.Square),
                 r=[(xk, m)], w=[("zq", m % 2)])
        ep()
        if m > 0:
            stats(m - 1)
    stats(7)
    ln_tail(P, C, l, j, X, xk, n, 6, 7, msq, "f", before_ln, every=True)


def inproj_phase(P, C, l, tiles, n_layers=DEPTH):
    j = 1
    P.barrier()
    mk = P.mark()
    win = P.sb("win", [128, 8, 2304], BF16)
    Wv = C.w_in[l].rearrange("(k p) n -> p k n", p=128)
    for k in range(8):
        P.dma("pool", win[:, k, :], Wv[:, k, :], w=[("win", k)], stream=f"w{k % 4}")
    fw = P.sb("fw", [128, 2, 64], F32)
    cbd = P.sb("cbd", [128, 128], F32)
    sbd = P.sb("sbd", [128, 128], F32)
    BD = P.sb("BD", [128, 2, 512], BF16)
    P.dma("sp", fw[:], C.fno_w[l].rearrange("(kc g2) c j -> (g2 c) kc j", g2=2), w=["fw"], stream="m0")
    P.dma("sp", cbd[:], C.c64bd, w=["cbd"], stream="m1")
    P.dma("sp", sbd[:], C.s64bd, w=["sbd"], stream="m2")
    P.op("pool", lambda e: e.memset(BD[:], 0.0), w=["BD"])
    for kc in range(2):
        for ti_, tab in enumerate((cbd, sbd)):
            def bdm(kc=kc, ti_=ti_, tab=tab):
                tk = "cbd" if ti_ == 0 else "sbd"
                P.op("pe", lambda e: e.matmul(C.psb[0][:, 0:64], lhsT=tab[:, :], rhs=fw[:, kc, :], start=True, stop=True),
                     r=[tk, "fw"], w=[PS(0)])
                for g2 in range(2):
                    c0 = ti_ * 256 + (2 * kc + g2) * 64
                    P.op("dve", lambda e, g2=g2, c0=c0: e.tensor_copy(
                        out=BD[g2 * 64:(g2 + 1) * 64, kc, c0:c0 + 64], in_=C.psb[0][g2 * 64:(g2 + 1) * 64, 0:64]),
                        w=[PS(0), "BD"])
            bdm()
    xb = [P.sb(f"xb{i}", [128, 8, 512], F32) for i in range(2)]
    h = P.sb("h", [128, 8, 512], BF16)
    qk = [P.sb(f"qk{i}", [128, 8, 512], BF16) for i in range(2)]
    xg = [P.sb(f"xg{i}", [128, 4, 512], F32) for i in range(2)]
    fsb = P.sb("fsb", [128, 2, 512], BF16)
    vas = [P.sb(f"vas{i}", [128, 4, 1024], BF16) for i in range(2)]
    gsb = [P.sb(f"gsb{i}", [128, 4, 512], BF16) for i in range(2)]
    for i in range(2):
        P.op("pool", lambda e, i=i: e.memset(vas[i][:], 1.0), w=[("vas", i)])
    srcv = C.XT.rearrange("(m p) t -> p m t", p=128)
    QKv = C.QK.rearrange("(c p) t -> p c t", p=128)
    XGv = C.XG.rearrange("(c p) t -> p c t", p=128)
    cols = [c * 128 for c in range(8)] + [1536 + c * 128 for c in range(6)]

    def load(ti):
        t0, n, s = tiles[ti]
        P.dma("sp", xb[ti % 2][:, :, :n], srcv[:, :, t0:t0 + n], w=[(f"xb{ti % 2}", m) for m in range(8)],
              stream=f"xl{ti % 2}")

    def tile_body(ti, t0, n, s):
        X = xb[ti % 2]
        xk = f"xb{ti % 2}"
        b2 = ti % 2
        for m in range(8):
            s1p, sh, gs = mod_aps(C, l, j, m, s)
            P.op("act", lambda e, m=m, s1p=s1p, sh=sh: e.activation(
                out=h[:, m, :n], in_=X[:, m, :n], func=AF.Identity, scale=s1p, bias=sh),
                r=[(xk, m), "S1P", "M"], w=[("h", m)])
        for ci, col in enumerate(cols):
            bank = ci % 4
            for k in range(8):
                P.op("pe", lambda e, k=k, col=col, bank=bank: e.matmul(
                    C.psb[bank][:, :n], lhsT=win[:, k, col:col + 128], rhs=h[:, k, :n],
                    start=(k == 0), stop=(k == 7)), r=[("win", k), ("h", k)], w=[PS(bank)])
            if ci < 4:
                P.op("act", lambda e, ci=ci, bank=bank: e.activation(
                    out=qk[b2][:, ci, :n], in_=C.psb[bank][:, :n], func=AF.Identity, scale=0.125),
                    w=[PS(bank), ("qk", b2)])
            elif ci < 8:
                P.op("dve", lambda e, ci=ci, bank=bank: e.tensor_copy(out=qk[b2][:, ci, :n], in_=C.psb[bank][:, :n]),
                     w=[PS(bank), ("qk", b2)])
            elif ci < 12:
                eng = "act" if ci % 2 == 0 else "dve"
                if eng == "act":
                    P.op("act", lambda e, ci=ci, bank=bank: e.activation(
                        out=xg[b2][:, ci - 8, :n], in_=C.psb[bank][:, :n], func=AF.Identity),
                        w=[PS(bank), ("xg", b2)])
                else:
                    P.op("dve", lambda e, ci=ci, bank=bank: e.tensor_copy(
                        out=xg[b2][:, ci - 8, :n], in_=C.psb[bank][:, :n]), w=[PS(bank), ("xg", b2)])
            else:
                P.op("dve", lambda e, ci=ci, bank=bank: e.tensor_copy(out=fsb[:, ci - 12, :n], in_=C.psb[bank][:, :n]),
                     w=[PS(bank), "fsb"])
        nsub = n // 128
        for sub in range(nsub):
            bank = 4 + sub % 2
            for k in range(8):
                P.op("pe", lambda e, k=k, sub=sub, bank=bank: e.matmul(
                    C.psb[bank][:, :], lhsT=h[:, k, sub * 128:(sub + 1) * 128], rhs=win[:, k, 1024:1536],
                    start=(k == 0), stop=(k == 7)), r=[("win", k), ("h", k)], w=[PS(bank)])
            pv = C.psb[bank][:, :].rearrange("p (hp two d) -> p hp two d", two=2, d=64)
            vv = vas[b2][:, sub, :].rearrange("p (hp two d) -> p hp two d", two=2, d=128)
            P.op("act", lambda e, pv=pv, vv=vv: e.activation(out=vv[:, :, 0, 0:64], in_=pv[:, :, 0, :], func=AF.Identity),
                 w=[PS(bank), ("vas", b2)])
            P.op("dve", lambda e, pv=pv, vv=vv: e.tensor_copy(out=vv[:, :, 1, 64:128], in_=pv[:, :, 1, :]),
                 w=[PS(bank), ("vas", b2)])
        for sub in range(nsub):
            bank = 6
            for kc in range(2):
                P.op("pe", lambda e, kc=kc, sub=sub, bank=bank: e.matmul(
                    C.psb[bank][:, :], lhsT=fsb[:, kc, sub * 128:(sub + 1) * 128], rhs=BD[:, kc, :],
                    start=(kc == 0), stop=(kc == 1)), r=["fsb", "BD"], w=[PS(bank)])
            if sub % 2 == 0:
                P.op("act", lambda e, sub=sub, bank=bank: e.activation(out=gsb[b2][:, sub, :], in_=C.psb[bank][:, :],
                                                                       func=AF.Identity), w=[PS(bank), ("gsb", b2)])
            else:
                P.op("dve", lambda e, sub=sub, bank=bank: e.tensor_copy(out=gsb[b2][:, sub, :], in_=C.psb[bank][:, :]),
                     w=[PS(bank), ("gsb", b2)])
        P.dma("sp", QKv[:, :, t0:t0 + n], qk[b2][:, :, :n], r=[("qk", b2)], stream=f"sq{b2}")
        P.dma("sp", XGv[:, :, t0:t0 + n], xg[b2][:, :, :n], r=[("xg", b2)], stream=f"sx{b2}")
        P.dma("sp", C.VA[t0:t0 + n, :].rearrange("(s p) c -> p s c", p=128), vas[b2][:, :nsub, :],
              r=[("vas", b2)], stream=f"sv{b2}")
        P.dma("sp", C.G[t0:t0 + n, :].rearrange("(s p) c -> p s c", p=128), gsb[b2][:, :nsub, :],
              r=[("gsb", b2)], stream=f"sg{b2}")

    nxt = (l + 1 < n_layers)
    if nxt:
        abufs = adaln_alloc(P)
    load(0)
    for ti, (t0, n, s) in enumerate(tiles):
        if ti + 1 < len(tiles):
            load(ti + 1)
        tile_body(ti, t0, n, s)
        if nxt and ti < 10:
            adaln_step(P, C, l + 1, abufs, ti, 7)
    P.release(mk)


def attn_phase(P, C, l, need_ctx):
    P.barrier()
    mk = P.mark()
    Tbi = P.sb("Tbi", [128, 8, 1920], BF16)
    Tbf = P.sb("Tbf", [128, 8, 1920], BF16)
    rm = [P.sb(f"rm{i}", [128, 8, 512], BF16) for i in range(2)]
    kTc = P.sb("kTc", [128, 4, 256], BF16)
    vac = P.sb("vac", [128, 2, 1024], BF16)
    mk2 = P.mark()
    cmi = P.sb("cmi", [128, 1920], F32)
    cmf = P.sb("cmf", [128, 1920], F32)
    stg = [P.sb(f"stg{i}", [128, 1920], F32) for i in range(2)]
    P.dma("sp", cmi[:], C.cmask[0], w=["cmi"], stream="m0")
    P.dma("sp", cmf[:], C.cmask[1], w=["cmf"], stream="m1")
    for i in range(2):
        P.dma("pool", rm[i][:].rearrange("p a c -> p (a c)"), C.rmask[i], w=[("rm", i)], stream=f"w{i}")
    for hh in range(8):
        def tb(hh=hh):
            P.dma("sp", stg[hh % 2][:], C.rpbT[l, hh], w=[("stg", hh % 2)], stream=f"m{2 + hh % 2}")
            P.op("dve", lambda e: e.tensor_tensor(out=Tbi[:, hh, :], in0=stg[hh % 2][:], in1=cmi[:], op=ALU.add),
                 r=[("stg", hh % 2), "cmi"], w=["Tbi"])
            P.op("pool", lambda e: e.tensor_tensor(out=Tbf[:, hh, :], in0=stg[hh % 2][:], in1=cmf[:], op=ALU.add),
                 r=[("stg", hh % 2), "cmf"], w=["Tbf"])
        tb()
    QKv = C.QK.rearrange("(c p) t -> p c t", p=128)
    P.dma("sp", kTc[:], QKv[:, 4:8, 0:256], w=["kTc"], stream="m4")
    P.dma("sp", vac[:], C.VA[0:256, :].rearrange("(a p) c -> p a c", p=128), w=["vac"], stream="m5")
    P.barrier()
    P.release(mk2)
    qT = [P.sb(f"qT{i}", [128, 4, 512], BF16) for i in range(2)]
    kT = [P.sb(f"kT{i}", [128, 4, 1024], BF16) for i in range(2)]
    va = [P.sb(f"va{i}", [128, 8, 1024], BF16) for i in range(2)]
    NS = 6
    SBK = [0, 1, 2, 3, 6, 7]
    PT = [P.sb(f"PT{i}", [128, 512], BF16) for i in range(NS)]
    rd = [P.sb(f"rd{i}", [128, 512], F32) for i in range(2)]
    oc = [P.sb(f"oc{i}", [128, 4, 512], BF16) for i in range(2)]
    CATv = C.CAT.rearrange("(c p) t -> p c t", p=128)

    blocks = []
    if need_ctx:
        blocks.append(("ctx", 0, 256, None, None))
    for b in range(16):
        a0 = min(max(4 * b - 2, 0), 56)
        blocks.append(("lat", 256 + 512 * b, 512, b, a0))

    def load(bi):
        kind, q0, n, b, a0 = blocks[bi]
        P.dma("sp", qT[bi % 2][:, :, :n], QKv[:, 0:4, q0:q0 + n], w=[("qT", bi % 2)], stream=f"q{bi % 2}")
        if kind == "lat":
            k0 = 256 + 128 * a0
            P.dma("sp", kT[bi % 2][:], QKv[:, 4:8, k0:k0 + 1024], w=[("kT", bi % 2)], stream=f"k{bi % 2}")
            P.dma("sp", va[bi % 2][:], C.VA[k0:k0 + 1024, :].rearrange("(a p) c -> p a c", p=128),
                  w=[("va", bi % 2)], stream=f"v{bi % 2}")

    state = {"sidx": 0}

    def block_body(bi):
        kind, q0, n, b, a0 = blocks[bi]
        b2 = bi % 2
        chunks = [("ctx", 0, 0, n), ("ctx", 1, 0, n)]
        if kind == "lat":
            if b == 0:
                chunks += [("loc", ai, 0, n) for ai in range(0, 6)]
            elif b == 15:
                chunks += [("loc", ai, 0, n) for ai in range(2, 8)]
            else:
                for ai in range(8):
                    ilo, ihi = max(0, 2 * ai - 7), min(7, 2 * ai + 1)
                    chunks.append(("loc", ai, 64 * ilo, 64 * (ihi + 1)))
        items = [(hh, ch) for hh in range(8) for ch in chunks]
        nch = len(chunks)
        base = state["sidx"]
        state["sidx"] += len(items)
        edge = b in (0, 15)

        def emit_S(idx):
            hh, ch = items[idx]
            hc, pb = hh // 2, 64 * (hh % 2)
            si = (base + idx) % NS
            sb_ = SBK[si]
            c0, c1 = ch[2], ch[3]
            q_ap = qT[b2][pb:pb + 64, hc, c0:c1]
            if ch[0] == "ctx":
                ci = ch[1]
                P.op("pe", lambda e: e.matmul(C.psb[sb_][:, c0:c1], lhsT=kTc[pb:pb + 64, hc, ci * 128:(ci + 1) * 128],
                                              rhs=q_ap, start=True, stop=True),
                     r=["kTc", ("qT", b2)], w=[PS(sb_)])
            else:
                ai = ch[1]
                a = a0 + ai
                e0 = 8 * b - 2 * a + 14
                tb_ = Tbf if edge else Tbi
                P.op("pe", lambda e: e.matmul(C.psb[sb_][:, c0:c1], lhsT=kT[b2][pb:pb + 64, hc, ai * 128:(ai + 1) * 128],
                                              rhs=q_ap, start=True, stop=False),
                     r=[("kT", b2), ("qT", b2)], w=[PS(sb_)])
                P.op("pe", lambda e: e.matmul(C.psb[sb_][:, c0:c1], lhsT=C.identb[:, :],
                                              rhs=tb_[:, hh, e0 * 64 + c0:e0 * 64 + c1], start=False, stop=(not edge)),
                     r=["identb", "Tbf" if edge else "Tbi"], w=[PS(sb_)])
                if edge:
                    ri = 0 if b == 0 else 1
                    P.op("pe", lambda e: e.matmul(C.psb[sb_][:, c0:c1], lhsT=C.identb[:, :], rhs=rm[ri][:, ai, c0:c1],
                                                  start=False, stop=True), r=["identb", ("rm", ri)], w=[PS(sb_)])

        def emit_exp(idx):
            hh, ch = items[idx]
            si = (base + idx) % NS
            sb_ = SBK[si]
            c0, c1 = ch[2], ch[3]
            P.op("act", lambda e: e.activation(out=PT[si][:, c0:c1], in_=C.psb[sb_][:, c0:c1], func=AF.Exp),
                 w=[PS(sb_), ("PT", si)])

        def emit_PV(idx):
            hh, ch = items[idx]
            hc, pb = hh // 2, 64 * (hh % 2)
            si = (base + idx) % NS
            ob = 4 + hh % 2
            c0, c1 = ch[2], ch[3]
            ci_in_head = idx % nch
            if ch[0] == "ctx":
                lhs = vac[:, ch[1], hh * 128:(hh + 1) * 128]
                rk = ["vac", ("PT", si)]
            else:
                lhs = va[b2][:, ch[1], hh * 128:(hh + 1) * 128]
                rk = [("va", b2), ("PT", si)]
            P.op("pe", lambda e: e.matmul(C.psb[ob][:, c0:c1], lhsT=lhs, rhs=PT[si][:, c0:c1],
                                          start=(ci_in_head == 0), stop=(ci_in_head == nch - 1)),
                 r=rk, w=[PS(ob)])
            if ci_in_head == nch - 1:
                r2 = hh % 2
                if hh % 2 == 0:
                    num, den = slice(0, 64), slice(64, 128)
                else:
                    num, den = slice(64, 128), slice(0, 64)
                P.op("dve", lambda e: e.reciprocal(out=rd[r2][num, :n], in_=C.psb[ob][den, :n]),
                     w=[PS(ob), ("rd", r2)])
                P.op("dve", lambda e: e.tensor_tensor(out=oc[b2][num, hc, :n], in0=C.psb[ob][num, :n],
                                                      in1=rd[r2][num, :n], op=ALU.mult),
                     r=[("rd", r2)], w=[PS(ob), ("oc", b2)])

        AHEAD = 4
        for q_ in range(min(AHEAD, len(items))):
            emit_S(q_)
        for idx in range(len(items)):
            emit_exp(idx)
            if idx + AHEAD < len(items):
                emit_S(idx + AHEAD)
            emit_PV(idx)
        P.dma("sp", CATv[:, 0:4, q0:q0 + n], oc[b2][:, :, :n], r=[("oc", b2)], stream=f"so{b2}")

    load(0)
    for bi in range(len(blocks)):
        if bi + 1 < len(blocks):
            load(bi + 1)
        block_body(bi)
    P.release(mk)


def lru_phase(P, C, l):
    P.barrier()
    mk = P.mark()
    W = T + 8
    SEG = 1024
    XP = P.sb("XP", [128, W], F32)
    XC = P.sb("XC", [128, T], F32)
    XCB = P.sb("XCB", [128, T], BF16)
    HF = P.sb("HF", [128, T], F32)
    HB = P.sb("HB", [128, T], F32)
    half = P.sb("half", [128, 1], F32)
    LV = P.sb("LV", [128, 22], F32)
    NK = P.sb("NK", [128, 4], F32)
    NK2 = P.sb("NK2", [128, 4], F32)
    HBA = P.sb("HBA", [128, 8], F32)
    tiny = P.sb("tiny", [128, 1], F32)
    Wbf = P.sb("Wbf", [128, 8, 128], BF16)
    mkw = P.mark()
    Wst = P.sb("Wst", [128, 8, 128], F32)
    P.dma("sp", LV[:], C.lruv[:, l * 22:(l + 1) * 22], w=["LV"], stream="m0")
    P.op("dve", lambda e: e.memset(tiny[:], 1e-20), w=["tiny"])
    P.op("dve", lambda e: e.memset(half[:], 0.5), w=["half"])
    P.op("pool", lambda e: e.memset(Wst[:], 0.0), w=["Wst"])
    si = 0
    for d in range(2):
        for gi, src in enumerate((C.lru_wa, C.lru_wx)):
            for cc in range(2):
                idx = (d * 2 + gi) * 2 + cc
                for h2 in range(2):
                    P.dma("sp", Wst[h2 * 64:(h2 + 1) * 64, idx, h2 * 64:(h2 + 1) * 64], src[l, d, 2 * cc + h2],
                          w=["Wst"], stream=f"m{1 + si % 4}")
                    si += 1
    P.op("dve", lambda e: e.tensor_copy(out=Wbf[:], in_=Wst[:]), r=["Wst"], w=["Wbf"])
    P.barrier()
    P.release(mkw)
    Rb = [[P.sb(f"Rb{d}{k}", [128, SEG], F32) for k in range(2)] for d in range(2)]
    Ib = [[P.sb(f"Ib{d}{k}", [128, SEG], F32) for k in range(2)] for d in range(2)]
    Tb = [[P.sb(f"Tb{d}{k}", [128, SEG], F32) for k in range(2)] for d in range(2)]
    for cc in range(2):
        for d in range(2):
            def kap(cc=cc, d=d):
                lam = LV[:, cc * 11 + 7 + 3 * d: cc * 11 + 8 + 3 * d]
                o = NK[:, cc * 2 + d: cc * 2 + d + 1]
                o2 = NK2[:, cc * 2 + d: cc * 2 + d + 1]
                P.op("act", lambda e: e.activation(out=o2, in_=lam, func=AF.Exp, scale=-1.0), r=["LV"], w=["NK2"])
                P.op("act", lambda e: e.activation(out=o2, in_=o2, func=AF.Ln, bias=1.0), w=["NK2"])
                P.op("dve", lambda e: e.tensor_scalar(out=o, in0=o2, scalar1=-4.0, scalar2=None, op0=ALU.mult),
                     r=["NK2"], w=["NK"])
                for gi in range(2):
                    bsrc = LV[:, cc * 11 + 5 + 3 * d + gi: cc * 11 + 6 + 3 * d + gi]
                    bo = HBA[:, (cc * 2 + d) * 2 + gi:(cc * 2 + d) * 2 + gi + 1]
                    P.op("dve", lambda e, bsrc=bsrc, bo=bo: e.tensor_scalar(out=bo, in0=bsrc, scalar1=0.5, scalar2=None,
                                                                            op0=ALU.mult), r=["LV"], w=["HBA"])
            kap()
    segs = [(0, 256)] + [(256 + SEG * i, SEG) for i in range(8)]
    XGv = C.XG
    CATv = C.CAT

    def rev(ap):
        nn = ap.shape[-1]
        return bass.AP(ap.tensor, ap.offset + (nn - 1), [list(ap.ap[0]), [-1, nn]])

    def per_cc(cc):
        lv = lambda v: LV[:, cc * 11 + v: cc * 11 + v + 1]
        P.op("pool", lambda e: e.memset(XP[:, 0:2], 0.0), w=["XP"])
        P.op("pool", lambda e: e.memset(XP[:, 258:261], 0.0), w=["XP"])
        P.op("pool", lambda e: e.memset(XP[:, 8453:8456], 0.0), w=["XP"])
        P.dma("sp", XP[:, 2:258], XGv[cc * 128:(cc + 1) * 128, 0:256], w=["XP"], stream="l0")
        P.dma("sp", XP[:, 261:8453], XGv[cc * 128:(cc + 1) * 128, 256:T], w=["XP"], stream="l1")
        for si_, (s0, sn) in enumerate(segs):
            def cv(si_=si_, s0=s0, sn=sn):
                i0 = s0 if s0 < 256 else s0 + 3
                P.op("dve", lambda e: e.tensor_scalar(out=XC[:, s0:s0 + sn], in0=XP[:, i0:i0 + sn], scalar1=lv(0),
                                                      scalar2=lv(4), op0=ALU.mult, op1=ALU.add),
                     r=["XP", "LV"], w=[("XC", si_)])
                for jx in range(1, 4):
                    P.op("dve", lambda e, jx=jx: e.scalar_tensor_tensor(
                        out=XC[:, s0:s0 + sn], in0=XP[:, i0 + jx:i0 + jx + sn], scalar=lv(jx), in1=XC[:, s0:s0 + sn],
                        op0=ALU.mult, op1=ALU.add), r=["XP", "LV"], w=[("XC", si_)])
                P.op("act", lambda e: e.activation(out=XCB[:, s0:s0 + sn], in_=XC[:, s0:s0 + sn], func=AF.Identity),
                     r=[("XC", si_)], w=[("XCB", si_)])
            cv()

        cnt = [0, 0]
        prev = [None, None]

        def seg_step(d, si_):
            s0, sn = segs[si_]
            k = cnt[d] % 2
            cnt[d] += 1
            R_, I_, T_ = Rb[d][k], Ib[d][k], Tb[d][k]
            rk, ik, tk = ("Rb", d, k), ("Ib", d, k), ("Tb", d, k)
            nk = NK[:, cc * 2 + d: cc * 2 + d + 1]
            for sub in range(0, sn, 512):
                n = min(512, sn - sub)
                for gi, dst, dk in ((0, R_, rk), (1, I_, ik)):
                    def gate(sub=sub, n=n, gi=gi, dst=dst, dk=dk):
                        bank = d * 2 + gi
                        idx = (d * 2 + gi) * 2 + cc
                        hb = HBA[:, (cc * 2 + d) * 2 + gi:(cc * 2 + d) * 2 + gi + 1]
                        P.op("pe", lambda e: e.matmul(C.psb[bank][:, :n], lhsT=Wbf[:, idx, :],
                                                      rhs=XCB[:, s0 + sub:s0 + sub + n], start=True, stop=True),
                             r=["Wbf", ("XCB", si_)], w=[PS(bank)])
                        P.op("act", lambda e: e.activation(out=dst[:, sub:sub + n], in_=C.psb[bank][:, :n],
                                                           func=AF.Tanh, scale=0.5, bias=hb),
                             r=["HBA"], w=[PS(bank), dk])
                    gate()
            P.op("act", lambda e: e.activation(out=R_[:, :sn], in_=R_[:, :sn], func=AF.Exp, scale=nk, bias=nk),
                 r=["NK"], w=[rk])
            P.op("pool", lambda e: e.tensor_tensor(out=T_[:, :sn], in0=R_[:, :sn], in1=R_[:, :sn], op=ALU.mult),
                 r=[rk], w=[tk])
            P.op("dve", lambda e: e.tensor_scalar(out=T_[:, :sn], in0=T_[:, :sn], scalar1=-0.25, scalar2=0.25 + 1e-20,
                                                  op0=ALU.mult, op1=ALU.add), w=[tk])
            return lambda: seg_step_b(d, si_, k)

        def seg_step_b(d, si_, k):
            s0, sn = segs[si_]
            R_, I_, T_ = Rb[d][k], Ib[d][k], Tb[d][k]
            rk, ik, tk = ("Rb", d, k), ("Ib", d, k), ("Tb", d, k)
            P.op("act", lambda e: e.activation(out=T_[:, :sn], in_=T_[:, :sn], func=AF.Sqrt, bias=tiny[:, 0:1]),
                 r=["tiny"], w=[tk])
            P.op("dve", lambda e: e.scalar_tensor_tensor(out=I_[:, :sn], in0=I_[:, :sn], scalar=1.0, in1=T_[:, :sn],
                                                         op0=ALU.add, op1=ALU.mult), r=[tk], w=[ik])
            P.op("dve", lambda e: e.tensor_tensor(out=I_[:, :sn], in0=I_[:, :sn], in1=XC[:, s0:s0 + sn], op=ALU.mult),
                 r=[("XC", si_)], w=[ik])
            if d == 0:
                init = 0.0 if prev[0] is None else HF[:, prev[0][0] + prev[0][1] - 1:prev[0][0] + prev[0][1]]
                rr = [rk, ik] + ([("HF", prev[0][2])] if prev[0] is not None else [])
                P.op("dve", lambda e: e.tensor_tensor_scan(out=HF[:, s0:s0 + sn], data0=R_[:, :sn], data1=I_[:, :sn],
                                                           initial=init, op0=ALU.mult, op1=ALU.add),
                     r=rr, w=[("HF", si_)])
            else:
                init = 0.0 if prev[1] is None else HB[:, prev[1][0]:prev[1][0] + 1]
                rr = [rk, ik] + ([("HB", prev[1][2])] if prev[1] is not None else [])
                P.op("dve", lambda e: e.tensor_tensor_scan(out=rev(HB[:, s0:s0 + sn]), data0=rev(R_[:, :sn]),
                                                           data1=rev(I_[:, :sn]), initial=init,
                                                           op0=ALU.mult, op1=ALU.add),
                     r=rr, w=[("HB", si_)])
            prev[d] = (s0, sn, si_)

        order_f = list(range(9))
        order_b = [0] + list(range(8, 0, -1))
        for q in range(9):
            fb = seg_step(0, order_f[q])
            bb = seg_step(1, order_b[q])
            fb()
            bb()
        GRb = XP
        P.dma("sp", GRb[:, 0:T], XGv[256 + cc * 128:256 + (cc + 1) * 128, :], w=["XP"], stream="l2")
        for si_, (s0, sn) in enumerate(segs):
            def gl(si_=si_, s0=s0, sn=sn):
                k = si_ % 2
                U = Rb[0][k]
                S_ = Ib[0][k]
                uk, sk = ("Rb", 0, k), ("Ib", 0, k)
                g_ = GRb[:, s0:s0 + sn]
                P.op("act", lambda e: e.activation(out=U[:, :sn], in_=g_, func=AF.Square), r=["XP"], w=[uk])
                P.op("dve", lambda e: e.tensor_scalar(out=U[:, :sn], in0=U[:, :sn], scalar1=0.044715, scalar2=1.0,
                                                      op0=ALU.mult, op1=ALU.add), w=[uk])
                P.op("pool", lambda e: e.tensor_tensor(out=U[:, :sn], in0=U[:, :sn], in1=g_, op=ALU.mult),
                     r=["XP"], w=[uk])
                P.op("act", lambda e: e.activation(out=U[:, :sn], in_=U[:, :sn], func=AF.Tanh, scale=0.7978845608028654),
                     w=[uk])
                P.op("dve", lambda e: e.scalar_tensor_tensor(out=U[:, :sn], in0=U[:, :sn], scalar=1.0, in1=g_,
                                                             op0=ALU.add, op1=ALU.mult), r=["XP"], w=[uk])
                P.op("pool", lambda e: e.tensor_tensor(out=S_[:, :sn], in0=HF[:, s0:s0 + sn], in1=HB[:, s0:s0 + sn],
                                                       op=ALU.add), r=[("HF", si_), ("HB", si_)], w=[sk])
                P.op("dve", lambda e: e.scalar_tensor_tensor(out=XCB[:, s0:s0 + sn], in0=U[:, :sn], scalar=0.5,
                                                             in1=S_[:, :sn], op0=ALU.mult, op1=ALU.mult),
                     r=[uk, sk], w=[("XCB", si_)])
            gl()
        P.dma("sp", CATv[512 + cc * 128:512 + (cc + 1) * 128, :], XCB[:, :], r=[("XCB", q) for q in range(9)],
              w=["XCBst"], stream="l3")

    for cc in range(2):
        per_cc(cc)
    P.release(mk)


def fno_phase(P, C, l, need_ctx):
    P.barrier()
    mk = P.mark()
    cw = P.sb("cw", [128, 128], BF16)
    sw = P.sb("sw", [128, 128], BF16)
    nsw = P.sb("nsw", [128, 128], BF16)
    Wt = P.sb("Wt", [128, 128, 64], BF16)
    t256 = P.sb("t256", [128, 2, 2, 256], BF16)
    P.dma("pool", cw[:], C.cw128, w=["cw"], stream="w0")
    P.dma("pool", sw[:], C.sw128, w=["sw"], stream="w1")
    P.dma("pool", nsw[:], C.nsw128, w=["nsw"], stream="w2")
    P.dma("pool", Wt[:].rearrange("p a b -> p (a b)"), C.wtC, w=["Wt"], stream="w3")
    P.dma("pool", t256[:].rearrange("p a b c -> p (a b c)"), C.t256, w=["t256"], stream="w4")
    Gb = [P.sb(f"Gb{i}", [128, 16, 512], BF16) for i in range(2)]
    Ab = [[P.sb(f"Ab{i}{ri}", [128, 16, 256], BF16) for ri in range(2)] for i in range(2)]
    Glat = C.G[256:T, :].rearrange("(n1 n2) c -> n1 n2 c", n2=64)
    sA = 1.0
    for blk in range(4):
        def stA(blk=blk):
            b2 = blk % 2
            P.dma("sp", Gb[b2][:], Glat[:, blk * 16:(blk + 1) * 16, :], w=[("Gb", b2)], stream=f"g{b2}")
            for pair in range(8):
                def pr(pair=pair):
                    gc = Gb[b2][:, 2 * pair:2 * pair + 2, 0:256]
                    gs_ = Gb[b2][:, 2 * pair:2 * pair + 2, 256:512]
                    bre = (pair % 2) * 2
                    bim = bre + 1
                    P.op("pe", lambda e: e.matmul(C.psb[bre][:, :], lhsT=cw[:, :], rhs=gc, start=True, stop=False),
                         r=["cw", ("Gb", b2)], w=[PS(bre)])
                    P.op("pe", lambda e: e.matmul(C.psb[bre][:, :], lhsT=nsw[:, :], rhs=gs_, start=False, stop=True),
                         r=["nsw", ("Gb", b2)], w=[PS(bre)])
                    P.op("pe", lambda e: e.matmul(C.psb[bim][:, :], lhsT=cw[:, :], rhs=gs_, start=True, stop=False),
                         r=["cw", ("Gb", b2)], w=[PS(bim)])
                    P.op("pe", lambda e: e.matmul(C.psb[bim][:, :], lhsT=sw[:, :], rhs=gc, start=False, stop=True),
                         r=["sw", ("Gb", b2)], w=[PS(bim)])
                    P.op("act", lambda e: e.activation(
                        out=Ab[b2][0][:, 2 * pair:2 * pair + 2, :].rearrange("p a c -> p (a c)"),
                        in_=C.psb[bre][:, :], func=AF.Identity), w=[PS(bre), ("Ab", b2, 0)])
                    P.op("dve", lambda e: e.tensor_copy(
                        out=Ab[b2][1][:, 2 * pair:2 * pair + 2, :].rearrange("p a c -> p (a c)"),
                        in_=C.psb[bim][:, :]), w=[PS(bim), ("Ab", b2, 1)])
                pr()
            for ri in range(2):
                P.dma("sp", C.AB[ri, :, blk * 16:(blk + 1) * 16, :], Ab[b2][ri][:], r=[("Ab", b2, ri)],
                      w=[("AB", blk)], stream=f"a{b2}{ri}")
        stA()
    Ap = [P.sb(f"Ap{i}", [128, 32, 256], BF16) for i in range(2)]
    Yt = P.sb("Yt", [128, 2, 8192], BF16)
    ABv = C.AB.rearrange("ri k1 n2 c -> ri n2 k1 c")
    scl = 1.0 / float(np.sqrt(8192.0 * 64.0))
    for q in range(4):
        def stC(q=q):
            b2 = q % 2
            for ri in range(2):
                P.dma("sp", Ap[b2][ri * 64:(ri + 1) * 64, :, :], ABv[ri, :, q * 32:(q + 1) * 32, :],
                      r=[("AB", bb) for bb in range(4)], w=[("Ap", b2)], stream=f"p{b2}{ri}")
            for cc in range(2):
                for kb in range(4):
                    def grp(cc=cc, kb=kb):
                        bank = 4 + (cc * 4 + kb) % 4
                        pv = C.psb[bank][:, :].rearrange("p (k2 j) -> p k2 j", j=8)
                        for jx in range(8):
                            k1l = kb * 8 + jx
                            k1 = q * 32 + k1l
                            P.op("pe", lambda e, jx=jx, k1l=k1l, k1=k1: e.matmul(
                                pv[:, :, jx], lhsT=Ap[b2][:, k1l, cc * 128:(cc + 1) * 128], rhs=Wt[:, k1, :],
                                start=True, stop=True), r=[("Ap", b2), "Wt"], w=[PS(bank)])
                        k1b = q * 32 + kb * 8
                        yv = Yt[:, cc, :].rearrange("p (k2 k1) -> p k2 k1", k1=128)[:, :, k1b:k1b + 8]
                        if kb % 2 == 0:
                            P.op("act", lambda e: e.activation(out=yv, in_=pv, func=AF.Identity, scale=scl),
                                 w=[PS(bank), "Yt"])
                        else:
                            P.op("dve", lambda e: e.tensor_scalar(out=yv, in0=pv, scalar1=scl, scalar2=None,
                                                                  op0=ALU.mult), w=[PS(bank), "Yt"])
                    grp()
        stC()
    for cc in range(2):
        P.dma("sp", C.CAT[768 + cc * 128:768 + (cc + 1) * 128, 256:T], Yt[:, cc, :], r=["Yt"], stream=f"y{cc}")
    if need_ctx:
        Gc_ = P.sb("Gctx", [128, 2, 512], BF16)
        Ytc = P.sb("Ytc", [128, 2, 256], BF16)
        sclc = 1.0 / float(np.sqrt(256.0 * 64.0))
        P.dma("sp", Gc_[:], C.G[0:256, :].rearrange("(a p) c -> p a c", p=128), w=["Gctx"], stream="g0")
        for cc in range(2):
            def cx(cc=cc):
                bank = cc
                i = 0
                for nchk in range(2):
                    for part in range(2):
                        P.op("pe", lambda e, nchk=nchk, part=part, i=i: e.matmul(
                            C.psb[bank][:, 0:256], lhsT=Gc_[:, nchk, part * 256 + cc * 128: part * 256 + (cc + 1) * 128],
                            rhs=t256[:, nchk, part, :], start=(i == 0), stop=(i == 3)),
                            r=["Gctx", "t256"], w=[PS(bank)])
                        i += 1
                P.op("act", lambda e: e.activation(out=Ytc[:, cc, :], in_=C.psb[bank][:, 0:256], func=AF.Identity,
                                                   scale=sclc), w=[PS(bank), "Ytc"])
            cx()
        P.dma("sp", C.CAT.rearrange("(c p) t -> p c t", p=128)[:, 6:8, 0:256], Ytc[:], r=["Ytc"], stream="y2")
    P.release(mk)


def outproj_phase(P, C, l, tiles):
    j = 1
    P.barrier()
    mk = P.mark()
    wo = P.sb("wo", [128, 8, D], BF16)
    Wv = C.w_out[l].rearrange("(k p) n -> p k n", p=128)
    for k in range(8):
        P.dma("pool", wo[:, k, :], Wv[:, k, :], w=[("wo", k)], stream=f"w{k % 4}")
    xb = [P.sb(f"xb{i}", [128, 8, 512], F32) for i in range(3)]
    cb = [P.sb(f"cb{i}", [128, 8, 512], BF16) for i in range(2)]
    zb = [P.sb(f"zb{i}", [128, 512], BF16) for i in range(2)]
    zq = [P.sb(f"zq{i}", [128, 512], BF16) for i in range(2)]
    msq = [P.sb(f"msq{i}", [128, 512], F32) for i in range(2)]
    srcv = C.XT.rearrange("(m p) t -> p m t", p=128)
    catv = C.CAT.rearrange("(m p) t -> p m t", p=128)
    nt = len(tiles)

    def load(ti):
        t0, n, s = tiles[ti]
        P.dma("sp", xb[ti % 3][:, :, :n], srcv[:, :, t0:t0 + n], w=[(f"xb{ti % 3}", m) for m in range(8)],
              stream=f"xl{ti % 3}")
        P.dma("sp", cb[ti % 2][:, :, :n], catv[:, :, t0:t0 + n], w=[("cb", ti % 2)], stream=f"cl{ti % 2}")

    def resid_piece(ti, m):
        t0, n, s = tiles[ti]
        X = xb[ti % 3]
        xk = f"xb{ti % 3}"
        cbt = cb[ti % 2]
        bm, be = (6, 7) if ti % 2 == 0 else (2, 3)
        py = 4 + m % 2
        s1p, sh, gs = mod_aps(C, l, j, m, s)

        def stats(mm):
            P.op("pe", lambda e: e.matmul(C.psb[bm][:, :n], lhsT=C.onesb[:, :], rhs=zb[mm % 2][:, :n],
                                          start=(mm == 0), stop=(mm == 7)), r=["onesb", ("zb", mm % 2)], w=[PS(bm)])
            P.op("pe", lambda e: e.matmul(C.psb[be][:, :n], lhsT=C.onesb[:, :], rhs=zq[mm % 2][:, :n],
                                          start=(mm == 0), stop=(mm == 7)), r=["onesb", ("zq", mm % 2)], w=[PS(be)])
        for k in range(8):
            P.op("pe", lambda e, k=k: e.matmul(C.psb[py][:, :n], lhsT=wo[:, k, m * 128:(m + 1) * 128],
                                               rhs=cbt[:, k, :n], start=(k == 0), stop=(k == 7)),
                 r=[("wo", k), ("cb", ti % 2)], w=[PS(py)])
        P.op("dve", lambda e: e.scalar_tensor_tensor(
            out=X[:, m, :n], in0=C.psb[py][:, :n], scalar=gs, in1=X[:, m, :n], op0=ALU.mult, op1=ALU.add),
            r=["GS"], w=[PS(py), (xk, m)])
        P.op("act", lambda e: e.activation(out=zb[m % 2][:, :n], in_=X[:, m, :n], func=AF.Identity),
             r=[(xk, m)], w=[("zb", m % 2)])
        P.op("act", lambda e: e.activation(out=zq[m % 2][:, :n], in_=X[:, m, :n], func=AF.Square),
             r=[(xk, m)], w=[("zq", m % 2)])
        if m > 0:
            stats(m - 1)
        if m == 7:
            stats(7)

    def finish(ti, inter):
        t0, n, s = tiles[ti]
        X = xb[ti % 3]
        xk = f"xb{ti % 3}"
        bm, be = (6, 7) if ti % 2 == 0 else (2, 3)
        ln_tail(P, C, l, j, X, xk, n, bm, be, msq[ti % 2], f"o{ti % 2}", inter, every=True)
        P.dma("sp", srcv[:, :, t0:t0 + n], X[:, :, :n], r=[(xk, m) for m in range(8)], stream=f"xs{ti % 3}")

    load(0)
    if nt > 1:
        load(1)
    for m in range(8):
        resid_piece(0, m)
    for ti in range(nt):
        if ti + 2 < nt:
            load(ti + 2)
        inter = []
        if ti + 1 < nt:
            inter = [(lambda m=m: resid_piece(ti + 1, m)) for m in range(8)]
        finish(ti, inter)
    P.release(mk)


def build(n_layers=DEPTH, stop_after=None, dbg=False, tiles=None, only=None):
    nc = bass.Bass("TRN2", target_bir_lowering=False)
    C = Ctx()
    P = Prog(nc)
    declare(nc, C, n_layers, dbg)
    prologue(P, C, n_layers)
    tl = tiles if tiles is not None else tiles_all()
    XTv = C.XT.rearrange("(m p) t -> p m t", p=128)

    def to_xt(tiles):
        return lambda ti: XTv[:, :, tiles[ti][0]:tiles[ti][0] + tiles[ti][1]]

    stages = ["ffn1", "inproj", "attn", "lru", "fno", "mix", "ffn2"]
    outv = C.out.rearrange("(m p) t -> p m t", p=128)

    def run_layers():
        for l in range(n_layers):
            last = (l == DEPTH - 1)
            tl2 = tl if not last else tl[1:]

            def dst_last(ti, tl2=tl2):
                t0, n, s_ = tl2[ti]
                return outv[:, :, t0 - NCTX:t0 - NCTX + n]
            seq = [
                ("ffn1", lambda: ffn_phase(P, C, l, 0, C.xin if l == 0 else C.XT, to_xt(tl), tl)),
                ("inproj", lambda: inproj_phase(P, C, l, tl, n_layers)),
                ("attn", lambda: attn_phase(P, C, l, not last)),
                ("lru", lambda: lru_phase(P, C, l)),
                ("fno", lambda: fno_phase(P, C, l, not last)),
                ("mix", lambda: outproj_phase(P, C, l, tl2)),
                ("ffn2", lambda: ffn_phase(P, C, l, 1, C.XT, dst_last if last else to_xt(tl2), tl2)),
            ]
            if l == 0 and only is not None and "ffn1" not in only:
                P.dma("sp", C.XT, C.xin, stream="cp")
            for name, fn in seq:
                if only is None or name in only:
                    fn()
                if stop_after == (l, name):
                    return
    run_layers()
    if dbg:
        P.barrier()
        P.dma("sp", C.dbg, C.XT, stream="dbg")
        P.dma("sp", C.dbg2, C.M[:], stream="dbg2")
    P.emit()
    C.P = P
    return nc, C


def host_inputs(inp, b):
    f = np.float32
    x, ctx = inp["x"], inp["ctx"]
    xin = np.ascontiguousarray(np.concatenate([ctx[b].T, x[b].T], axis=1), dtype=f)
    cv = np.stack([np.asarray(inp["c"][b]), np.asarray(inp["c_ctx"])], axis=-1)
    cvec = np.ascontiguousarray(cv.reshape(8, 128, 2).transpose(1, 0, 2).reshape(128, 16), dtype=f)
    return {"xin": xin, "cvec": cvec}


def host_shared(inp):
    f = np.float32
    ba = np.asarray(inp["b_ada"]).reshape(DEPTH, 72, 128).transpose(2, 0, 1)
    bada = np.ascontiguousarray(np.repeat(ba[:, :, :, None], 2, axis=3).reshape(128, DEPTH * 144), dtype=f)
    lng = np.ascontiguousarray(np.asarray(inp["ln_g"]).reshape(DEPTH * 3 * 8, 128).T, dtype=f)
    lnb = np.ascontiguousarray(np.asarray(inp["ln_b"]).reshape(DEPTH * 3 * 8, 128).T, dtype=f)
    sh = {"bada": bada, "lng": lng, "lnb": lnb, "ident": np.eye(128, dtype=f)}
    for k in ("w_ada", "ff1_gate", "ff1_up", "ff1_down", "ff2_gate", "ff2_up", "ff2_down", "w_in", "w_out"):
        sh[k] = np.ascontiguousarray(inp[k], dtype=f)
    sh.update(host_consts(inp))
    return sh


def host_consts(inp):
    f = np.float32
    k64 = np.arange(64)
    a64 = 2 * np.pi * ((np.outer(k64, k64)) % 64) / 64.0
    C64, S64 = np.cos(a64), np.sin(a64)
    z = np.zeros((64, 64))
    c64bd = np.block([[C64, z], [z, C64]])
    s64bd = np.block([[S64, z], [z, S64]])
    n1 = np.arange(128)
    a128 = 2 * np.pi * ((np.outer(n1, n1)) % 128) / 128.0
    n2 = np.arange(64)
    kk = n1[:, None] + 128 * k64[None, :]
    ang = 2 * np.pi * ((n2[:, None, None] * kk[None]) % 8192) / 8192.0
    wtC = np.concatenate([np.cos(ang), -np.sin(ang)], axis=0).reshape(128, 128 * 64)
    n256 = np.arange(256)
    a256 = 2 * np.pi * ((np.outer(n256, n256)) % 256) / 256.0
    t256 = np.stack([np.cos(a256), -np.sin(a256)], axis=1)
    t256 = t256.reshape(2, 128, 2, 256).transpose(1, 0, 2, 3).reshape(128, 2 * 2 * 256)
    p = np.arange(128)
    krl, kc = p // 64, p % 64
    e = np.arange(30)
    qc = np.arange(64)
    d = krl[:, None] - (e[None, :] - 14)
    dr = d + 7
    dc = kc[:, None] - qc[None, :] + 15
    col0 = np.clip(qc - 8, 0, 48)
    colv = (kc[:, None] >= col0[None, :]) & (kc[:, None] < col0[None, :] + 16)
    drv = (dr >= 0) & (dr <= 14)
    rpb = np.asarray(inp["na_rpb"])
    dri = np.clip(dr, 0, 14)
    dci = np.clip(dc, 0, 30)
    gat = rpb[:, :, dri[:, :, None], dci[:, None, :]]
    okf = drv[:, :, None] & colv[:, None, :]
    rpbT = np.where(okf[None, None], gat, 0.0).reshape(DEPTH, 8, 128, 1920)
    oki = okf & ((d >= -4) & (d <= 3))[:, :, None]
    cmask = np.stack([np.where(oki, 0.0, NEG), np.where(okf, 0.0, NEG)], 0).reshape(2, 128, 1920)
    rm = np.zeros((2, 128, 8, 8, 64))
    for ri, (kr0, qr0) in enumerate(((0, 0), (112, 120))):
        for ai in range(8):
            for i in range(8):
                qr = qr0 + i
                rs = min(max(qr - 4, 0), 120)
                kr = kr0 + 2 * ai + krl
                ok = (kr >= rs) & (kr < rs + 8)
                rm[ri, :, ai, i, :] = np.where(ok, 0.0, NEG)[:, None]
    rmask = rm.reshape(2, 128, 8 * 512)
    L = DEPTH
    lv = np.zeros((L, 256, 11))
    lv[:, :, 0:4] = np.asarray(inp["lru_conv_w"]).transpose(0, 2, 1)
    lv[:, :, 4] = np.asarray(inp["lru_conv_b"])
    for dd in range(2):
        lv[:, :, 5 + 3 * dd] = np.asarray(inp["lru_ba"])[:, dd]
        lv[:, :, 6 + 3 * dd] = np.asarray(inp["lru_bx"])[:, dd]
        lv[:, :, 7 + 3 * dd] = np.asarray(inp["lru_lambda"])[:, dd]
    lruv = lv.reshape(L, 2, 128, 11).transpose(2, 0, 1, 3).reshape(128, L * 2 * 11)
    out = {"c64bd": c64bd, "s64bd": s64bd, "cw128": np.cos(a128), "sw128": np.sin(a128), "nsw128": -np.sin(a128),
           "wtC": wtC, "t256": t256, "rpbT": rpbT, "cmask": cmask, "rmask": rmask, "lruv": lruv}
    out = {k: np.ascontiguousarray(v, dtype=f) for k, v in out.items()}
    for k in ("lru_wa", "lru_wx", "fno_w"):
        out[k] = np.ascontiguousarray(inp[k], dtype=f)
    return out


_CACHE = {}


def kernel(**inputs):
    if "nc" not in _CACHE:
        _CACHE["nc"] = build()[0]
    nc = _CACHE["nc"]
    sh = host_shared(inputs)
    in_maps = []
    for b in range(8):
        d = dict(sh)
        d.update(host_inputs(inputs, b))
        in_maps.append(d)
    res = run_bass_kernel_spmd(nc, in_maps, core_ids=list(range(8)))
    out = np.stack([np.ascontiguousarray(r["out"].T) for r in res.results], axis=0)
    return out.astype(np.float32)
```

```python
import contextlib
import numpy as np
import concourse.bass as bass
import concourse.mybir as mybir
from concourse.bass_utils import run_bass_kernel_spmd

F32 = mybir.dt.float32
BF16 = mybir.dt.bfloat16
I32 = mybir.dt.int32
AF = mybir.ActivationFunctionType
ALU = mybir.AluOpType

D = 1024
DEPTH = 4
NCTX = 256
NLAT = 8192
T = NCTX + NLAT
DFF = 2816
NJ = DFF // 128
ALPHA = (2 * DEPTH) ** 0.25
EPS_P = 1e-5 / (ALPHA * ALPHA)
NEG = -30000.0

COMPUTE = ("pe", "act", "dve", "pool")


class Prog:
    def __init__(self, nc):
        self.nc = nc
        self.ops = []
        self.barriers = set()
        self.sb_base = 16512
        self.sb_top = 229344
        self.sb_off = self.sb_base
        self.sb_max = 0
        self._n = 0

    def sb(self, name, shape, dtype):
        sz = int(np.prod(shape[1:])) * mybir.dt.size(dtype)
        off = (self.sb_off + 63) // 64 * 64
        self._n += 1
        assert off + sz <= self.sb_top, (name, off + sz, self.sb_top)
        t = self.nc.alloc_sbuf_tensor_at(f"{name}_{self._n}", list(shape), dtype, offset=off)
        self.sb_off = off + sz
        self.sb_max = max(self.sb_max, self.sb_off)
        return t

    def mark(self):
        return self.sb_off

    def release(self, m):
        self.sb_off = m

    def op(self, eng, fn, r=(), w=(), stream=None):
        self.ops.append([eng, fn, tuple(r), tuple(w), stream])

    def dma(self, eng, out, in_, r=(), w=(), stream=None, **kw):
        assert stream is not None
        self.op(eng, lambda e: e.dma_start(out=out, in_=in_, **kw), r, w, stream)

    def barrier(self):
        self.barriers.add(len(self.ops))

    def emit(self):
        nc = self.nc
        ops = self.ops
        n = len(ops)
        last_w, readers = {}, {}
        deps = [None] * n
        eng_idx = [0] * n
        eng_count, last_on_eng, last_on_stream, pending = {}, {}, {}, {}
        for i, (eng, fn, r, w, stream) in enumerate(ops):
            if i in self.barriers:
                bd = set(last_on_eng.values()) | set(last_on_stream.values())
                for e in COMPUTE + ("sp",):
                    pending.setdefault(e, set()).update(bd)
            d = set()
            if eng in pending:
                d |= pending.pop(eng)
            for k in r:
                if k in last_w:
                    d.add(last_w[k])
            for k in w:
                if k in last_w:
                    d.add(last_w[k])
                d.update(readers.get(k, ()))
            if stream is not None and stream in last_on_stream:
                d.add(last_on_stream[stream])
            d.discard(i)
            best = {}
            d2 = set()
            for jx in d:
                if ops[jx][4] is None:
                    ej = ops[jx][0]
                    if ej not in best or jx > best[ej]:
                        best[ej] = jx
                else:
                    d2.add(jx)
            d2.update(best.values())
            deps[i] = d2
            for k in r:
                readers.setdefault(k, []).append(i)
            for k in w:
                last_w[k] = i
                readers[k] = []
            eng_idx[i] = eng_count.get(eng, 0)
            eng_count[eng] = eng_idx[i] + 1
            if stream is None:
                last_on_eng[eng] = i
            else:
                last_on_stream[stream] = i
        need_sig = [False] * n
        for i in range(n):
            eng = ops[i][0]
            keep = set()
            for j in deps[i]:
                ej, sj = ops[j][0], ops[j][4]
                if sj is None and ej == eng:
                    if eng == "pe":
                        continue
                    if ops[i][4] is None and eng_idx[i] - eng_idx[j] >= 3:
                        continue
                keep.add(j)
                need_sig[j] = True
            deps[i] = keep
        sigval = [0] * n
        cnt = {}
        for i in range(n):
            eng, _, _, _, stream = ops[i]
            if stream is not None:
                key = ("dma", stream)
                cnt[key] = cnt.get(key, 0) + 16
                sigval[i] = cnt[key]
            elif need_sig[i]:
                key = ("eng", eng)
                cnt[key] = cnt.get(key, 0) + 1
                sigval[i] = cnt[key]
        waits = [None] * n
        seen = {}
        for i in range(n):
            eng = ops[i][0]
            need = {}
            for j in deps[i]:
                ej, sj = ops[j][0], ops[j][4]
                key = ("dma", sj) if sj is not None else ("eng", ej)
                need[key] = max(need.get(key, 0), sigval[j])
            wl = []
            for key, v in need.items():
                if seen.get((eng, key), 0) >= v:
                    continue
                seen[(eng, key)] = v
                wl.append((key, v))
            waits[i] = wl
        keys = list(cnt.keys())
        self.n_sems = len(keys)
        self.cnt = cnt
        self.plan = (deps, sigval, need_sig, waits)
        with contextlib.ExitStack() as es:
            sems = {}
            for k in keys:
                sems[k] = es.enter_context(nc.semaphore(f"s{len(sems)}"))
            block = es.enter_context(nc.Block())
            per_eng = {}
            for i, o in enumerate(ops):
                per_eng.setdefault(o[0], []).append(i)

            def run(eng_name, e):
                for i in per_eng.get(eng_name, []):
                    _, fn, _, _, stream = ops[i]
                    for key, v in waits[i]:
                        e.wait_ge(sems[key], v)
                    ins = fn(e)
                    if stream is not None:
                        ins.then_inc(sems[("dma", stream)], 16)
                    elif need_sig[i]:
                        ins.then_inc(sems[("eng", eng_name)], 1)
                if eng_name == "sp":
                    for k in keys:
                        e.wait_ge(sems[k], cnt[k])

            @block.sync
            def _(e):
                run("sp", e)

            @block.tensor
            def _(e):
                run("pe", e)

            @block.scalar
            def _(e):
                run("act", e)

            @block.vector
            def _(e):
                run("dve", e)

            @block.gpsimd
            def _(e):
                run("pool", e)
        return nc


class Ctx:
    pass


def PS(i):
    return ("ps", i)


def declare(nc, C, n_layers, dbg):
    def inp(name, shape, dt=F32):
        return nc.dram_tensor(name, list(shape), dt, kind="ExternalInput").ap()

    C.xin = inp("xin", [D, T])
    C.cvec = inp("cvec", [128, 16])
    C.bada = inp("bada", [128, DEPTH * 144])
    C.lng = inp("lng", [128, 96])
    C.lnb = inp("lnb", [128, 96])
    C.ident = inp("ident", [128, 128])
    C.w_ada = inp("w_ada", [DEPTH, D, 9 * D])
    C.ffg = [inp("ff1_gate", [DEPTH, D, DFF]), inp("ff2_gate", [DEPTH, D, DFF])]
    C.ffu = [inp("ff1_up", [DEPTH, D, DFF]), inp("ff2_up", [DEPTH, D, DFF])]
    C.ffd = [inp("ff1_down", [DEPTH, DFF, D]), inp("ff2_down", [DEPTH, DFF, D])]
    C.w_in = inp("w_in", [DEPTH, D, 2304])
    C.w_out = inp("w_out", [DEPTH, D, D])
    C.lruv = inp("lruv", [128, DEPTH * 2 * 11])
    C.lru_wa = inp("lru_wa", [DEPTH, 2, 4, 64, 64])
    C.lru_wx = inp("lru_wx", [DEPTH, 2, 4, 64, 64])
    C.fno_w = inp("fno_w", [DEPTH, 4, 64, 64])
    C.c64bd = inp("c64bd", [128, 128])
    C.s64bd = inp("s64bd", [128, 128])
    C.cw128 = inp("cw128", [128, 128])
    C.sw128 = inp("sw128", [128, 128])
    C.nsw128 = inp("nsw128", [128, 128])
    C.wtC = inp("wtC", [128, 128 * 64])
    C.t256 = inp("t256", [128, 2 * 2 * 256])
    C.rpbT = inp("rpbT", [DEPTH, 8, 128, 1920])
    C.cmask = inp("cmask", [2, 128, 1920])
    C.rmask = inp("rmask", [2, 128, 8 * 512])
    C.out = nc.dram_tensor("out", [D, NLAT], F32, kind="ExternalOutput").ap()
    sk = "ExternalOutput" if dbg else "Internal"
    C.QK = nc.dram_tensor("QK", [1024, T], BF16, kind=sk).ap()
    C.XG = nc.dram_tensor("XG", [512, T], F32, kind=sk).ap()
    C.VA = nc.dram_tensor("VA", [T, 1024], BF16, kind=sk).ap()
    C.G = nc.dram_tensor("G", [T, 512], BF16, kind=sk).ap()
    C.AB = nc.dram_tensor("AB", [2, 128, 64, 256], BF16, kind=sk).ap()
    C.CAT = nc.dram_tensor("CAT", [1024, T], BF16, kind=sk).ap()
    C.XT = nc.dram_tensor("XT", [D, T], F32, kind="Internal").ap()
    if dbg:
        C.dbg = nc.dram_tensor("dbg", [D, T], F32, kind="ExternalOutput").ap()
        C.dbg2 = nc.dram_tensor("dbg2", [128, DEPTH * 144], F32, kind="ExternalOutput").ap()
    C.psb = [nc.alloc_psum_tensor(f"psb{i}", [128, 512], F32) for i in range(8)]


def tiles_all():
    tl = [(0, NCTX, 1)]
    for t in range(NLAT // 512):
        tl.append((NCTX + 512 * t, 512, 0))
    return tl


def prologue(P, C, n_layers):
    nc = P.nc
    C.identb = P.sb("identb", [128, 128], BF16)
    C.onesb = P.sb("onesb", [128, 128], BF16)
    C.M = P.sb("M", [128, DEPTH * 144], F32)
    C.S1P = P.sb("S1P", [128, DEPTH * 48], F32)
    C.GS = P.sb("GS", [128, DEPTH * 48], F32)
    C.LNG = P.sb("LNG", [128, 96], F32)
    C.LNB = P.sb("LNB", [128, 96], F32)
    C.epsc = P.sb("epsc", [128, 1], F32)
    C.zc = P.sb("zc", [128, 1], F32)
    P.op("dve", lambda e: e.memset(C.zc[:], 0.0), w=["zc"])
    P.dma("pool", C.identb[:], C.ident, w=["identb"], stream="c0")
    P.dma("sp", C.LNG[:], C.lng, w=["LNG"], stream="c1")
    P.dma("sp", C.LNB[:], C.lnb, w=["LNB"], stream="c2")
    P.op("dve", lambda e: e.memset(C.onesb[:], 1.0 / D), w=["onesb"])
    P.op("dve", lambda e: e.memset(C.epsc[:], EPS_P), w=["epsc"])
    C.scsb = P.sb("scsb", [128, 16], BF16)
    mk = P.mark()
    cs = P.sb("cs", [128, 16], F32)
    P.dma("sp", cs[:], C.cvec, w=["cs"], stream="c3")
    P.op("act", lambda e: e.activation(out=C.scsb[:], in_=cs[:], func=AF.Silu), r=["cs"], w=["scsb"])
    bufs = adaln_alloc(P)
    for j9 in range(10):
        adaln_step(P, C, 0, bufs, j9, 7)
    P.barrier()
    P.release(mk)


def adaln_alloc(P):
    wa = [P.sb(f"wa{i}", [128, 8, D], BF16) for i in range(2)]
    bad = P.sb("bad", [128, 144], F32)
    return wa, bad


def adaln_step(P, C, l, bufs, j9, bank):
    wa, bad = bufs
    ps = C.psb[bank]
    if j9 < 9:
        buf = wa[j9 % 2]
        bk = ("wa", j9 % 2)
        if j9 == 0:
            P.dma("sp", bad[:], C.bada[:, l * 144:(l + 1) * 144], w=["bad"], stream="c4")
        src = C.w_ada[l, :, j9 * D:(j9 + 1) * D].rearrange("(k p) n -> p k n", p=128)
        P.dma("pool", buf[:], src, w=[bk], stream=f"wa{j9 % 2}")
        for mo in range(8):
            col = (j9 * 8 + mo) * 2
            for k in range(8):
                P.op("pe", lambda e, k=k, mo=mo, col=col: e.matmul(
                    ps[:, col:col + 2], lhsT=buf[:, k, mo * 128:(mo + 1) * 128], rhs=C.scsb[:, 2 * k:2 * k + 2],
                    start=(k == 0), stop=(k == 7)), r=[bk, "scsb"], w=[PS(bank)])
        return
    P.op("dve", lambda e: e.tensor_tensor(out=C.M[:, l * 144:(l + 1) * 144], in0=ps[:, 0:144], in1=bad[:, :],
                                          op=ALU.add), r=["bad"], w=[PS(bank), "M"])
    for j in range(3):
        gc = (0.5 if j != 1 else 1.0) / ALPHA
        P.op("dve", lambda e, j=j: e.tensor_scalar(
            out=C.S1P[:, (l * 3 + j) * 16:(l * 3 + j + 1) * 16],
            in0=C.M[:, l * 144 + (3 * j + 1) * 16: l * 144 + (3 * j + 2) * 16],
            scalar1=1.0, scalar2=None, op0=ALU.add), r=["M"], w=["S1P"])
        P.op("dve", lambda e, j=j, gc=gc: e.tensor_scalar(
            out=C.GS[:, (l * 3 + j) * 16:(l * 3 + j + 1) * 16],
            in0=C.M[:, l * 144 + (3 * j + 2) * 16: l * 144 + (3 * j + 3) * 16],
            scalar1=gc, scalar2=None, op0=ALU.mult), r=["M"], w=["GS"])


def mod_aps(C, l, j, m, s):
    i = ((l * 3 + j) * 8 + m) * 2 + s
    sh = l * 144 + ((3 * j) * 8 + m) * 2 + s
    return C.S1P[:, i:i + 1], C.M[:, sh:sh + 1], C.GS[:, i:i + 1]


def ln_tail(P, C, l, j, xb, xkey, n, ps_mean, ps_ex2, st, pfx, inter=None, every=False):
    msq = st
    kmean, kex2 = PS(ps_mean), PS(ps_ex2)
    pm, pe2 = C.psb[ps_mean], C.psb[ps_ex2]
    inter = list(inter) if inter else []

    def nxt():
        if inter:
            inter.pop(0)()

    nxt()
    P.op("act", lambda e: e.activation(out=msq[:, :n], in_=pm[:, :n], func=AF.Square), w=[kmean, pfx + "msq"])
    P.op("dve", lambda e: e.tensor_tensor(out=msq[:, :n], in0=pe2[:, :n], in1=msq[:, :n], op=ALU.subtract),
         w=[kex2, pfx + "msq"])
    P.op("act", lambda e: e.activation(out=msq[:, :n], in_=msq[:, :n], func=AF.Sqrt, bias=C.epsc[:, 0:1]),
         r=["epsc"], w=[pfx + "msq"])
    P.op("dve", lambda e: e.reciprocal(out=msq[:, :n], in_=msq[:, :n]), w=[pfx + "msq"])
    nxt()
    for m in range(8):
        g = C.LNG[:, (l * 3 + j) * 8 + m:(l * 3 + j) * 8 + m + 1]
        b = C.LNB[:, (l * 3 + j) * 8 + m:(l * 3 + j) * 8 + m + 1]
        P.op("dve", (lambda m=m: lambda e: e.tensor_tensor(out=xb[:, m, :n], in0=xb[:, m, :n], in1=pm[:, :n],
                                                           op=ALU.subtract))(), w=[kmean, (xkey, m)])
        P.op("dve", (lambda m=m: lambda e: e.tensor_tensor(out=xb[:, m, :n], in0=xb[:, m, :n], in1=msq[:, :n],
                                                           op=ALU.mult))(), r=[pfx + "msq"], w=[(xkey, m)])
        P.op("act", (lambda m=m, g=g, b=b: lambda e: e.activation(out=xb[:, m, :n], in_=xb[:, m, :n],
                                                                  func=AF.Identity, scale=g, bias=b))(),
             r=["LNG", "LNB"], w=[(xkey, m)])
        if every or m % 2 == 1:
            nxt()
    while inter:
        nxt()


def ffn_phase(P, C, l, which, src, dst_fn, tiles):
    j = 0 if which == 0 else 2
    P.barrier()
    mk = P.mark()
    wg = P.sb("wg", [128, 8, DFF], BF16)
    wu = P.sb("wu", [128, 8, DFF], BF16)
    wd = P.sb("wd", [128, NJ, D], BF16)
    Wg = C.ffg[which][l].rearrange("(k p) n -> p k n", p=128)
    Wu = C.ffu[which][l].rearrange("(k p) n -> p k n", p=128)
    Wd = C.ffd[which][l].rearrange("(k p) n -> p k n", p=128)
    for k in range(8):
        P.dma("pool", wg[:, k, :], Wg[:, k, :], w=[("wg", k)], stream=f"w{k % 4}")
        P.dma("pool", wu[:, k, :], Wu[:, k, :], w=[("wu", k)], stream=f"w{4 + k % 4}")
    for k in range(0, NJ, 2):
        P.dma("pool", wd[:, k:k + 2, :], Wd[:, k:k + 2, :], w=[("wd", k), ("wd", k + 1)], stream=f"w{8 + (k // 2) % 4}")
    xb = [P.sb(f"xb{i}", [128, 8, 512], F32) for i in range(2)]
    h = P.sb("h", [128, 8, 512], BF16)
    a = P.sb("a", [128, NJ, 512], BF16)
    sg = [P.sb(f"sg{i}", [128, 512], BF16) for i in range(2)]
    zb = [P.sb(f"zb{i}", [128, 512], BF16) for i in range(2)]
    zq = [P.sb(f"zq{i}", [128, 512], BF16) for i in range(2)]
    st = P.sb("msq", [128, 512], F32)
    srcv = src.rearrange("(m p) t -> p m t", p=128)

    def load(ti):
        t0, n, s = tiles[ti]
        P.dma("sp", xb[ti % 2][:, :, :n], srcv[:, :, t0:t0 + n], w=[(f"xb{ti % 2}", m) for m in range(8)],
              stream=f"xl{ti % 2}")

    def modulate(ti):
        t0, n, s = tiles[ti]
        X = xb[ti % 2]
        xk = f"xb{ti % 2}"
        for m in range(8):
            s1p, sh, gs = mod_aps(C, l, j, m, s)
            P.op("act", lambda e, m=m, s1p=s1p, sh=sh: e.activation(
                out=h[:, m, :n], in_=X[:, m, :n], func=AF.Identity, scale=s1p, bias=sh),
                r=[(xk, m), "S1P", "M"], w=[("h", m)])

    KPRE = 6

    def gate_up(ti, jlo, jhi):
        t0, n, s = tiles[ti]
        for jj in range(jlo, jhi):
            pg, pu = jj % 2, 2 + jj % 2
            for k in range(8):
                P.op("pe", lambda e, k=k, jj=jj, pg=pg: e.matmul(
                    C.psb[pg][:, :n], lhsT=wg[:, k, jj * 128:(jj + 1) * 128], rhs=h[:, k, :n],
                    start=(k == 0), stop=(k == 7)), r=[("wg", k), ("h", k)], w=[PS(pg)])
            for k in range(8):
                P.op("pe", lambda e, k=k, jj=jj, pu=pu: e.matmul(
                    C.psb[pu][:, :n], lhsT=wu[:, k, jj * 128:(jj + 1) * 128], rhs=h[:, k, :n],
                    start=(k == 0), stop=(k == 7)), r=[("wu", k), ("h", k)], w=[PS(pu)])
            P.op("act", lambda e, jj=jj, pg=pg: e.activation(
                out=sg[jj % 2][:, :n], in_=C.psb[pg][:, :n], func=AF.Silu), w=[PS(pg), ("sg", jj % 2)])
            P.op("dve", lambda e, jj=jj, pu=pu: e.tensor_tensor(
                out=a[:, jj, :n], in0=C.psb[pu][:, :n], in1=sg[jj % 2][:, :n], op=ALU.mult),
                r=[("sg", jj % 2)], w=[PS(pu), ("a", jj)])

    def tile_body(ti, t0, n, s):
        X = xb[ti % 2]
        xk = f"xb{ti % 2}"
        gate_up(ti, 0 if ti == 0 else KPRE, NJ)
        if ti + 1 < len(tiles):
            modulate(ti + 1)

        def y_mms(m, py):
            for jj in range(NJ):
                P.op("pe", lambda e, jj=jj: e.matmul(
                    C.psb[py][:, :n], lhsT=wd[:, jj, m * 128:(m + 1) * 128], rhs=a[:, jj, :n],
                    start=(jj == 0), stop=(jj == NJ - 1)), r=[("wd", jj), ("a", jj)], w=[PS(py)])

        inter = []
        if ti + 1 < len(tiles):
            inter = [(lambda q=q: gate_up(ti + 1, q, q + 1)) for q in range(KPRE)]
        resid_ln_part(P, C, l, j, X, xk, n, s, zb, zq, st, y_mms, inter)
        dstv = dst_fn(ti)
        if dstv is not None:
            P.dma("sp", dstv, X[:, :, :n], r=[(xk, m) for m in range(8)], stream=f"xs{ti % 2}")

    load(0)
    modulate(0)
    for ti, (t0, n, s) in enumerate(tiles):
        if ti + 1 < len(tiles):
            load(ti + 1)
        tile_body(ti, t0, n, s)
    P.release(mk)


def resid_ln_part(P, C, l, j, X, xk, n, s, zb, zq, msq, y_mms, before_ln=None):
    def stats(m):
        P.op("pe", lambda e: e.matmul(C.psb[6][:, :n], lhsT=C.onesb[:, :], rhs=zb[m % 2][:, :n],
                                      start=(m == 0), stop=(m == 7)), r=["onesb", ("zb", m % 2)], w=[PS(6)])
        P.op("pe", lambda e: e.matmul(C.psb[7][:, :n], lhsT=C.onesb[:, :], rhs=zq[m % 2][:, :n],
                                      start=(m == 0), stop=(m == 7)), r=["onesb", ("zq", m % 2)], w=[PS(7)])

    for m in range(8):
        py = 4 + m % 2
        s1p, sh, gs = mod_aps(C, l, j, m, s)
        y_mms(m, py)

        def ep(m=m, py=py, gs=gs):
            P.op("dve", lambda e: e.scalar_tensor_tensor(
                out=X[:, m, :n], in0=C.psb[py][:, :n], scalar=gs, in1=X[:, m, :n], op0=ALU.mult, op1=ALU.add),
                r=["GS"], w=[PS(py), (xk, m)])
            P.op("act", lambda e: e.activation(out=zb[m % 2][:, :n], in_=X[:, m, :n], func=AF.Identity),
                 r=[(xk, m)], w=[("zb", m % 2)])
            P.op("act", lambda e: e.activation(out=zq[m % 2][:, :n], in_=X[:, m, :n], func=AF.Square),
                 r=[(xk, m)], w=[("zq", m % 2)])
        ep()
        if m > 0:
            stats(m - 1)
    stats(7)
    ln_tail(P, C, l, j, X, xk, n, 6, 7, msq, "f", before_ln)


def inproj_phase(P, C, l, tiles, n_layers=DEPTH):
    j = 1
    P.barrier()
    mk = P.mark()
    win = P.sb("win", [128, 8, 2304], BF16)
    Wv = C.w_in[l].rearrange("(k p) n -> p k n", p=128)
    for k in range(8):
        P.dma("pool", win[:, k, :], Wv[:, k, :], w=[("win", k)], stream=f"w{k % 4}")
    fw = P.sb("fw", [128, 2, 64], F32)
    cbd = P.sb("cbd", [128, 128], F32)
    sbd = P.sb("sbd", [128, 128], F32)
    BD = P.sb("BD", [128, 2, 512], BF16)
    P.dma("sp", fw[:], C.fno_w[l].rearrange("(kc g2) c j -> (g2 c) kc j", g2=2), w=["fw"], stream="m0")
    P.dma("sp", cbd[:], C.c64bd, w=["cbd"], stream="m1")
    P.dma("sp", sbd[:], C.s64bd, w=["sbd"], stream="m2")
    P.op("pool", lambda e: e.memset(BD[:], 0.0), w=["BD"])
    for kc in range(2):
        for ti_, tab in enumerate((cbd, sbd)):
            def bdm(kc=kc, ti_=ti_, tab=tab):
                tk = "cbd" if ti_ == 0 else "sbd"
                P.op("pe", lambda e: e.matmul(C.psb[0][:, 0:64], lhsT=tab[:, :], rhs=fw[:, kc, :], start=True, stop=True),
                     r=[tk, "fw"], w=[PS(0)])
                for g2 in range(2):
                    c0 = ti_ * 256 + (2 * kc + g2) * 64
                    P.op("dve", lambda e, g2=g2, c0=c0: e.tensor_copy(
                        out=BD[g2 * 64:(g2 + 1) * 64, kc, c0:c0 + 64], in_=C.psb[0][g2 * 64:(g2 + 1) * 64, 0:64]),
                        w=[PS(0), "BD"])
            bdm()
    xb = [P.sb(f"xb{i}", [128, 8, 512], F32) for i in range(2)]
    h = P.sb("h", [128, 8, 512], BF16)
    qk = [P.sb(f"qk{i}", [128, 8, 512], BF16) for i in range(2)]
    xg = [P.sb(f"xg{i}", [128, 4, 512], F32) for i in range(2)]
    fsb = P.sb("fsb", [128, 2, 512], BF16)
    vas = [P.sb(f"vas{i}", [128, 4, 1024], BF16) for i in range(2)]
    gsb = [P.sb(f"gsb{i}", [128, 4, 512], BF16) for i in range(2)]
    for i in range(2):
        P.op("pool", lambda e, i=i: e.memset(vas[i][:], 1.0), w=[("vas", i)])
    srcv = C.XT.rearrange("(m p) t -> p m t", p=128)
    QKv = C.QK.rearrange("(c p) t -> p c t", p=128)
    XGv = C.XG.rearrange("(c p) t -> p c t", p=128)
    cols = [c * 128 for c in range(8)] + [1536 + c * 128 for c in range(6)]

    def load(ti):
        t0, n, s = tiles[ti]
        P.dma("sp", xb[ti % 2][:, :, :n], srcv[:, :, t0:t0 + n], w=[(f"xb{ti % 2}", m) for m in range(8)],
              stream=f"xl{ti % 2}")

    def tile_body(ti, t0, n, s):
        X = xb[ti % 2]
        xk = f"xb{ti % 2}"
        b2 = ti % 2
        for m in range(8):
            s1p, sh, gs = mod_aps(C, l, j, m, s)
            P.op("act", lambda e, m=m, s1p=s1p, sh=sh: e.activation(
                out=h[:, m, :n], in_=X[:, m, :n], func=AF.Identity, scale=s1p, bias=sh),
                r=[(xk, m), "S1P", "M"], w=[("h", m)])
        for ci, col in enumerate(cols):
            bank = ci % 4
            for k in range(8):
                P.op("pe", lambda e, k=k, col=col, bank=bank: e.matmul(
                    C.psb[bank][:, :n], lhsT=win[:, k, col:col + 128], rhs=h[:, k, :n],
                    start=(k == 0), stop=(k == 7)), r=[("win", k), ("h", k)], w=[PS(bank)])
            if ci < 4:
                P.op("act", lambda e, ci=ci, bank=bank: e.activation(
                    out=qk[b2][:, ci, :n], in_=C.psb[bank][:, :n], func=AF.Identity, scale=0.125),
                    w=[PS(bank), ("qk", b2)])
            elif ci < 8:
                P.op("dve", lambda e, ci=ci, bank=bank: e.tensor_copy(out=qk[b2][:, ci, :n], in_=C.psb[bank][:, :n]),
                     w=[PS(bank), ("qk", b2)])
            elif ci < 12:
                eng = "act" if ci % 2 == 0 else "dve"
                if eng == "act":
                    P.op("act", lambda e, ci=ci, bank=bank: e.activation(
                        out=xg[b2][:, ci - 8, :n], in_=C.psb[bank][:, :n], func=AF.Identity),
                        w=[PS(bank), ("xg", b2)])
                else:
                    P.op("dve", lambda e, ci=ci, bank=bank: e.tensor_copy(
                        out=xg[b2][:, ci - 8, :n], in_=C.psb[bank][:, :n]), w=[PS(bank), ("xg", b2)])
            else:
                P.op("dve", lambda e, ci=ci, bank=bank: e.tensor_copy(out=fsb[:, ci - 12, :n], in_=C.psb[bank][:, :n]),
                     w=[PS(bank), "fsb"])
        nsub = n // 128
        for sub in range(nsub):
            bank = 4 + sub % 2
            for k in range(8):
                P.op("pe", lambda e, k=k, sub=sub, bank=bank: e.matmul(
                    C.psb[bank][:, :], lhsT=h[:, k, sub * 128:(sub + 1) * 128], rhs=win[:, k, 1024:1536],
                    start=(k == 0), stop=(k == 7)), r=[("win", k), ("h", k)], w=[PS(bank)])
            pv = C.psb[bank][:, :].rearrange("p (hp two d) -> p hp two d", two=2, d=64)
            vv = vas[b2][:, sub, :].rearrange("p (hp two d) -> p hp two d", two=2, d=128)
            P.op("act", lambda e, pv=pv, vv=vv: e.activation(out=vv[:, :, 0, 0:64], in_=pv[:, :, 0, :], func=AF.Identity),
                 w=[PS(bank), ("vas", b2)])
            P.op("dve", lambda e, pv=pv, vv=vv: e.tensor_copy(out=vv[:, :, 1, 64:128], in_=pv[:, :, 1, :]),
                 w=[PS(bank), ("vas", b2)])
        for sub in range(nsub):
            bank = 6
            for kc in range(2):
                P.op("pe", lambda e, kc=kc, sub=sub, bank=bank: e.matmul(
                    C.psb[bank][:, :], lhsT=fsb[:, kc, sub * 128:(sub + 1) * 128], rhs=BD[:, kc, :],
                    start=(kc == 0), stop=(kc == 1)), r=["fsb", "BD"], w=[PS(bank)])
            if sub % 2 == 0:
                P.op("act", lambda e, sub=sub, bank=bank: e.activation(out=gsb[b2][:, sub, :], in_=C.psb[bank][:, :],
                                                                       func=AF.Identity), w=[PS(bank), ("gsb", b2)])
            else:
                P.op("dve", lambda e, sub=sub, bank=bank: e.tensor_copy(out=gsb[b2][:, sub, :], in_=C.psb[bank][:, :]),
                     w=[PS(bank), ("gsb", b2)])
        P.dma("sp", QKv[:, :, t0:t0 + n], qk[b2][:, :, :n], r=[("qk", b2)], stream=f"sq{b2}")
        P.dma("sp", XGv[:, :, t0:t0 + n], xg[b2][:, :, :n], r=[("xg", b2)], stream=f"sx{b2}")
        P.dma("sp", C.VA[t0:t0 + n, :].rearrange("(s p) c -> p s c", p=128), vas[b2][:, :nsub, :],
              r=[("vas", b2)], stream=f"sv{b2}")
        P.dma("sp", C.G[t0:t0 + n, :].rearrange("(s p) c -> p s c", p=128), gsb[b2][:, :nsub, :],
              r=[("gsb", b2)], stream=f"sg{b2}")

    nxt = (l + 1 < n_layers)
    if nxt:
        abufs = adaln_alloc(P)
    load(0)
    for ti, (t0, n, s) in enumerate(tiles):
        if ti + 1 < len(tiles):
            load(ti + 1)
        tile_body(ti, t0, n, s)
        if nxt and ti < 10:
            adaln_step(P, C, l + 1, abufs, ti, 7)
    P.release(mk)


def attn_phase(P, C, l, need_ctx):
    P.barrier()
    mk = P.mark()
    Tbi = P.sb("Tbi", [128, 8, 1920], BF16)
    Tbf = P.sb("Tbf", [128, 8, 1920], BF16)
    rm = [P.sb(f"rm{i}", [128, 8, 512], BF16) for i in range(2)]
    kTc = P.sb("kTc", [128, 4, 256], BF16)
    vac = P.sb("vac", [128, 2, 1024], BF16)
    mk2 = P.mark()
    cmi = P.sb("cmi", [128, 1920], F32)
    cmf = P.sb("cmf", [128, 1920], F32)
    stg = [P.sb(f"stg{i}", [128, 1920], F32) for i in range(2)]
    P.dma("sp", cmi[:], C.cmask[0], w=["cmi"], stream="m0")
    P.dma("sp", cmf[:], C.cmask[1], w=["cmf"], stream="m1")
    for i in range(2):
        P.dma("pool", rm[i][:].rearrange("p a c -> p (a c)"), C.rmask[i], w=[("rm", i)], stream=f"w{i}")
    for hh in range(8):
        def tb(hh=hh):
            P.dma("sp", stg[hh % 2][:], C.rpbT[l, hh], w=[("stg", hh % 2)], stream=f"m{2 + hh % 2}")
            P.op("dve", lambda e: e.tensor_tensor(out=Tbi[:, hh, :], in0=stg[hh % 2][:], in1=cmi[:], op=ALU.add),
                 r=[("stg", hh % 2), "cmi"], w=["Tbi"])
            P.op("pool", lambda e: e.tensor_tensor(out=Tbf[:, hh, :], in0=stg[hh % 2][:], in1=cmf[:], op=ALU.add),
                 r=[("stg", hh % 2), "cmf"], w=["Tbf"])
        tb()
    QKv = C.QK.rearrange("(c p) t -> p c t", p=128)
    P.dma("sp", kTc[:], QKv[:, 4:8, 0:256], w=["kTc"], stream="m4")
    P.dma("sp", vac[:], C.VA[0:256, :].rearrange("(a p) c -> p a c", p=128), w=["vac"], stream="m5")
    P.barrier()
    P.release(mk2)
    qT = [P.sb(f"qT{i}", [128, 4, 512], BF16) for i in range(2)]
    kT = [P.sb(f"kT{i}", [128, 4, 1024], BF16) for i in range(2)]
    va = [P.sb(f"va{i}", [128, 8, 1024], BF16) for i in range(2)]
    NS = 6
    SBK = [0, 1, 2, 3, 6, 7]
    PT = [P.sb(f"PT{i}", [128, 512], BF16) for i in range(NS)]
    rd = [P.sb(f"rd{i}", [128, 512], F32) for i in range(2)]
    oc = [P.sb(f"oc{i}", [128, 4, 512], BF16) for i in range(2)]
    CATv = C.CAT.rearrange("(c p) t -> p c t", p=128)

    blocks = []
    if need_ctx:
        blocks.append(("ctx", 0, 256, None, None))
    for b in range(16):
        a0 = min(max(4 * b - 2, 0), 56)
        blocks.append(("lat", 256 + 512 * b, 512, b, a0))

    def load(bi):
        kind, q0, n, b, a0 = blocks[bi]
        P.dma("sp", qT[bi % 2][:, :, :n], QKv[:, 0:4, q0:q0 + n], w=[("qT", bi % 2)], stream=f"q{bi % 2}")
        if kind == "lat":
            k0 = 256 + 128 * a0
            P.dma("sp", kT[bi % 2][:], QKv[:, 4:8, k0:k0 + 1024], w=[("kT", bi % 2)], stream=f"k{bi % 2}")
            P.dma("sp", va[bi % 2][:], C.VA[k0:k0 + 1024, :].rearrange("(a p) c -> p a c", p=128),
                  w=[("va", bi % 2)], stream=f"v{bi % 2}")

    state = {"sidx": 0}

    def block_body(bi):
        kind, q0, n, b, a0 = blocks[bi]
        b2 = bi % 2
        chunks = [("ctx", 0, 0, n), ("ctx", 1, 0, n)]
        if kind == "lat":
            if b == 0:
                chunks += [("loc", ai, 0, n) for ai in range(0, 6)]
            elif b == 15:
                chunks += [("loc", ai, 0, n) for ai in range(2, 8)]
            else:
                for ai in range(8):
                    ilo, ihi = max(0, 2 * ai - 7), min(7, 2 * ai + 1)
                    chunks.append(("loc", ai, 64 * ilo, 64 * (ihi + 1)))
        items = [(hh, ch) for hh in range(8) for ch in chunks]
        nch = len(chunks)
        base = state["sidx"]
        state["sidx"] += len(items)
        edge = b in (0, 15)

        def emit_S(idx):
            hh, ch = items[idx]
            hc, pb = hh // 2, 64 * (hh % 2)
            si = (base + idx) % NS
            sb_ = SBK[si]
            c0, c1 = ch[2], ch[3]
            q_ap = qT[b2][pb:pb + 64, hc, c0:c1]
            if ch[0] == "ctx":
                ci = ch[1]
                P.op("pe", lambda e: e.matmul(C.psb[sb_][:, c0:c1], lhsT=kTc[pb:pb + 64, hc, ci * 128:(ci + 1) * 128],
                                              rhs=q_ap, start=True, stop=True),
                     r=["kTc", ("qT", b2)], w=[PS(sb_)])
            else:
                ai = ch[1]
                a = a0 + ai
                e0 = 8 * b - 2 * a + 14
                tb_ = Tbf if edge else Tbi
                P.op("pe", lambda e: e.matmul(C.psb[sb_][:, c0:c1], lhsT=kT[b2][pb:pb + 64, hc, ai * 128:(ai + 1) * 128],
                                              rhs=q_ap, start=True, stop=False),
                     r=[("kT", b2), ("qT", b2)], w=[PS(sb_)])
                P.op("pe", lambda e: e.matmul(C.psb[sb_][:, c0:c1], lhsT=C.identb[:, :],
                                              rhs=tb_[:, hh, e0 * 64 + c0:e0 * 64 + c1], start=False, stop=(not edge)),
                     r=["identb", "Tbf" if edge else "Tbi"], w=[PS(sb_)])
                if edge:
                    ri = 0 if b == 0 else 1
                    P.op("pe", lambda e: e.matmul(C.psb[sb_][:, c0:c1], lhsT=C.identb[:, :], rhs=rm[ri][:, ai, c0:c1],
                                                  start=False, stop=True), r=["identb", ("rm", ri)], w=[PS(sb_)])

        def emit_exp(idx):
            hh, ch = items[idx]
            si = (base + idx) % NS
            sb_ = SBK[si]
            c0, c1 = ch[2], ch[3]
            P.op("act", lambda e: e.activation(out=PT[si][:, c0:c1], in_=C.psb[sb_][:, c0:c1], func=AF.Exp),
                 w=[PS(sb_), ("PT", si)])

        def emit_PV(idx):
            hh, ch = items[idx]
            hc, pb = hh // 2, 64 * (hh % 2)
            si = (base + idx) % NS
            ob = 4 + hh % 2
            c0, c1 = ch[2], ch[3]
            ci_in_head = idx % nch
            if ch[0] == "ctx":
                lhs = vac[:, ch[1], hh * 128:(hh + 1) * 128]
                rk = ["vac", ("PT", si)]
            else:
                lhs = va[b2][:, ch[1], hh * 128:(hh + 1) * 128]
                rk = [("va", b2), ("PT", si)]
            P.op("pe", lambda e: e.matmul(C.psb[ob][:, c0:c1], lhsT=lhs, rhs=PT[si][:, c0:c1],
                                          start=(ci_in_head == 0), stop=(ci_in_head == nch - 1)),
                 r=rk, w=[PS(ob)])
            if ci_in_head == nch - 1:
                r2 = hh % 2
                if hh % 2 == 0:
                    num, den = slice(0, 64), slice(64, 128)
                else:
                    num, den = slice(64, 128), slice(0, 64)
                P.op("dve", lambda e: e.reciprocal(out=rd[r2][num, :n], in_=C.psb[ob][den, :n]),
                     w=[PS(ob), ("rd", r2)])
                P.op("dve", lambda e: e.tensor_tensor(out=oc[b2][num, hc, :n], in0=C.psb[ob][num, :n],
                                                      in1=rd[r2][num, :n], op=ALU.mult),
                     r=[("rd", r2)], w=[PS(ob), ("oc", b2)])

        AHEAD = 4
        for q_ in range(min(AHEAD, len(items))):
            emit_S(q_)
        for idx in range(len(items)):
            emit_exp(idx)
            if idx + AHEAD < len(items):
                emit_S(idx + AHEAD)
            emit_PV(idx)
        P.dma("sp", CATv[:, 0:4, q0:q0 + n], oc[b2][:, :, :n], r=[("oc", b2)], stream=f"so{b2}")

    load(0)
    for bi in range(len(blocks)):
        if bi + 1 < len(blocks):
            load(bi + 1)
        block_body(bi)
    P.release(mk)


def lru_phase(P, C, l):
    P.barrier()
    mk = P.mark()
    W = T + 8
    SEG = 1024
    XP = P.sb("XP", [128, W], F32)
    XC = P.sb("XC", [128, T], F32)
    XCB = P.sb("XCB", [128, T], BF16)
    HF = P.sb("HF", [128, T], F32)
    HB = P.sb("HB", [128, T], F32)
    half = P.sb("half", [128, 1], F32)
    LV = P.sb("LV", [128, 22], F32)
    NK = P.sb("NK", [128, 4], F32)
    NK2 = P.sb("NK2", [128, 4], F32)
    HBA = P.sb("HBA", [128, 8], F32)
    tiny = P.sb("tiny", [128, 1], F32)
    Wbf = P.sb("Wbf", [128, 8, 128], BF16)
    mkw = P.mark()
    Wst = P.sb("Wst", [128, 8, 128], F32)
    P.dma("sp", LV[:], C.lruv[:, l * 22:(l + 1) * 22], w=["LV"], stream="m0")
    P.op("dve", lambda e: e.memset(tiny[:], 1e-20), w=["tiny"])
    P.op("dve", lambda e: e.memset(half[:], 0.5), w=["half"])
    P.op("pool", lambda e: e.memset(Wst[:], 0.0), w=["Wst"])
    si = 0
    for d in range(2):
        for gi, src in enumerate((C.lru_wa, C.lru_wx)):
            for cc in range(2):
                idx = (d * 2 + gi) * 2 + cc
                for h2 in range(2):
                    P.dma("sp", Wst[h2 * 64:(h2 + 1) * 64, idx, h2 * 64:(h2 + 1) * 64], src[l, d, 2 * cc + h2],
                          w=["Wst"], stream=f"m{1 + si % 4}")
                    si += 1
    P.op("dve", lambda e: e.tensor_copy(out=Wbf[:], in_=Wst[:]), r=["Wst"], w=["Wbf"])

    def xp_load(cc):
        P.op("pool", lambda e: e.memset(XP[:, 0:2], 0.0), w=["XP"])
        P.op("pool", lambda e: e.memset(XP[:, 258:261], 0.0), w=["XP"])
        P.op("pool", lambda e: e.memset(XP[:, 8453:8456], 0.0), w=["XP"])
        P.dma("sp", XP[:, 2:258], C.XG[cc * 128:(cc + 1) * 128, 0:256], w=["XP"], stream="l0")
        P.dma("sp", XP[:, 261:8453], C.XG[cc * 128:(cc + 1) * 128, 256:T], w=["XP"], stream="l1")

    xp_load(0)
    P.barrier()
    P.release(mkw)
    Rb = [[P.sb(f"Rb{d}{k}", [128, SEG], F32) for k in range(2)] for d in range(2)]
    Ib = [[P.sb(f"Ib{d}{k}", [128, SEG], F32) for k in range(2)] for d in range(2)]
    Tb = [[P.sb(f"Tb{d}{k}", [128, SEG], F32) for k in range(2)] for d in range(2)]
    for cc in range(2):
        for d in range(2):
            def kap(cc=cc, d=d):
                lam = LV[:, cc * 11 + 7 + 3 * d: cc * 11 + 8 + 3 * d]
                o = NK[:, cc * 2 + d: cc * 2 + d + 1]
                o2 = NK2[:, cc * 2 + d: cc * 2 + d + 1]
                P.op("act", lambda e: e.activation(out=o2, in_=lam, func=AF.Exp, scale=-1.0), r=["LV"], w=["NK2"])
                P.op("act", lambda e: e.activation(out=o2, in_=o2, func=AF.Ln, bias=1.0), w=["NK2"])
                P.op("dve", lambda e: e.tensor_scalar(out=o, in0=o2, scalar1=-4.0, scalar2=None, op0=ALU.mult),
                     r=["NK2"], w=["NK"])
                for gi in range(2):
                    bsrc = LV[:, cc * 11 + 5 + 3 * d + gi: cc * 11 + 6 + 3 * d + gi]
                    bo = HBA[:, (cc * 2 + d) * 2 + gi:(cc * 2 + d) * 2 + gi + 1]
                    P.op("dve", lambda e, bsrc=bsrc, bo=bo: e.tensor_scalar(out=bo, in0=bsrc, scalar1=0.5, scalar2=None,
                                                                            op0=ALU.mult), r=["LV"], w=["HBA"])
            kap()
    segs = [(0, 256)] + [(256 + SEG * i, SEG) for i in range(8)]
    XGv = C.XG
    CATv = C.CAT

    def rev(ap):
        nn = ap.shape[-1]
        return bass.AP(ap.tensor, ap.offset + (nn - 1), [list(ap.ap[0]), [-1, nn]])

    def per_cc(cc):
        lv = lambda v: LV[:, cc * 11 + v: cc * 11 + v + 1]
        if cc > 0:
            xp_load(cc)
        conv_done = set()

        def conv(si_):
            if si_ in conv_done:
                return
            conv_done.add(si_)
            s0, sn = segs[si_]

            def cv(si_=si_, s0=s0, sn=sn):
                i0 = s0 if s0 < 256 else s0 + 3
                P.op("dve", lambda e: e.tensor_scalar(out=XC[:, s0:s0 + sn], in0=XP[:, i0:i0 + sn], scalar1=lv(0),
                                                      scalar2=lv(4), op0=ALU.mult, op1=ALU.add),
                     r=["XP", "LV"], w=[("XC", si_)])
                for jx in range(1, 4):
                    P.op("dve", lambda e, jx=jx: e.scalar_tensor_tensor(
                        out=XC[:, s0:s0 + sn], in0=XP[:, i0 + jx:i0 + jx + sn], scalar=lv(jx), in1=XC[:, s0:s0 + sn],
                        op0=ALU.mult, op1=ALU.add), r=["XP", "LV"], w=[("XC", si_)])
                P.op("act", lambda e: e.activation(out=XCB[:, s0:s0 + sn], in_=XC[:, s0:s0 + sn], func=AF.Identity),
                     r=[("XC", si_)], w=[("XCB", si_)])
            cv()

        cnt = [0, 0]
        prev = [None, None]

        def seg_step(d, si_):
            s0, sn = segs[si_]
            k = cnt[d] % 2
            cnt[d] += 1
            R_, I_, T_ = Rb[d][k], Ib[d][k], Tb[d][k]
            rk, ik, tk = ("Rb", d, k), ("Ib", d, k), ("Tb", d, k)
            nk = NK[:, cc * 2 + d: cc * 2 + d + 1]
            for sub in range(0, sn, 512):
                n = min(512, sn - sub)
                for gi, dst, dk in ((0, R_, rk), (1, I_, ik)):
                    def gate(sub=sub, n=n, gi=gi, dst=dst, dk=dk):
                        bank = d * 2 + gi
                        idx = (d * 2 + gi) * 2 + cc
                        hb = HBA[:, (cc * 2 + d) * 2 + gi:(cc * 2 + d) * 2 + gi + 1]
                        P.op("pe", lambda e: e.matmul(C.psb[bank][:, :n], lhsT=Wbf[:, idx, :],
                                                      rhs=XCB[:, s0 + sub:s0 + sub + n], start=True, stop=True),
                             r=["Wbf", ("XCB", si_)], w=[PS(bank)])
                        P.op("act", lambda e: e.activation(out=dst[:, sub:sub + n], in_=C.psb[bank][:, :n],
                                                           func=AF.Tanh, scale=0.5, bias=hb),
                             r=["HBA"], w=[PS(bank), dk])
                    gate()
            P.op("act", lambda e: e.activation(out=R_[:, :sn], in_=R_[:, :sn], func=AF.Exp, scale=nk, bias=nk),
                 r=["NK"], w=[rk])
            P.op("pool", lambda e: e.tensor_tensor(out=T_[:, :sn], in0=R_[:, :sn], in1=R_[:, :sn], op=ALU.mult),
                 r=[rk], w=[tk])
            P.op("dve", lambda e: e.tensor_scalar(out=T_[:, :sn], in0=T_[:, :sn], scalar1=-0.25, scalar2=0.25 + 1e-20,
                                                  op0=ALU.mult, op1=ALU.add), w=[tk])
            return lambda: seg_step_b(d, si_, k)

        def seg_step_b(d, si_, k):
            s0, sn = segs[si_]
            R_, I_, T_ = Rb[d][k], Ib[d][k], Tb[d][k]
            rk, ik, tk = ("Rb", d, k), ("Ib", d, k), ("Tb", d, k)
            P.op("act", lambda e: e.activation(out=T_[:, :sn], in_=T_[:, :sn], func=AF.Sqrt, bias=tiny[:, 0:1]),
                 r=["tiny"], w=[tk])
            P.op("dve", lambda e: e.scalar_tensor_tensor(out=I_[:, :sn], in0=I_[:, :sn], scalar=1.0, in1=T_[:, :sn],
                                                         op0=ALU.add, op1=ALU.mult), r=[tk], w=[ik])
            P.op("dve", lambda e: e.tensor_tensor(out=I_[:, :sn], in0=I_[:, :sn], in1=XC[:, s0:s0 + sn], op=ALU.mult),
                 r=[("XC", si_)], w=[ik])
            if d == 0:
                init = 0.0 if prev[0] is None else HF[:, prev[0][0] + prev[0][1] - 1:prev[0][0] + prev[0][1]]
                rr = [rk, ik] + ([("HF", prev[0][2])] if prev[0] is not None else [])
                P.op("dve", lambda e: e.tensor_tensor_scan(out=HF[:, s0:s0 + sn], data0=R_[:, :sn], data1=I_[:, :sn],
                                                           initial=init, op0=ALU.mult, op1=ALU.add),
                     r=rr, w=[("HF", si_)])
            else:
                init = 0.0 if prev[1] is None else HB[:, prev[1][0]:prev[1][0] + 1]
                rr = [rk, ik] + ([("HB", prev[1][2])] if prev[1] is not None else [])
                P.op("dve", lambda e: e.tensor_tensor_scan(out=rev(HB[:, s0:s0 + sn]), data0=rev(R_[:, :sn]),
                                                           data1=rev(I_[:, :sn]), initial=init,
                                                           op0=ALU.mult, op1=ALU.add),
                     r=rr, w=[("HB", si_)])
            prev[d] = (s0, sn, si_)

        order_f = list(range(9))
        order_b = [0] + list(range(8, 0, -1))
        for q0_ in range(2):
            conv(order_f[q0_])
            conv(order_b[q0_])
        for q in range(9):
            if q + 2 < 9:
                conv(order_f[q + 2])
                conv(order_b[q + 2])
            fb = seg_step(0, order_f[q])
            bb = seg_step(1, order_b[q])
            fb()
            bb()
        for si_ in range(9):
            conv(si_)
        GRb = XP
        P.dma("sp", GRb[:, 0:T], XGv[256 + cc * 128:256 + (cc + 1) * 128, :], w=["XP"], stream="l2")
        for si_, (s0, sn) in enumerate(segs):
            def gl(si_=si_, s0=s0, sn=sn):
                k = si_ % 2
                U = Rb[0][k]
                S_ = Ib[0][k]
                uk, sk = ("Rb", 0, k), ("Ib", 0, k)
                g_ = GRb[:, s0:s0 + sn]
                P.op("act", lambda e: e.activation(out=U[:, :sn], in_=g_, func=AF.Square), r=["XP"], w=[uk])
                P.op("dve", lambda e: e.tensor_scalar(out=U[:, :sn], in0=U[:, :sn], scalar1=0.044715, scalar2=1.0,
                                                      op0=ALU.mult, op1=ALU.add), w=[uk])
                P.op("pool", lambda e: e.tensor_tensor(out=U[:, :sn], in0=U[:, :sn], in1=g_, op=ALU.mult),
                     r=["XP"], w=[uk])
                P.op("act", lambda e: e.activation(out=U[:, :sn], in_=U[:, :sn], func=AF.Tanh, scale=0.7978845608028654),
                     w=[uk])
                P.op("dve", lambda e: e.scalar_tensor_tensor(out=U[:, :sn], in0=U[:, :sn], scalar=1.0, in1=g_,
                                                             op0=ALU.add, op1=ALU.mult), r=["XP"], w=[uk])
                P.op("pool", lambda e: e.tensor_tensor(out=S_[:, :sn], in0=HF[:, s0:s0 + sn], in1=HB[:, s0:s0 + sn],
                                                       op=ALU.add), r=[("HF", si_), ("HB", si_)], w=[sk])
                P.op("dve", lambda e: e.scalar_tensor_tensor(out=XCB[:, s0:s0 + sn], in0=U[:, :sn], scalar=0.5,
                                                             in1=S_[:, :sn], op0=ALU.mult, op1=ALU.mult),
                     r=[uk, sk], w=[("XCB", si_)])
            gl()
        P.dma("sp", CATv[512 + cc * 128:512 + (cc + 1) * 128, :], XCB[:, :], r=[("XCB", q) for q in range(9)],
              w=["XCBst"], stream="l3")

    for cc in range(2):
        per_cc(cc)
    P.release(mk)


def fno_phase(P, C, l, need_ctx):
    P.barrier()
    mk = P.mark()
    cw = P.sb("cw", [128, 128], BF16)
    sw = P.sb("sw", [128, 128], BF16)
    nsw = P.sb("nsw", [128, 128], BF16)
    Wt = P.sb("Wt", [128, 128, 64], BF16)
    t256 = P.sb("t256", [128, 2, 2, 256], BF16)
    P.dma("pool", cw[:], C.cw128, w=["cw"], stream="w0")
    P.dma("pool", sw[:], C.sw128, w=["sw"], stream="w1")
    P.dma("pool", nsw[:], C.nsw128, w=["nsw"], stream="w2")
    P.dma("pool", Wt[:].rearrange("p a b -> p (a b)"), C.wtC, w=["Wt"], stream="w3")
    P.dma("pool", t256[:].rearrange("p a b c -> p (a b c)"), C.t256, w=["t256"], stream="w4")
    Gb = [P.sb(f"Gb{i}", [128, 16, 512], BF16) for i in range(2)]
    Ab = [[P.sb(f"Ab{i}{ri}", [128, 16, 256], BF16) for ri in range(2)] for i in range(2)]
    Glat = C.G[256:T, :].rearrange("(n1 n2) c -> n1 n2 c", n2=64)
    sA = 1.0
    for blk in range(4):
        def stA(blk=blk):
            b2 = blk % 2
            P.dma("sp", Gb[b2][:], Glat[:, blk * 16:(blk + 1) * 16, :], w=[("Gb", b2)], stream=f"g{b2}")
            for pair in range(8):
                def pr(pair=pair):
                    gc = Gb[b2][:, 2 * pair:2 * pair + 2, 0:256]
                    gs_ = Gb[b2][:, 2 * pair:2 * pair + 2, 256:512]
                    bre = (pair % 2) * 2
                    bim = bre + 1
                    P.op("pe", lambda e: e.matmul(C.psb[bre][:, :], lhsT=cw[:, :], rhs=gc, start=True, stop=False),
                         r=["cw", ("Gb", b2)], w=[PS(bre)])
                    P.op("pe", lambda e: e.matmul(C.psb[bre][:, :], lhsT=nsw[:, :], rhs=gs_, start=False, stop=True),
                         r=["nsw", ("Gb", b2)], w=[PS(bre)])
                    P.op("pe", lambda e: e.matmul(C.psb[bim][:, :], lhsT=cw[:, :], rhs=gs_, start=True, stop=False),
                         r=["cw", ("Gb", b2)], w=[PS(bim)])
                    P.op("pe", lambda e: e.matmul(C.psb[bim][:, :], lhsT=sw[:, :], rhs=gc, start=False, stop=True),
                         r=["sw", ("Gb", b2)], w=[PS(bim)])
                    P.op("act", lambda e: e.activation(
                        out=Ab[b2][0][:, 2 * pair:2 * pair + 2, :].rearrange("p a c -> p (a c)"),
                        in_=C.psb[bre][:, :], func=AF.Identity), w=[PS(bre), ("Ab", b2, 0)])
                    P.op("dve", lambda e: e.tensor_copy(
                        out=Ab[b2][1][:, 2 * pair:2 * pair + 2, :].rearrange("p a c -> p (a c)"),
                        in_=C.psb[bim][:, :]), w=[PS(bim), ("Ab", b2, 1)])
                pr()
            for ri in range(2):
                P.dma("sp", C.AB[ri, :, blk * 16:(blk + 1) * 16, :], Ab[b2][ri][:], r=[("Ab", b2, ri)],
                      w=[("AB", blk)], stream=f"a{b2}{ri}")
        stA()
    Ap = [P.sb(f"Ap{i}", [128, 32, 256], BF16) for i in range(2)]
    Yt = P.sb("Yt", [128, 2, 8192], BF16)
    ABv = C.AB.rearrange("ri k1 n2 c -> ri n2 k1 c")
    scl = 1.0 / float(np.sqrt(8192.0 * 64.0))
    for q in range(4):
        def stC(q=q):
            b2 = q % 2
            for ri in range(2):
                P.dma("sp", Ap[b2][ri * 64:(ri + 1) * 64, :, :], ABv[ri, :, q * 32:(q + 1) * 32, :],
                      r=[("AB", bb) for bb in range(4)], w=[("Ap", b2)], stream=f"p{b2}{ri}")
            for cc in range(2):
                for kb in range(4):
                    def grp(cc=cc, kb=kb):
                        bank = 4 + (cc * 4 + kb) % 4
                        pv = C.psb[bank][:, :].rearrange("p (k2 j) -> p k2 j", j=8)
                        for jx in range(8):
                            k1l = kb * 8 + jx
                            k1 = q * 32 + k1l
                            P.op("pe", lambda e, jx=jx, k1l=k1l, k1=k1: e.matmul(
                                pv[:, :, jx], lhsT=Ap[b2][:, k1l, cc * 128:(cc + 1) * 128], rhs=Wt[:, k1, :],
                                start=True, stop=True), r=[("Ap", b2), "Wt"], w=[PS(bank)])
                        k1b = q * 32 + kb * 8
                        yv = Yt[:, cc, :].rearrange("p (k2 k1) -> p k2 k1", k1=128)[:, :, k1b:k1b + 8]
                        if kb % 2 == 0:
                            P.op("act", lambda e: e.activation(out=yv, in_=pv, func=AF.Identity, scale=scl),
                                 w=[PS(bank), "Yt"])
                        else:
                            P.op("dve", lambda e: e.tensor_scalar(out=yv, in0=pv, scalar1=scl, scalar2=None,
                                                                  op0=ALU.mult), w=[PS(bank), "Yt"])
                    grp()
        stC()
    for cc in range(2):
        P.dma("sp", C.CAT[768 + cc * 128:768 + (cc + 1) * 128, 256:T], Yt[:, cc, :], r=["Yt"], stream=f"y{cc}")
    if need_ctx:
        Gc_ = P.sb("Gctx", [128, 2, 512], BF16)
        Ytc = P.sb("Ytc", [128, 2, 256], BF16)
        sclc = 1.0 / float(np.sqrt(256.0 * 64.0))
        P.dma("sp", Gc_[:], C.G[0:256, :].rearrange("(a p) c -> p a c", p=128), w=["Gctx"], stream="g0")
        for cc in range(2):
            def cx(cc=cc):
                bank = cc
                i = 0
                for nchk in range(2):
                    for part in range(2):
                        P.op("pe", lambda e, nchk=nchk, part=part, i=i: e.matmul(
                            C.psb[bank][:, 0:256], lhsT=Gc_[:, nchk, part * 256 + cc * 128: part * 256 + (cc + 1) * 128],
                            rhs=t256[:, nchk, part, :], start=(i == 0), stop=(i == 3)),
                            r=["Gctx", "t256"], w=[PS(bank)])
                        i += 1
                P.op("act", lambda e: e.activation(out=Ytc[:, cc, :], in_=C.psb[bank][:, 0:256], func=AF.Identity,
                                                   scale=sclc), w=[PS(bank), "Ytc"])
            cx()
        P.dma("sp", C.CAT.rearrange("(c p) t -> p c t", p=128)[:, 6:8, 0:256], Ytc[:], r=["Ytc"], stream="y2")
    P.release(mk)


def outproj_phase(P, C, l, tiles):
    j = 1
    P.barrier()
    mk = P.mark()
    wo = P.sb("wo", [128, 8, D], BF16)
    Wv = C.w_out[l].rearrange("(k p) n -> p k n", p=128)
    for k in range(8):
        P.dma("pool", wo[:, k, :], Wv[:, k, :], w=[("wo", k)], stream=f"w{k % 4}")
    xb = [P.sb(f"xb{i}", [128, 8, 512], F32) for i in range(3)]
    cb = [P.sb(f"cb{i}", [128, 8, 512], BF16) for i in range(2)]
    zb = [P.sb(f"zb{i}", [128, 512], BF16) for i in range(2)]
    zq = [P.sb(f"zq{i}", [128, 512], BF16) for i in range(2)]
    msq = [P.sb(f"msq{i}", [128, 512], F32) for i in range(2)]
    srcv = C.XT.rearrange("(m p) t -> p m t", p=128)
    catv = C.CAT.rearrange("(m p) t -> p m t", p=128)
    nt = len(tiles)

    def load(ti):
        t0, n, s = tiles[ti]
        P.dma("sp", xb[ti % 3][:, :, :n], srcv[:, :, t0:t0 + n], w=[(f"xb{ti % 3}", m) for m in range(8)],
              stream=f"xl{ti % 3}")
        P.dma("sp", cb[ti % 2][:, :, :n], catv[:, :, t0:t0 + n], w=[("cb", ti % 2)], stream=f"cl{ti % 2}")

    def resid_piece(ti, m):
        t0, n, s = tiles[ti]
        X = xb[ti % 3]
        xk = f"xb{ti % 3}"
        cbt = cb[ti % 2]
        bm, be = (6, 7) if ti % 2 == 0 else (2, 3)
        py = 4 + m % 2
        s1p, sh, gs = mod_aps(C, l, j, m, s)

        def stats(mm):
            P.op("pe", lambda e: e.matmul(C.psb[bm][:, :n], lhsT=C.onesb[:, :], rhs=zb[mm % 2][:, :n],
                                          start=(mm == 0), stop=(mm == 7)), r=["onesb", ("zb", mm % 2)], w=[PS(bm)])
            P.op("pe", lambda e: e.matmul(C.psb[be][:, :n], lhsT=C.onesb[:, :], rhs=zq[mm % 2][:, :n],
                                          start=(mm == 0), stop=(mm == 7)), r=["onesb", ("zq", mm % 2)], w=[PS(be)])
        for k in range(8):
            P.op("pe", lambda e, k=k: e.matmul(C.psb[py][:, :n], lhsT=wo[:, k, m * 128:(m + 1) * 128],
                                               rhs=cbt[:, k, :n], start=(k == 0), stop=(k == 7)),
                 r=[("wo", k), ("cb", ti % 2)], w=[PS(py)])
        P.op("dve", lambda e: e.scalar_tensor_tensor(
            out=X[:, m, :n], in0=C.psb[py][:, :n], scalar=gs, in1=X[:, m, :n], op0=ALU.mult, op1=ALU.add),
            r=["GS"], w=[PS(py), (xk, m)])
        P.op("act", lambda e: e.activation(out=zb[m % 2][:, :n], in_=X[:, m, :n], func=AF.Identity),
             r=[(xk, m)], w=[("zb", m % 2)])
        P.op("act", lambda e: e.activation(out=zq[m % 2][:, :n], in_=X[:, m, :n], func=AF.Square),
             r=[(xk, m)], w=[("zq", m % 2)])
        if m > 0:
            stats(m - 1)
        if m == 7:
            stats(7)

    def finish(ti, inter):
        t0, n, s = tiles[ti]
        X = xb[ti % 3]
        xk = f"xb{ti % 3}"
        bm, be = (6, 7) if ti % 2 == 0 else (2, 3)
        ln_tail(P, C, l, j, X, xk, n, bm, be, msq[ti % 2], f"o{ti % 2}", inter, every=True)
        P.dma("sp", srcv[:, :, t0:t0 + n], X[:, :, :n], r=[(xk, m) for m in range(8)], stream=f"xs{ti % 3}")

    load(0)
    if nt > 1:
        load(1)
    for m in range(8):
        resid_piece(0, m)
    for ti in range(nt):
        if ti + 2 < nt:
            load(ti + 2)
        inter = []
        if ti + 1 < nt:
            inter = [(lambda m=m: resid_piece(ti + 1, m)) for m in range(8)]
        finish(ti, inter)
    P.release(mk)


def build(n_layers=DEPTH, stop_after=None, dbg=False, tiles=None, only=None):
    nc = bass.Bass("TRN2", target_bir_lowering=False)
    C = Ctx()
    P = Prog(nc)
    declare(nc, C, n_layers, dbg)
    prologue(P, C, n_layers)
    tl = tiles if tiles is not None else tiles_all()
    XTv = C.XT.rearrange("(m p) t -> p m t", p=128)

    def to_xt(tiles):
        return lambda ti: XTv[:, :, tiles[ti][0]:tiles[ti][0] + tiles[ti][1]]

    stages = ["ffn1", "inproj", "attn", "lru", "fno", "mix", "ffn2"]
    outv = C.out.rearrange("(m p) t -> p m t", p=128)

    def run_layers():
        for l in range(n_layers):
            last = (l == DEPTH - 1)
            tl2 = tl if not last else tl[1:]

            def dst_last(ti, tl2=tl2):
                t0, n, s_ = tl2[ti]
                return outv[:, :, t0 - NCTX:t0 - NCTX + n]
            seq = [
                ("ffn1", lambda: ffn_phase(P, C, l, 0, C.xin if l == 0 else C.XT, to_xt(tl), tl)),
                ("inproj", lambda: inproj_phase(P, C, l, tl, n_layers)),
                ("attn", lambda: attn_phase(P, C, l, not last)),
                ("lru", lambda: lru_phase(P, C, l)),
                ("fno", lambda: fno_phase(P, C, l, not last)),
                ("mix", lambda: outproj_phase(P, C, l, tl2)),
                ("ffn2", lambda: ffn_phase(P, C, l, 1, C.XT, dst_last if last else to_xt(tl2), tl2)),
            ]
            if l == 0 and only is not None and "ffn1" not in only:
                P.dma("sp", C.XT, C.xin, stream="cp")
            for name, fn in seq:
                if only is None or name in only:
                    fn()
                if stop_after == (l, name):
                    return
    run_layers()
    if dbg:
        P.barrier()
        P.dma("sp", C.dbg, C.XT, stream="dbg")
        P.dma("sp", C.dbg2, C.M[:], stream="dbg2")
    P.emit()
    C.P = P
    return nc, C


def host_inputs(inp, b):
    f = np.float32
    x, ctx = inp["x"], inp["ctx"]
    xin = np.ascontiguousarray(np.concatenate([ctx[b].T, x[b].T], axis=1), dtype=f)
    cv = np.stack([np.asarray(inp["c"][b]), np.asarray(inp["c_ctx"])], axis=-1)
    cvec = np.ascontiguousarray(cv.reshape(8, 128, 2).transpose(1, 0, 2).reshape(128, 16), dtype=f)
    return {"xin": xin, "cvec": cvec}


def host_shared(inp):
    f = np.float32
    ba = np.asarray(inp["b_ada"]).reshape(DEPTH, 72, 128).transpose(2, 0, 1)
    bada = np.ascontiguousarray(np.repeat(ba[:, :, :, None], 2, axis=3).reshape(128, DEPTH * 144), dtype=f)
    lng = np.ascontiguousarray(np.asarray(inp["ln_g"]).reshape(DEPTH * 3 * 8, 128).T, dtype=f)
    lnb = np.ascontiguousarray(np.asarray(inp["ln_b"]).reshape(DEPTH * 3 * 8, 128).T, dtype=f)
    sh = {"bada": bada, "lng": lng, "lnb": lnb, "ident": np.eye(128, dtype=f)}
    for k in ("w_ada", "ff1_gate", "ff1_up", "ff1_down", "ff2_gate", "ff2_up", "ff2_down", "w_in", "w_out"):
        sh[k] = np.ascontiguousarray(inp[k], dtype=f)
    sh.update(host_consts(inp))
    return sh


def host_consts(inp):
    f = np.float32
    k64 = np.arange(64)
    a64 = 2 * np.pi * ((np.outer(k64, k64)) % 64) / 64.0
    C64, S64 = np.cos(a64), np.sin(a64)
    z = np.zeros((64, 64))
    c64bd = np.block([[C64, z], [z, C64]])
    s64bd = np.block([[S64, z], [z, S64]])
    n1 = np.arange(128)
    a128 = 2 * np.pi * ((np.outer(n1, n1)) % 128) / 128.0
    n2 = np.arange(64)
    kk = n1[:, None] + 128 * k64[None, :]
    ang = 2 * np.pi * ((n2[:, None, None] * kk[None]) % 8192) / 8192.0
    wtC = np.concatenate([np.cos(ang), -np.sin(ang)], axis=0).reshape(128, 128 * 64)
    n256 = np.arange(256)
    a256 = 2 * np.pi * ((np.outer(n256, n256)) % 256) / 256.0
    t256 = np.stack([np.cos(a256), -np.sin(a256)], axis=1)
    t256 = t256.reshape(2, 128, 2, 256).transpose(1, 0, 2, 3).reshape(128, 2 * 2 * 256)
    p = np.arange(128)
    krl, kc = p // 64, p % 64
    e = np.arange(30)
    qc = np.arange(64)
    d = krl[:, None] - (e[None, :] - 14)
    dr = d + 7
    dc = kc[:, None] - qc[None, :] + 15
    col0 = np.clip(qc - 8, 0, 48)
    colv = (kc[:, None] >= col0[None, :]) & (kc[:, None] < col0[None, :] + 16)
    drv = (dr >= 0) & (dr <= 14)
    rpb = np.asarray(inp["na_rpb"])
    dri = np.clip(dr, 0, 14)
    dci = np.clip(dc, 0, 30)
    gat = rpb[:, :, dri[:, :, None], dci[:, None, :]]
    okf = drv[:, :, None] & colv[:, None, :]
    rpbT = np.where(okf[None, None], gat, 0.0).reshape(DEPTH, 8, 128, 1920)
    oki = okf & ((d >= -4) & (d <= 3))[:, :, None]
    cmask = np.stack([np.where(oki, 0.0, NEG), np.where(okf, 0.0, NEG)], 0).reshape(2, 128, 1920)
    rm = np.zeros((2, 128, 8, 8, 64))
    for ri, (kr0, qr0) in enumerate(((0, 0), (112, 120))):
        for ai in range(8):
            for i in range(8):
                qr = qr0 + i
                rs = min(max(qr - 4, 0), 120)
                kr = kr0 + 2 * ai + krl
                ok = (kr >= rs) & (kr < rs + 8)
                rm[ri, :, ai, i, :] = np.where(ok, 0.0, NEG)[:, None]
    rmask = rm.reshape(2, 128, 8 * 512)
    L = DEPTH
    lv = np.zeros((L, 256, 11))
    lv[:, :, 0:4] = np.asarray(inp["lru_conv_w"]).transpose(0, 2, 1)
    lv[:, :, 4] = np.asarray(inp["lru_conv_b"])
    for dd in range(2):
        lv[:, :, 5 + 3 * dd] = np.asarray(inp["lru_ba"])[:, dd]
        lv[:, :, 6 + 3 * dd] = np.asarray(inp["lru_bx"])[:, dd]
        lv[:, :, 7 + 3 * dd] = np.asarray(inp["lru_lambda"])[:, dd]
    lruv = lv.reshape(L, 2, 128, 11).transpose(2, 0, 1, 3).reshape(128, L * 2 * 11)
    out = {"c64bd": c64bd, "s64bd": s64bd, "cw128": np.cos(a128), "sw128": np.sin(a128), "nsw128": -np.sin(a128),
           "wtC": wtC, "t256": t256, "rpbT": rpbT, "cmask": cmask, "rmask": rmask, "lruv": lruv}
    out = {k: np.ascontiguousarray(v, dtype=f) for k, v in out.items()}
    for k in ("lru_wa", "lru_wx", "fno_w"):
        out[k] = np.ascontiguousarray(inp[k], dtype=f)
    return out


_CACHE = {}


def kernel(**inputs):
    if "nc" not in _CACHE:
        _CACHE["nc"] = build()[0]
    nc = _CACHE["nc"]
    sh = host_shared(inputs)
    in_maps = []
    for b in range(8):
        d = dict(sh)
        d.update(host_inputs(inputs, b))
        in_maps.append(d)
    res = run_bass_kernel_spmd(nc, in_maps, core_ids=list(range(8)))
    out = np.stack([np.ascontiguousarray(r["out"].T) for r in res.results], axis=0)
    return out.astype(np.float32)
```

```python
import contextlib
import numpy as np
import concourse.bass as bass
import concourse.mybir as mybir
from concourse.bass_utils import run_bass_kernel_spmd

F32 = mybir.dt.float32
BF16 = mybir.dt.bfloat16
I32 = mybir.dt.int32
AF = mybir.ActivationFunctionType
ALU = mybir.AluOpType

D = 1024
DEPTH = 4
NCTX = 256
NLAT = 8192
T = NCTX + NLAT
DFF = 2816
NJ = DFF // 128
ALPHA = (2 * DEPTH) ** 0.25
EPS_P = 1e-5 / (ALPHA * ALPHA)
NEG = -30000.0

COMPUTE = ("pe", "act", "dve", "pool")


class Prog:
    def __init__(self, nc):
        self.nc = nc
        self.ops = []
        self.barriers = set()
        self.sb_base = 16512
        self.sb_top = 229344
        self.sb_off = self.sb_base
        self.sb_max = 0
        self._n = 0

    def sb(self, name, shape, dtype):
        sz = int(np.prod(shape[1:])) * mybir.dt.size(dtype)
        off = (self.sb_off + 63) // 64 * 64
        self._n += 1
        assert off + sz <= self.sb_top, (name, off + sz, self.sb_top)
        t = self.nc.alloc_sbuf_tensor_at(f"{name}_{self._n}", list(shape), dtype, offset=off)
        self.sb_off = off + sz
        self.sb_max = max(self.sb_max, self.sb_off)
        return t

    def mark(self):
        return self.sb_off

    def release(self, m):
        self.sb_off = m

    def op(self, eng, fn, r=(), w=(), stream=None):
        self.ops.append([eng, fn, tuple(r), tuple(w), stream])

    def dma(self, eng, out, in_, r=(), w=(), stream=None, **kw):
        assert stream is not None
        self.op(eng, lambda e: e.dma_start(out=out, in_=in_, **kw), r, w, stream)

    def barrier(self):
        self.barriers.add(len(self.ops))

    def emit(self):
        nc = self.nc
        ops = self.ops
        n = len(ops)
        last_w, readers = {}, {}
        deps = [None] * n
        eng_idx = [0] * n
        eng_count, last_on_eng, last_on_stream, pending = {}, {}, {}, {}
        for i, (eng, fn, r, w, stream) in enumerate(ops):
            if i in self.barriers:
                bd = set(last_on_eng.values()) | set(last_on_stream.values())
                for e in COMPUTE + ("sp",):
                    pending.setdefault(e, set()).update(bd)
            d = set()
            if eng in pending:
                d |= pending.pop(eng)
            for k in r:
                if k in last_w:
                    d.add(last_w[k])
            for k in w:
                if k in last_w:
                    d.add(last_w[k])
                d.update(readers.get(k, ()))
            if stream is not None and stream in last_on_stream:
                d.add(last_on_stream[stream])
            d.discard(i)
            best = {}
            d2 = set()
            for jx in d:
                if ops[jx][4] is None:
                    ej = ops[jx][0]
                    if ej not in best or jx > best[ej]:
                        best[ej] = jx
                else:
                    d2.add(jx)
            d2.update(best.values())
            deps[i] = d2
            for k in r:
                readers.setdefault(k, []).append(i)
            for k in w:
                last_w[k] = i
                readers[k] = []
            eng_idx[i] = eng_count.get(eng, 0)
            eng_count[eng] = eng_idx[i] + 1
            if stream is None:
                last_on_eng[eng] = i
            else:
                last_on_stream[stream] = i
        need_sig = [False] * n
        for i in range(n):
            eng = ops[i][0]
            keep = set()
            for j in deps[i]:
                ej, sj = ops[j][0], ops[j][4]
                if sj is None and ej == eng:
                    if eng == "pe":
                        continue
                    if ops[i][4] is None and eng_idx[i] - eng_idx[j] >= 3:
                        continue
                keep.add(j)
                need_sig[j] = True
            deps[i] = keep
        sigval = [0] * n
        cnt = {}
        for i in range(n):
            eng, _, _, _, stream = ops[i]
            if stream is not None:
                key = ("dma", stream)
                cnt[key] = cnt.get(key, 0) + 16
                sigval[i] = cnt[key]
            elif need_sig[i]:
                key = ("eng", eng)
                cnt[key] = cnt.get(key, 0) + 1
                sigval[i] = cnt[key]
        waits = [None] * n
        seen = {}
        for i in range(n):
            eng = ops[i][0]
            need = {}
            for j in deps[i]:
                ej, sj = ops[j][0], ops[j][4]
                key = ("dma", sj) if sj is not None else ("eng", ej)
                need[key] = max(need.get(key, 0), sigval[j])
            wl = []
            for key, v in need.items():
                if seen.get((eng, key), 0) >= v:
                    continue
                seen[(eng, key)] = v
                wl.append((key, v))
            waits[i] = wl
        keys = list(cnt.keys())
        self.n_sems = len(keys)
        self.cnt = cnt
        self.plan = (deps, sigval, need_sig, waits)
        with contextlib.ExitStack() as es:
            sems = {}
            for k in keys:
                sems[k] = es.enter_context(nc.semaphore(f"s{len(sems)}"))
            block = es.enter_context(nc.Block())
            per_eng = {}
            for i, o in enumerate(ops):
                per_eng.setdefault(o[0], []).append(i)

            def run(eng_name, e):
                for i in per_eng.get(eng_name, []):
                    _, fn, _, _, stream = ops[i]
                    for key, v in waits[i]:
                        e.wait_ge(sems[key], v)
                    ins = fn(e)
                    if stream is not None:
                        ins.then_inc(sems[("dma", stream)], 16)
                    elif need_sig[i]:
                        ins.then_inc(sems[("eng", eng_name)], 1)
                if eng_name == "sp":
                    for k in keys:
                        e.wait_ge(sems[k], cnt[k])

            @block.sync
            def _(e):
                run("sp", e)

            @block.tensor
            def _(e):
                run("pe", e)

            @block.scalar
            def _(e):
                run("act", e)

            @block.vector
            def _(e):
                run("dve", e)

            @block.gpsimd
            def _(e):
                run("pool", e)
        return nc


class Ctx:
    pass


def PS(i):
    return ("ps", i)


def declare(nc, C, n_layers, dbg):
    def inp(name, shape, dt=F32):
        return nc.dram_tensor(name, list(shape), dt, kind="ExternalInput").ap()

    C.xin = inp("xin", [D, T])
    C.cvec = inp("cvec", [128, 16])
    C.bada = inp("bada", [128, DEPTH * 144])
    C.lng = inp("lng", [128, 96])
    C.lnb = inp("lnb", [128, 96])
    C.ident = inp("ident", [128, 128])
    C.w_ada = inp("w_ada", [DEPTH, D, 9 * D])
    C.ffg = [inp("ff1_gate", [DEPTH, D, DFF]), inp("ff2_gate", [DEPTH, D, DFF])]
    C.ffu = [inp("ff1_up", [DEPTH, D, DFF]), inp("ff2_up", [DEPTH, D, DFF])]
    C.ffd = [inp("ff1_down", [DEPTH, DFF, D]), inp("ff2_down", [DEPTH, DFF, D])]
    C.w_in = inp("w_in", [DEPTH, D, 2304])
    C.w_out = inp("w_out", [DEPTH, D, D])
    C.lruv = inp("lruv", [128, DEPTH * 2 * 11])
    C.lru_wa = inp("lru_wa", [DEPTH, 2, 4, 64, 64])
    C.lru_wx = inp("lru_wx", [DEPTH, 2, 4, 64, 64])
    C.fno_w = inp("fno_w", [DEPTH, 4, 64, 64])
    C.c64bd = inp("c64bd", [128, 128])
    C.s64bd = inp("s64bd", [128, 128])
    C.cw128 = inp("cw128", [128, 128])
    C.sw128 = inp("sw128", [128, 128])
    C.nsw128 = inp("nsw128", [128, 128])
    C.wtC = inp("wtC", [128, 128 * 64])
    C.t256 = inp("t256", [128, 2 * 2 * 256])
    C.rpbT = inp("rpbT", [DEPTH, 8, 128, 1920])
    C.cmask = inp("cmask", [2, 128, 1920])
    C.rmask = inp("rmask", [2, 128, 8 * 512])
    C.out = nc.dram_tensor("out", [D, NLAT], F32, kind="ExternalOutput").ap()
    sk = "ExternalOutput" if dbg else "Internal"
    C.QK = nc.dram_tensor("QK", [1024, T], BF16, kind=sk).ap()
    C.XG = nc.dram_tensor("XG", [512, T], F32, kind=sk).ap()
    C.VA = nc.dram_tensor("VA", [T, 1024], BF16, kind=sk).ap()
    C.G = nc.dram_tensor("G", [T, 512], BF16, kind=sk).ap()
    C.AB = nc.dram_tensor("AB", [2, 128, 64, 256], BF16, kind=sk).ap()
    C.CAT = nc.dram_tensor("CAT", [1024, T], BF16, kind=sk).ap()
    C.XT = nc.dram_tensor("XT", [D, T], F32, kind="Internal").ap()
    if dbg:
        C.dbg = nc.dram_tensor("dbg", [D, T], F32, kind="ExternalOutput").ap()
        C.dbg2 = nc.dram_tensor("dbg2", [128, DEPTH * 144], F32, kind="ExternalOutput").ap()
    C.psb = [nc.alloc_psum_tensor(f"psb{i}", [128, 512], F32) for i in range(8)]


def tiles_all():
    tl = [(0, NCTX, 1)]
    for t in range(NLAT // 512):
        tl.append((NCTX + 512 * t, 512, 0))
    return tl


def prologue(P, C, n_layers):
    nc = P.nc
    C.identb = P.sb("identb", [128, 128], BF16)
    C.onesb = P.sb("onesb", [128, 128], BF16)
    C.M = P.sb("M", [128, DEPTH * 144], F32)
    C.S1P = P.sb("S1P", [128, DEPTH * 48], F32)
    C.GS = P.sb("GS", [128, DEPTH * 48], F32)
    C.LNG = P.sb("LNG", [128, 96], F32)
    C.LNB = P.sb("LNB", [128, 96], F32)
    C.epsc = P.sb("epsc", [128, 1], F32)
    C.zc = P.sb("zc", [128, 1], F32)
    P.op("dve", lambda e: e.memset(C.zc[:], 0.0), w=["zc"])
    P.dma("pool", C.identb[:], C.ident, w=["identb"], stream="c0")
    P.dma("sp", C.LNG[:], C.lng, w=["LNG"], stream="c1")
    P.dma("sp", C.LNB[:], C.lnb, w=["LNB"], stream="c2")
    P.op("dve", lambda e: e.memset(C.onesb[:], 1.0 / D), w=["onesb"])
    P.op("dve", lambda e: e.memset(C.epsc[:], EPS_P), w=["epsc"])
    C.scsb = P.sb("scsb", [128, 16], BF16)
    mk = P.mark()
    cs = P.sb("cs", [128, 16], F32)
    P.dma("sp", cs[:], C.cvec, w=["cs"], stream="c3")
    P.op("act", lambda e: e.activation(out=C.scsb[:], in_=cs[:], func=AF.Silu), r=["cs"], w=["scsb"])
    bufs = adaln_alloc(P)
    for j9 in range(10):
        adaln_step(P, C, 0, bufs, j9, 7)
    P.barrier()
    P.release(mk)


def adaln_alloc(P):
    wa = [P.sb(f"wa{i}", [128, 8, D], BF16) for i in range(2)]
    bad = P.sb("bad", [128, 144], F32)
    return wa, bad


def adaln_step(P, C, l, bufs, j9, bank):
    wa, bad = bufs
    ps = C.psb[bank]
    if j9 < 9:
        buf = wa[j9 % 2]
        bk = ("wa", j9 % 2)
        if j9 == 0:
            P.dma("sp", bad[:], C.bada[:, l * 144:(l + 1) * 144], w=["bad"], stream="c4")
        src = C.w_ada[l, :, j9 * D:(j9 + 1) * D].rearrange("(k p) n -> p k n", p=128)
        P.dma("pool", buf[:], src, w=[bk], stream=f"wa{j9 % 2}")
        for mo in range(8):
            col = (j9 * 8 + mo) * 2
            for k in range(8):
                P.op("pe", lambda e, k=k, mo=mo, col=col: e.matmul(
                    ps[:, col:col + 2], lhsT=buf[:, k, mo * 128:(mo + 1) * 128], rhs=C.scsb[:, 2 * k:2 * k + 2],
                    start=(k == 0), stop=(k == 7)), r=[bk, "scsb"], w=[PS(bank)])
        return
    P.op("dve", lambda e: e.tensor_tensor(out=C.M[:, l * 144:(l + 1) * 144], in0=ps[:, 0:144], in1=bad[:, :],
                                          op=ALU.add), r=["bad"], w=[PS(bank), "M"])
    for j in range(3):
        gc = (0.5 if j != 1 else 1.0) / ALPHA
        P.op("dve", lambda e, j=j: e.tensor_scalar(
            out=C.S1P[:, (l * 3 + j) * 16:(l * 3 + j + 1) * 16],
            in0=C.M[:, l * 144 + (3 * j + 1) * 16: l * 144 + (3 * j + 2) * 16],
            scalar1=1.0, scalar2=None, op0=ALU.add), r=["M"], w=["S1P"])
        P.op("dve", lambda e, j=j, gc=gc: e.tensor_scalar(
            out=C.GS[:, (l * 3 + j) * 16:(l * 3 + j + 1) * 16],
            in0=C.M[:, l * 144 + (3 * j + 2) * 16: l * 144 + (3 * j + 3) * 16],
            scalar1=gc, scalar2=None, op0=ALU.mult), r=["M"], w=["GS"])


def mod_aps(C, l, j, m, s):
    i = ((l * 3 + j) * 8 + m) * 2 + s
    sh = l * 144 + ((3 * j) * 8 + m) * 2 + s
    return C.S1P[:, i:i + 1], C.M[:, sh:sh + 1], C.GS[:, i:i + 1]


def ln_tail(P, C, l, j, xb, xkey, n, ps_mean, ps_ex2, st, pfx, inter=None, every=False):
    msq = st
    kmean, kex2 = PS(ps_mean), PS(ps_ex2)
    pm, pe2 = C.psb[ps_mean], C.psb[ps_ex2]
    inter = list(inter) if inter else []

    def nxt():
        if inter:
            inter.pop(0)()

    nxt()
    P.op("act", lambda e: e.activation(out=msq[:, :n], in_=pm[:, :n], func=AF.Square), w=[kmean, pfx + "msq"])
    P.op("dve", lambda e: e.tensor_tensor(out=msq[:, :n], in0=pe2[:, :n], in1=msq[:, :n], op=ALU.subtract),
         w=[kex2, pfx + "msq"])
    P.op("act", lambda e: e.activation(out=msq[:, :n], in_=msq[:, :n], func=AF.Sqrt, bias=C.epsc[:, 0:1]),
         r=["epsc"], w=[pfx + "msq"])
    P.op("dve", lambda e: e.reciprocal(out=msq[:, :n], in_=msq[:, :n]), w=[pfx + "msq"])
    nxt()
    for m in range(8):
        g = C.LNG[:, (l * 3 + j) * 8 + m:(l * 3 + j) * 8 + m + 1]
        b = C.LNB[:, (l * 3 + j) * 8 + m:(l * 3 + j) * 8 + m + 1]
        P.op("dve", (lambda m=m: lambda e: e.tensor_tensor(out=xb[:, m, :n], in0=xb[:, m, :n], in1=pm[:, :n],
                                                           op=ALU.subtract))(), w=[kmean, (xkey, m)])
        P.op("dve", (lambda m=m: lambda e: e.tensor_tensor(out=xb[:, m, :n], in0=xb[:, m, :n], in1=msq[:, :n],
                                                           op=ALU.mult))(), r=[pfx + "msq"], w=[(xkey, m)])
        P.op("act", (lambda m=m, g=g, b=b: lambda e: e.activation(out=xb[:, m, :n], in_=xb[:, m, :n],
                                                                  func=AF.Identity, scale=g, bias=b))(),
             r=["LNG", "LNB"], w=[(xkey, m)])
        if every or m % 2 == 1:
            nxt()
    while inter:
        nxt()


def ffn_phase(P, C, l, which, src, dst_fn, tiles):
    j = 0 if which == 0 else 2
    P.barrier()
    mk = P.mark()
    wg = P.sb("wg", [128, 8, DFF], BF16)
    wu = P.sb("wu", [128, 8, DFF], BF16)
    wd = P.sb("wd", [128, NJ, D], BF16)
    Wg = C.ffg[which][l].rearrange("(k p) n -> p k n", p=128)
    Wu = C.ffu[which][l].rearrange("(k p) n -> p k n", p=128)
    Wd = C.ffd[which][l].rearrange("(k p) n -> p k n", p=128)
    for k in range(8):
        P.dma("pool", wg[:, k, :], Wg[:, k, :], w=[("wg", k)], stream=f"w{k % 4}")
        P.dma("pool", wu[:, k, :], Wu[:, k, :], w=[("wu", k)], stream=f"w{4 + k % 4}")
    for k in range(0, NJ, 2):
        P.dma("pool", wd[:, k:k + 2, :], Wd[:, k:k + 2, :], w=[("wd", k), ("wd", k + 1)], stream=f"w{8 + (k // 2) % 4}")
    xb = [P.sb(f"xb{i}", [128, 8, 512], F32) for i in range(2)]
    h = P.sb("h", [128, 8, 512], BF16)
    a = P.sb("a", [128, NJ, 512], BF16)
    sg = [P.sb(f"sg{i}", [128, 512], BF16) for i in range(2)]
    zb = [P.sb(f"zb{i}", [128, 512], BF16) for i in range(2)]
    zq = [P.sb(f"zq{i}", [128, 512], BF16) for i in range(2)]
    st = P.sb("msq", [128, 512], F32)
    srcv = src.rearrange("(m p) t -> p m t", p=128)

    def load(ti):
        t0, n, s = tiles[ti]
        P.dma("sp", xb[ti % 2][:, :, :n], srcv[:, :, t0:t0 + n], w=[(f"xb{ti % 2}", m) for m in range(8)],
              stream=f"xl{ti % 2}")

    def modulate(ti):
        t0, n, s = tiles[ti]
        X = xb[ti % 2]
        xk = f"xb{ti % 2}"
        for m in range(8):
            s1p, sh, gs = mod_aps(C, l, j, m, s)
            P.op("act", lambda e, m=m, s1p=s1p, sh=sh: e.activation(
                out=h[:, m, :n], in_=X[:, m, :n], func=AF.Identity, scale=s1p, bias=sh),
                r=[(xk, m), "S1P", "M"], w=[("h", m)])

    KPRE = 9

    def gate_up(ti, jlo, jhi):
        t0, n, s = tiles[ti]
        for jj in range(jlo, jhi):
            pg, pu = jj % 2, 2 + jj % 2
            for k in range(8):
                P.op("pe", lambda e, k=k, jj=jj, pg=pg: e.matmul(
                    C.psb[pg][:, :n], lhsT=wg[:, k, jj * 128:(jj + 1) * 128], rhs=h[:, k, :n],
                    start=(k == 0), stop=(k == 7)), r=[("wg", k), ("h", k)], w=[PS(pg)])
            for k in range(8):
                P.op("pe", lambda e, k=k, jj=jj, pu=pu: e.matmul(
                    C.psb[pu][:, :n], lhsT=wu[:, k, jj * 128:(jj + 1) * 128], rhs=h[:, k, :n],
                    start=(k == 0), stop=(k == 7)), r=[("wu", k), ("h", k)], w=[PS(pu)])
            P.op("act", lambda e, jj=jj, pg=pg: e.activation(
                out=sg[jj % 2][:, :n], in_=C.psb[pg][:, :n], func=AF.Silu), w=[PS(pg), ("sg", jj % 2)])
            P.op("dve", lambda e, jj=jj, pu=pu: e.tensor_tensor(
                out=a[:, jj, :n], in0=C.psb[pu][:, :n], in1=sg[jj % 2][:, :n], op=ALU.mult),
                r=[("sg", jj % 2)], w=[PS(pu), ("a", jj)])

    def tile_body(ti, t0, n, s):
        X = xb[ti % 2]
        xk = f"xb{ti % 2}"
        gate_up(ti, 0 if ti == 0 else KPRE, NJ)
        if ti + 1 < len(tiles):
            modulate(ti + 1)

        def y_mms(m, py):
            for jj in range(NJ):
                P.op("pe", lambda e, jj=jj: e.matmul(
                    C.psb[py][:, :n], lhsT=wd[:, jj, m * 128:(m + 1) * 128], rhs=a[:, jj, :n],
                    start=(jj == 0), stop=(jj == NJ - 1)), r=[("wd", jj), ("a", jj)], w=[PS(py)])

        inter = []
        if ti + 1 < len(tiles):
            inter = [(lambda q=q: gate_up(ti + 1, q, q + 1)) for q in range(KPRE)]
        resid_ln_part(P, C, l, j, X, xk, n, s, zb, zq, st, y_mms, inter)
        dstv = dst_fn(ti)
        if dstv is not None:
            P.dma("sp", dstv, X[:, :, :n], r=[(xk, m) for m in range(8)], stream=f"xs{ti % 2}")

    load(0)
    modulate(0)
    for ti, (t0, n, s) in enumerate(tiles):
        if ti + 1 < len(tiles):
            load(ti + 1)
        tile_body(ti, t0, n, s)
    P.release(mk)


def resid_ln_part(P, C, l, j, X, xk, n, s, zb, zq, msq, y_mms, before_ln=None):
    def stats(m):
        P.op("pe", lambda e: e.matmul(C.psb[6][:, :n], lhsT=C.onesb[:, :], rhs=zb[m % 2][:, :n],
                                      start=(m == 0), stop=(m == 7)), r=["onesb", ("zb", m % 2)], w=[PS(6)])
        P.op("pe", lambda e: e.matmul(C.psb[7][:, :n], lhsT=C.onesb[:, :], rhs=zq[m % 2][:, :n],
                                      start=(m == 0), stop=(m == 7)), r=["onesb", ("zq", m % 2)], w=[PS(7)])

    for m in range(8):
        py = 4 + m % 2
        s1p, sh, gs = mod_aps(C, l, j, m, s)
        y_mms(m, py)

        def ep(m=m, py=py, gs=gs):
            P.op("dve", lambda e: e.scalar_tensor_tensor(
                out=X[:, m, :n], in0=C.psb[py][:, :n], scalar=gs, in1=X[:, m, :n], op0=ALU.mult, op1=ALU.add),
                r=["GS"], w=[PS(py), (xk, m)])
            P.op("act", lambda e: e.activation(out=zb[m % 2][:, :n], in_=X[:, m, :n], func=AF.Identity),
                 r=[(xk, m)], w=[("zb", m % 2)])
            P.op("act", lambda e: e.activation(out=zq[m % 2][:, :n], in_=X[:, m, :n], func=AF.Square),
                 r=[(xk, m)], w=[("zq", m % 2)])
        ep()
        if m > 0:
            stats(m - 1)
    before_ln = list(before_ln) if before_ln else []
    if before_ln:
        before_ln.pop(0)()
    stats(7)
    ln_tail(P, C, l, j, X, xk, n, 6, 7, msq, "f", before_ln, every=True)


def inproj_phase(P, C, l, tiles, n_layers=DEPTH):
    j = 1
    P.barrier()
    mk = P.mark()
    win = P.sb("win", [128, 8, 2304], BF16)
    Wv = C.w_in[l].rearrange("(k p) n -> p k n", p=128)
    for k in range(8):
        P.dma("pool", win[:, k, :], Wv[:, k, :], w=[("win", k)], stream=f"w{k % 4}")
    fw = P.sb("fw", [128, 2, 64], F32)
    cbd = P.sb("cbd", [128, 128], F32)
    sbd = P.sb("sbd", [128, 128], F32)
    BD = P.sb("BD", [128, 2, 512], BF16)
    P.dma("sp", fw[:], C.fno_w[l].rearrange("(kc g2) c j -> (g2 c) kc j", g2=2), w=["fw"], stream="m0")
    P.dma("sp", cbd[:], C.c64bd, w=["cbd"], stream="m1")
    P.dma("sp", sbd[:], C.s64bd, w=["sbd"], stream="m2")
    P.op("pool", lambda e: e.memset(BD[:], 0.0), w=["BD"])
    for kc in range(2):
        for ti_, tab in enumerate((cbd, sbd)):
            def bdm(kc=kc, ti_=ti_, tab=tab):
                tk = "cbd" if ti_ == 0 else "sbd"
                P.op("pe", lambda e: e.matmul(C.psb[0][:, 0:64], lhsT=tab[:, :], rhs=fw[:, kc, :], start=True, stop=True),
                     r=[tk, "fw"], w=[PS(0)])
                for g2 in range(2):
                    c0 = ti_ * 256 + (2 * kc + g2) * 64
                    P.op("dve", lambda e, g2=g2, c0=c0: e.tensor_copy(
                        out=BD[g2 * 64:(g2 + 1) * 64, kc, c0:c0 + 64], in_=C.psb[0][g2 * 64:(g2 + 1) * 64, 0:64]),
                        w=[PS(0), "BD"])
            bdm()
    xb = [P.sb(f"xb{i}", [128, 8, 512], F32) for i in range(2)]
    h = P.sb("h", [128, 8, 512], BF16)
    qk = [P.sb(f"qk{i}", [128, 8, 512], BF16) for i in range(2)]
    xg = [P.sb(f"xg{i}", [128, 4, 512], F32) for i in range(2)]
    fsb = P.sb("fsb", [128, 2, 512], BF16)
    vas = [P.sb(f"vas{i}", [128, 4, 1024], BF16) for i in range(2)]
    gsb = [P.sb(f"gsb{i}", [128, 4, 512], BF16) for i in range(2)]
    for i in range(2):
        P.op("pool", lambda e, i=i: e.memset(vas[i][:], 1.0), w=[("vas", i)])
    srcv = C.XT.rearrange("(m p) t -> p m t", p=128)
    QKv = C.QK.rearrange("(c p) t -> p c t", p=128)
    XGv = C.XG.rearrange("(c p) t -> p c t", p=128)
    cols = [c * 128 for c in range(8)] + [1536 + c * 128 for c in range(6)]

    def load(ti):
        t0, n, s = tiles[ti]
        P.dma("sp", xb[ti % 2][:, :, :n], srcv[:, :, t0:t0 + n], w=[(f"xb{ti % 2}", m) for m in range(8)],
              stream=f"xl{ti % 2}")

    def tile_body(ti, t0, n, s):
        X = xb[ti % 2]
        xk = f"xb{ti % 2}"
        b2 = ti % 2
        for m in range(8):
            s1p, sh, gs = mod_aps(C, l, j, m, s)
            P.op("act", lambda e, m=m, s1p=s1p, sh=sh: e.activation(
                out=h[:, m, :n], in_=X[:, m, :n], func=AF.Identity, scale=s1p, bias=sh),
                r=[(xk, m), "S1P", "M"], w=[("h", m)])
        for ci, col in enumerate(cols):
            bank = ci % 4
            for k in range(8):
                P.op("pe", lambda e, k=k, col=col, bank=bank: e.matmul(
                    C.psb[bank][:, :n], lhsT=win[:, k, col:col + 128], rhs=h[:, k, :n],
                    start=(k == 0), stop=(k == 7)), r=[("win", k), ("h", k)], w=[PS(bank)])
            if ci < 4:
                P.op("act", lambda e, ci=ci, bank=bank: e.activation(
                    out=qk[b2][:, ci, :n], in_=C.psb[bank][:, :n], func=AF.Identity, scale=0.125),
                    w=[PS(bank), ("qk", b2)])
            elif ci < 8:
                P.op("dve", lambda e, ci=ci, bank=bank: e.tensor_copy(out=qk[b2][:, ci, :n], in_=C.psb[bank][:, :n]),
                     w=[PS(bank), ("qk", b2)])
            elif ci < 12:
                eng = "act" if ci % 2 == 0 else "dve"
                if eng == "act":
                    P.op("act", lambda e, ci=ci, bank=bank: e.activation(
                        out=xg[b2][:, ci - 8, :n], in_=C.psb[bank][:, :n], func=AF.Identity),
                        w=[PS(bank), ("xg", b2)])
                else:
                    P.op("dve", lambda e, ci=ci, bank=bank: e.tensor_copy(
                        out=xg[b2][:, ci - 8, :n], in_=C.psb[bank][:, :n]), w=[PS(bank), ("xg", b2)])
            else:
                P.op("dve", lambda e, ci=ci, bank=bank: e.tensor_copy(out=fsb[:, ci - 12, :n], in_=C.psb[bank][:, :n]),
                     w=[PS(bank), "fsb"])
        nsub = n // 128
        for sub in range(nsub):
            bank = 4 + sub % 2
            for k in range(8):
                P.op("pe", lambda e, k=k, sub=sub, bank=bank: e.matmul(
                    C.psb[bank][:, :], lhsT=h[:, k, sub * 128:(sub + 1) * 128], rhs=win[:, k, 1024:1536],
                    start=(k == 0), stop=(k == 7)), r=[("win", k), ("h", k)], w=[PS(bank)])
            pv = C.psb[bank][:, :].rearrange("p (hp two d) -> p hp two d", two=2, d=64)
            vv = vas[b2][:, sub, :].rearrange("p (hp two d) -> p hp two d", two=2, d=128)
            P.op("act", lambda e, pv=pv, vv=vv: e.activation(out=vv[:, :, 0, 0:64], in_=pv[:, :, 0, :], func=AF.Identity),
                 w=[PS(bank), ("vas", b2)])
            P.op("dve", lambda e, pv=pv, vv=vv: e.tensor_copy(out=vv[:, :, 1, 64:128], in_=pv[:, :, 1, :]),
                 w=[PS(bank), ("vas", b2)])
        for sub in range(nsub):
            bank = 6
            for kc in range(2):
                P.op("pe", lambda e, kc=kc, sub=sub, bank=bank: e.matmul(
                    C.psb[bank][:, :], lhsT=fsb[:, kc, sub * 128:(sub + 1) * 128], rhs=BD[:, kc, :],
                    start=(kc == 0), stop=(kc == 1)), r=["fsb", "BD"], w=[PS(bank)])
            if sub % 2 == 0:
                P.op("act", lambda e, sub=sub, bank=bank: e.activation(out=gsb[b2][:, sub, :], in_=C.psb[bank][:, :],
                                                                       func=AF.Identity), w=[PS(bank), ("gsb", b2)])
            else:
                P.op("dve", lambda e, sub=sub, bank=bank: e.tensor_copy(out=gsb[b2][:, sub, :], in_=C.psb[bank][:, :]),
                     w=[PS(bank), ("gsb", b2)])
        P.dma("sp", QKv[:, :, t0:t0 + n], qk[b2][:, :, :n], r=[("qk", b2)], stream=f"sq{b2}")
        P.dma("sp", XGv[:, :, t0:t0 + n], xg[b2][:, :, :n], r=[("xg", b2)], stream=f"sx{b2}")
        P.dma("sp", C.VA[t0:t0 + n, :].rearrange("(s p) c -> p s c", p=128), vas[b2][:, :nsub, :],
              r=[("vas", b2)], stream=f"sv{b2}")
        P.dma("sp", C.G[t0:t0 + n, :].rearrange("(s p) c -> p s c", p=128), gsb[b2][:, :nsub, :],
              r=[("gsb", b2)], stream=f"sg{b2}")

    nxt = (l + 1 < n_layers)
    if nxt:
        abufs = adaln_alloc(P)
    load(0)
    for ti, (t0, n, s) in enumerate(tiles):
        if ti + 1 < len(tiles):
            load(ti + 1)
        tile_body(ti, t0, n, s)
        if nxt and ti < 10:
            adaln_step(P, C, l + 1, abufs, ti, 7)
    P.release(mk)


def attn_phase(P, C, l, need_ctx):
    P.barrier()
    mk = P.mark()
    Tbi = P.sb("Tbi", [128, 8, 1920], BF16)
    Tbf = P.sb("Tbf", [128, 8, 1920], BF16)
    rm = [P.sb(f"rm{i}", [128, 8, 512], BF16) for i in range(2)]
    kTc = P.sb("kTc", [128, 4, 256], BF16)
    vac = P.sb("vac", [128, 2, 1024], BF16)
    mk2 = P.mark()
    cmi = P.sb("cmi", [128, 1920], F32)
    cmf = P.sb("cmf", [128, 1920], F32)
    stg = [P.sb(f"stg{i}", [128, 1920], F32) for i in range(2)]
    P.dma("sp", cmi[:], C.cmask[0], w=["cmi"], stream="m0")
    P.dma("sp", cmf[:], C.cmask[1], w=["cmf"], stream="m1")
    for i in range(2):
        P.dma("pool", rm[i][:].rearrange("p a c -> p (a c)"), C.rmask[i], w=[("rm", i)], stream=f"w{i}")
    for hh in range(8):
        def tb(hh=hh):
            P.dma("sp", stg[hh % 2][:], C.rpbT[l, hh], w=[("stg", hh % 2)], stream=f"m{2 + hh % 2}")
            P.op("dve", lambda e: e.tensor_tensor(out=Tbi[:, hh, :], in0=stg[hh % 2][:], in1=cmi[:], op=ALU.add),
                 r=[("stg", hh % 2), "cmi"], w=["Tbi"])
            P.op("dve", lambda e: e.tensor_tensor(out=Tbf[:, hh, :], in0=stg[hh % 2][:], in1=cmf[:], op=ALU.add),
                 r=[("stg", hh % 2), "cmf"], w=["Tbf"])
        tb()
    QKv = C.QK.rearrange("(c p) t -> p c t", p=128)
    P.dma("sp", kTc[:], QKv[:, 4:8, 0:256], w=["kTc"], stream="m4")
    P.dma("sp", vac[:], C.VA[0:256, :].rearrange("(a p) c -> p a c", p=128), w=["vac"], stream="m5")
    P.barrier()
    P.release(mk2)
    qT = [P.sb(f"qT{i}", [128, 4, 512], BF16) for i in range(2)]
    kT = [P.sb(f"kT{i}", [128, 4, 1024], BF16) for i in range(2)]
    va = [P.sb(f"va{i}", [128, 8, 1024], BF16) for i in range(2)]
    NS = 6
    SBK = [0, 1, 2, 3, 6, 7]
    PT = [P.sb(f"PT{i}", [128, 512], BF16) for i in range(NS)]
    rd = [P.sb(f"rd{i}", [128, 512], F32) for i in range(2)]
    oc = [P.sb(f"oc{i}", [128, 4, 512], BF16) for i in range(2)]
    CATv = C.CAT.rearrange("(c p) t -> p c t", p=128)

    blocks = []
    if need_ctx:
        blocks.append(("ctx", 0, 256, None, None))
    for b in range(16):
        a0 = min(max(4 * b - 2, 0), 56)
        blocks.append(("lat", 256 + 512 * b, 512, b, a0))

    def load(bi):
        kind, q0, n, b, a0 = blocks[bi]
        P.dma("sp", qT[bi % 2][:, :, :n], QKv[:, 0:4, q0:q0 + n], w=[("qT", bi % 2)], stream=f"q{bi % 2}")
        if kind == "lat":
            k0 = 256 + 128 * a0
            P.dma("sp", kT[bi % 2][:], QKv[:, 4:8, k0:k0 + 1024], w=[("kT", bi % 2)], stream=f"k{bi % 2}")
            P.dma("sp", va[bi % 2][:], C.VA[k0:k0 + 1024, :].rearrange("(a p) c -> p a c", p=128),
                  w=[("va", bi % 2)], stream=f"v{bi % 2}")

    state = {"sidx": 0}

    def block_body(bi):
        kind, q0, n, b, a0 = blocks[bi]
        b2 = bi % 2
        chunks = [("ctx", 0, 0, n), ("ctx", 1, 0, n)]
        if kind == "lat":
            if b == 0:
                chunks += [("loc", ai, 0, n) for ai in range(0, 6)]
            elif b == 15:
                chunks += [("loc", ai, 0, n) for ai in range(2, 8)]
            else:
                for ai in range(8):
                    ilo, ihi = max(0, 2 * ai - 7), min(7, 2 * ai + 1)
                    chunks.append(("loc", ai, 64 * ilo, 64 * (ihi + 1)))
        items = [(hh, ch) for hh in range(8) for ch in chunks]
        nch = len(chunks)
        base = state["sidx"]
        state["sidx"] += len(items)
        edge = b in (0, 15)

        def emit_S(idx):
            hh, ch = items[idx]
            hc, pb = hh // 2, 64 * (hh % 2)
            si = (base + idx) % NS
            sb_ = SBK[si]
            c0, c1 = ch[2], ch[3]
            q_ap = qT[b2][pb:pb + 64, hc, c0:c1]
            if ch[0] == "ctx":
                ci = ch[1]
                P.op("pe", lambda e: e.matmul(C.psb[sb_][:, c0:c1], lhsT=kTc[pb:pb + 64, hc, ci * 128:(ci + 1) * 128],
                                              rhs=q_ap, start=True, stop=True),
                     r=["kTc", ("qT", b2)], w=[PS(sb_)])
            else:
                ai = ch[1]
                a = a0 + ai
                e0 = 8 * b - 2 * a + 14
                tb_ = Tbf if edge else Tbi
                P.op("pe", lambda e: e.matmul(C.psb[sb_][:, c0:c1], lhsT=kT[b2][pb:pb + 64, hc, ai * 128:(ai + 1) * 128],
                                              rhs=q_ap, start=True, stop=False),
                     r=[("kT", b2), ("qT", b2)], w=[PS(sb_)])
                P.op("pe", lambda e: e.matmul(C.psb[sb_][:, c0:c1], lhsT=C.identb[:, :],
                                              rhs=tb_[:, hh, e0 * 64 + c0:e0 * 64 + c1], start=False, stop=(not edge)),
                     r=["identb", "Tbf" if edge else "Tbi"], w=[PS(sb_)])
                if edge:
                    ri = 0 if b == 0 else 1
                    P.op("pe", lambda e: e.matmul(C.psb[sb_][:, c0:c1], lhsT=C.identb[:, :], rhs=rm[ri][:, ai, c0:c1],
                                                  start=False, stop=True), r=["identb", ("rm", ri)], w=[PS(sb_)])

        def emit_exp(idx):
            hh, ch = items[idx]
            si = (base + idx) % NS
            sb_ = SBK[si]
            c0, c1 = ch[2], ch[3]
            P.op("act", lambda e: e.activation(out=PT[si][:, c0:c1], in_=C.psb[sb_][:, c0:c1], func=AF.Exp),
                 w=[PS(sb_), ("PT", si)])

        def emit_PV(idx):
            hh, ch = items[idx]
            hc, pb = hh // 2, 64 * (hh % 2)
            si = (base + idx) % NS
            ob = 4 + hh % 2
            c0, c1 = ch[2], ch[3]
            ci_in_head = idx % nch
            if ch[0] == "ctx":
                lhs = vac[:, ch[1], hh * 128:(hh + 1) * 128]
                rk = ["vac", ("PT", si)]
            else:
                lhs = va[b2][:, ch[1], hh * 128:(hh + 1) * 128]
                rk = [("va", b2), ("PT", si)]
            P.op("pe", lambda e: e.matmul(C.psb[ob][:, c0:c1], lhsT=lhs, rhs=PT[si][:, c0:c1],
                                          start=(ci_in_head == 0), stop=(ci_in_head == nch - 1)),
                 r=rk, w=[PS(ob)])
            if ci_in_head == nch - 1:
                r2 = hh % 2
                if hh % 2 == 0:
                    num, den = slice(0, 64), slice(64, 128)
                else:
                    num, den = slice(64, 128), slice(0, 64)
                P.op("dve", lambda e: e.reciprocal(out=rd[r2][num, :n], in_=C.psb[ob][den, :n]),
                     w=[PS(ob), ("rd", r2)])
                P.op("dve", lambda e: e.tensor_tensor(out=oc[b2][num, hc, :n], in0=C.psb[ob][num, :n],
                                                      in1=rd[r2][num, :n], op=ALU.mult),
                     r=[("rd", r2)], w=[PS(ob), ("oc", b2)])

        AHEAD = 4
        for q_ in range(min(AHEAD, len(items))):
            emit_S(q_)
        for idx in range(len(items)):
            emit_exp(idx)
            if idx + AHEAD < len(items):
                emit_S(idx + AHEAD)
            emit_PV(idx)
        P.dma("sp", CATv[:, 0:4, q0:q0 + n], oc[b2][:, :, :n], r=[("oc", b2)], stream=f"so{b2}")

    load(0)
    for bi in range(len(blocks)):
        if bi + 1 < len(blocks):
            load(bi + 1)
        block_body(bi)
    P.release(mk)


def lru_phase(P, C, l):
    P.barrier()
    mk = P.mark()
    W = T + 8
    SEG = 1024
    XP = P.sb("XP", [128, W], F32)
    XC = P.sb("XC", [128, T], F32)
    XCB = P.sb("XCB", [128, T], BF16)
    HF = P.sb("HF", [128, T], F32)
    HB = P.sb("HB", [128, T], F32)
    half = P.sb("half", [128, 1], F32)
    LV = P.sb("LV", [128, 22], F32)
    NK = P.sb("NK", [128, 4], F32)
    NK2 = P.sb("NK2", [128, 4], F32)
    HBA = P.sb("HBA", [128, 8], F32)
    tiny = P.sb("tiny", [128, 1], F32)
    Wbf = P.sb("Wbf", [128, 8, 128], BF16)
    mkw = P.mark()
    Wst = P.sb("Wst", [128, 8, 128], F32)
    P.dma("sp", LV[:], C.lruv[:, l * 22:(l + 1) * 22], w=["LV"], stream="m0")
    P.op("dve", lambda e: e.memset(tiny[:], 1e-20), w=["tiny"])
    P.op("dve", lambda e: e.memset(half[:], 0.5), w=["half"])
    P.op("pool", lambda e: e.memset(Wst[:], 0.0), w=["Wst"])
    si = 0
    for d in range(2):
        for gi, src in enumerate((C.lru_wa, C.lru_wx)):
            for cc in range(2):
                idx = (d * 2 + gi) * 2 + cc
                for h2 in range(2):
                    P.dma("sp", Wst[h2 * 64:(h2 + 1) * 64, idx, h2 * 64:(h2 + 1) * 64], src[l, d, 2 * cc + h2],
                          w=["Wst"], stream=f"m{1 + si % 4}")
                    si += 1
    P.op("dve", lambda e: e.tensor_copy(out=Wbf[:], in_=Wst[:]), r=["Wst"], w=["Wbf"])
    P.barrier()
    P.release(mkw)
    Rb = [[P.sb(f"Rb{d}{k}", [128, SEG], F32) for k in range(2)] for d in range(2)]
    Ib = [[P.sb(f"Ib{d}{k}", [128, SEG], F32) for k in range(2)] for d in range(2)]
    Tb = [[P.sb(f"Tb{d}{k}", [128, SEG], F32) for k in range(2)] for d in range(2)]
    for cc in range(2):
        for d in range(2):
            def kap(cc=cc, d=d):
                lam = LV[:, cc * 11 + 7 + 3 * d: cc * 11 + 8 + 3 * d]
                o = NK[:, cc * 2 + d: cc * 2 + d + 1]
                o2 = NK2[:, cc * 2 + d: cc * 2 + d + 1]
                P.op("act", lambda e: e.activation(out=o2, in_=lam, func=AF.Exp, scale=-1.0), r=["LV"], w=["NK2"])
                P.op("act", lambda e: e.activation(out=o2, in_=o2, func=AF.Ln, bias=1.0), w=["NK2"])
                P.op("dve", lambda e: e.tensor_scalar(out=o, in0=o2, scalar1=-4.0, scalar2=None, op0=ALU.mult),
                     r=["NK2"], w=["NK"])
                for gi in range(2):
                    bsrc = LV[:, cc * 11 + 5 + 3 * d + gi: cc * 11 + 6 + 3 * d + gi]
                    bo = HBA[:, (cc * 2 + d) * 2 + gi:(cc * 2 + d) * 2 + gi + 1]
                    P.op("dve", lambda e, bsrc=bsrc, bo=bo: e.tensor_scalar(out=bo, in0=bsrc, scalar1=0.5, scalar2=None,
                                                                            op0=ALU.mult), r=["LV"], w=["HBA"])
            kap()
    segs = [(0, 256)] + [(256 + SEG * i, SEG) for i in range(8)]
    XGv = C.XG
    CATv = C.CAT

    def rev(ap):
        nn = ap.shape[-1]
        return bass.AP(ap.tensor, ap.offset + (nn - 1), [list(ap.ap[0]), [-1, nn]])

    def per_cc(cc):
        lv = lambda v: LV[:, cc * 11 + v: cc * 11 + v + 1]
        P.op("pool", lambda e: e.memset(XP[:, 0:2], 0.0), w=["XP"])
        P.op("pool", lambda e: e.memset(XP[:, 258:261], 0.0), w=["XP"])
        P.op("pool", lambda e: e.memset(XP[:, 8453:8456], 0.0), w=["XP"])
        P.dma("sp", XP[:, 2:258], XGv[cc * 128:(cc + 1) * 128, 0:256], w=["XP"], stream="l0")
        P.dma("sp", XP[:, 261:8453], XGv[cc * 128:(cc + 1) * 128, 256:T], w=["XP"], stream="l1")
        for si_, (s0, sn) in enumerate(segs):
            def cv(si_=si_, s0=s0, sn=sn):
                i0 = s0 if s0 < 256 else s0 + 3
                P.op("dve", lambda e: e.tensor_scalar(out=XC[:, s0:s0 + sn], in0=XP[:, i0:i0 + sn], scalar1=lv(0),
                                                      scalar2=lv(4), op0=ALU.mult, op1=ALU.add),
                     r=["XP", "LV"], w=[("XC", si_)])
                for jx in range(1, 4):
                    P.op("dve", lambda e, jx=jx: e.scalar_tensor_tensor(
                        out=XC[:, s0:s0 + sn], in0=XP[:, i0 + jx:i0 + jx + sn], scalar=lv(jx), in1=XC[:, s0:s0 + sn],
                        op0=ALU.mult, op1=ALU.add), r=["XP", "LV"], w=[("XC", si_)])
                P.op("act", lambda e: e.activation(out=XCB[:, s0:s0 + sn], in_=XC[:, s0:s0 + sn], func=AF.Identity),
                     r=[("XC", si_)], w=[("XCB", si_)])
            cv()

        cnt = [0, 0]
        prev = [None, None]

        def seg_step(d, si_):
            s0, sn = segs[si_]
            k = cnt[d] % 2
            cnt[d] += 1
            R_, I_, T_ = Rb[d][k], Ib[d][k], Tb[d][k]
            rk, ik, tk = ("Rb", d, k), ("Ib", d, k), ("Tb", d, k)
            nk = NK[:, cc * 2 + d: cc * 2 + d + 1]
            for sub in range(0, sn, 512):
                n = min(512, sn - sub)
                for gi, dst, dk in ((0, R_, rk), (1, I_, ik)):
                    def gate(sub=sub, n=n, gi=gi, dst=dst, dk=dk):
                        bank = d * 2 + gi
                        idx = (d * 2 + gi) * 2 + cc
                        hb = HBA[:, (cc * 2 + d) * 2 + gi:(cc * 2 + d) * 2 + gi + 1]
                        P.op("pe", lambda e: e.matmul(C.psb[bank][:, :n], lhsT=Wbf[:, idx, :],
                                                      rhs=XCB[:, s0 + sub:s0 + sub + n], start=True, stop=True),
                             r=["Wbf", ("XCB", si_)], w=[PS(bank)])
                        P.op("act", lambda e: e.activation(out=dst[:, sub:sub + n], in_=C.psb[bank][:, :n],
                                                           func=AF.Tanh, scale=0.5, bias=hb),
                             r=["HBA"], w=[PS(bank), dk])
                    gate()
            P.op("act", lambda e: e.activation(out=R_[:, :sn], in_=R_[:, :sn], func=AF.Exp, scale=nk, bias=nk),
                 r=["NK"], w=[rk])
            P.op("pool", lambda e: e.tensor_tensor(out=T_[:, :sn], in0=R_[:, :sn], in1=R_[:, :sn], op=ALU.mult),
                 r=[rk], w=[tk])
            P.op("dve", lambda e: e.tensor_scalar(out=T_[:, :sn], in0=T_[:, :sn], scalar1=-0.25, scalar2=0.25 + 1e-20,
                                                  op0=ALU.mult, op1=ALU.add), w=[tk])
            return lambda: seg_step_b(d, si_, k)

        def seg_step_b(d, si_, k):
            s0, sn = segs[si_]
            R_, I_, T_ = Rb[d][k], Ib[d][k], Tb[d][k]
            rk, ik, tk = ("Rb", d, k), ("Ib", d, k), ("Tb", d, k)
            P.op("act", lambda e: e.activation(out=T_[:, :sn], in_=T_[:, :sn], func=AF.Sqrt, bias=tiny[:, 0:1]),
                 r=["tiny"], w=[tk])
            P.op("dve", lambda e: e.scalar_tensor_tensor(out=I_[:, :sn], in0=I_[:, :sn], scalar=1.0, in1=T_[:, :sn],
                                                         op0=ALU.add, op1=ALU.mult), r=[tk], w=[ik])
            P.op("dve", lambda e: e.tensor_tensor(out=I_[:, :sn], in0=I_[:, :sn], in1=XC[:, s0:s0 + sn], op=ALU.mult),
                 r=[("XC", si_)], w=[ik])
            if d == 0:
                init = 0.0 if prev[0] is None else HF[:, prev[0][0] + prev[0][1] - 1:prev[0][0] + prev[0][1]]
                rr = [rk, ik] + ([("HF", prev[0][2])] if prev[0] is not None else [])
                P.op("dve", lambda e: e.tensor_tensor_scan(out=HF[:, s0:s0 + sn], data0=R_[:, :sn], data1=I_[:, :sn],
                                                           initial=init, op0=ALU.mult, op1=ALU.add),
                     r=rr, w=[("HF", si_)])
            else:
                init = 0.0 if prev[1] is None else HB[:, prev[1][0]:prev[1][0] + 1]
                rr = [rk, ik] + ([("HB", prev[1][2])] if prev[1] is not None else [])
                P.op("dve", lambda e: e.tensor_tensor_scan(out=rev(HB[:, s0:s0 + sn]), data0=rev(R_[:, :sn]),
                                                           data1=rev(I_[:, :sn]), initial=init,
                                                           op0=ALU.mult, op1=ALU.add),
                     r=rr, w=[("HB", si_)])
            prev[d] = (s0, sn, si_)

        order_f = list(range(9))
        order_b = [0] + list(range(8, 0, -1))
        for q in range(9):
            fb = seg_step(0, order_f[q])
            bb = seg_step(1, order_b[q])
            fb()
            bb()
        GRb = XP
        P.dma("sp", GRb[:, 0:T], XGv[256 + cc * 128:256 + (cc + 1) * 128, :], w=["XP"], stream="l2")
        for si_, (s0, sn) in enumerate(segs):
            def gl(si_=si_, s0=s0, sn=sn):
                k = si_ % 2
                U = Rb[0][k]
                S_ = Ib[0][k]
                uk, sk = ("Rb", 0, k), ("Ib", 0, k)
                g_ = GRb[:, s0:s0 + sn]
                P.op("act", lambda e: e.activation(out=U[:, :sn], in_=g_, func=AF.Square), r=["XP"], w=[uk])
                P.op("dve", lambda e: e.tensor_scalar(out=U[:, :sn], in0=U[:, :sn], scalar1=0.044715, scalar2=1.0,
                                                      op0=ALU.mult, op1=ALU.add), w=[uk])
                P.op("pool", lambda e: e.tensor_tensor(out=U[:, :sn], in0=U[:, :sn], in1=g_, op=ALU.mult),
                     r=["XP"], w=[uk])
                P.op("act", lambda e: e.activation(out=U[:, :sn], in_=U[:, :sn], func=AF.Tanh, scale=0.7978845608028654),
                     w=[uk])
                P.op("dve", lambda e: e.scalar_tensor_tensor(out=U[:, :sn], in0=U[:, :sn], scalar=1.0, in1=g_,
                                                             op0=ALU.add, op1=ALU.mult), r=["XP"], w=[uk])
                P.op("pool", lambda e: e.tensor_tensor(out=S_[:, :sn], in0=HF[:, s0:s0 + sn], in1=HB[:, s0:s0 + sn],
                                                       op=ALU.add), r=[("HF", si_), ("HB", si_)], w=[sk])
                P.op("dve", lambda e: e.scalar_tensor_tensor(out=XCB[:, s0:s0 + sn], in0=U[:, :sn], scalar=0.5,
                                                             in1=S_[:, :sn], op0=ALU.mult, op1=ALU.mult),
                     r=[uk, sk], w=[("XCB", si_)])
            gl()
        P.dma("sp", CATv[512 + cc * 128:512 + (cc + 1) * 128, :], XCB[:, :], r=[("XCB", q) for q in range(9)],
              w=["XCBst"], stream="l3")

    for cc in range(2):
        per_cc(cc)
    P.release(mk)


def fno_phase(P, C, l, need_ctx):
    P.barrier()
    mk = P.mark()
    cw = P.sb("cw", [128, 128], BF16)
    sw = P.sb("sw", [128, 128], BF16)
    nsw = P.sb("nsw", [128, 128], BF16)
    Wt = P.sb("Wt", [128, 128, 64], BF16)
    t256 = P.sb("t256", [128, 2, 2, 256], BF16)
    P.dma("pool", cw[:], C.cw128, w=["cw"], stream="w0")
    P.dma("pool", sw[:], C.sw128, w=["sw"], stream="w1")
    P.dma("pool", nsw[:], C.nsw128, w=["nsw"], stream="w2")
    P.dma("pool", Wt[:].rearrange("p a b -> p (a b)"), C.wtC, w=["Wt"], stream="w3")
    P.dma("pool", t256[:].rearrange("p a b c -> p (a b c)"), C.t256, w=["t256"], stream="w4")
    Gb = [P.sb(f"Gb{i}", [128, 16, 512], BF16) for i in range(2)]
    Ab = [[P.sb(f"Ab{i}{ri}", [128, 16, 256], BF16) for ri in range(2)] for i in range(2)]
    Glat = C.G[256:T, :].rearrange("(n1 n2) c -> n1 n2 c", n2=64)
    sA = 1.0
    for blk in range(4):
        def stA(blk=blk):
            b2 = blk % 2
            P.dma("sp", Gb[b2][:], Glat[:, blk * 16:(blk + 1) * 16, :], w=[("Gb", b2)], stream=f"g{b2}")
            for pair in range(8):
                def pr(pair=pair):
                    gc = Gb[b2][:, 2 * pair:2 * pair + 2, 0:256]
                    gs_ = Gb[b2][:, 2 * pair:2 * pair + 2, 256:512]
                    bre = (pair % 2) * 2
                    bim = bre + 1
                    P.op("pe", lambda e: e.matmul(C.psb[bre][:, :], lhsT=cw[:, :], rhs=gc, start=True, stop=False),
                         r=["cw", ("Gb", b2)], w=[PS(bre)])
                    P.op("pe", lambda e: e.matmul(C.psb[bre][:, :], lhsT=nsw[:, :], rhs=gs_, start=False, stop=True),
                         r=["nsw", ("Gb", b2)], w=[PS(bre)])
                    P.op("pe", lambda e: e.matmul(C.psb[bim][:, :], lhsT=cw[:, :], rhs=gs_, start=True, stop=False),
                         r=["cw", ("Gb", b2)], w=[PS(bim)])
                    P.op("pe", lambda e: e.matmul(C.psb[bim][:, :], lhsT=sw[:, :], rhs=gc, start=False, stop=True),
                         r=["sw", ("Gb", b2)], w=[PS(bim)])
                    P.op("act", lambda e: e.activation(
                        out=Ab[b2][0][:, 2 * pair:2 * pair + 2, :].rearrange("p a c -> p (a c)"),
                        in_=C.psb[bre][:, :], func=AF.Identity), w=[PS(bre), ("Ab", b2, 0)])
                    P.op("dve", lambda e: e.tensor_copy(
                        out=Ab[b2][1][:, 2 * pair:2 * pair + 2, :].rearrange("p a c -> p (a c)"),
                        in_=C.psb[bim][:, :]), w=[PS(bim), ("Ab", b2, 1)])
                pr()
            for ri in range(2):
                P.dma("sp", C.AB[ri, :, blk * 16:(blk + 1) * 16, :], Ab[b2][ri][:], r=[("Ab", b2, ri)],
                      w=[("AB", blk)], stream=f"a{b2}{ri}")
        stA()
    Ap = [P.sb(f"Ap{i}", [128, 32, 256], BF16) for i in range(2)]
    Yt = P.sb("Yt", [128, 2, 8192], BF16)
    ABv = C.AB.rearrange("ri k1 n2 c -> ri n2 k1 c")
    scl = 1.0 / float(np.sqrt(8192.0 * 64.0))
    for q in range(4):
        def stC(q=q):
            b2 = q % 2
            for ri in range(2):
                P.dma("sp", Ap[b2][ri * 64:(ri + 1) * 64, :, :], ABv[ri, :, q * 32:(q + 1) * 32, :],
                      r=[("AB", bb) for bb in range(4)], w=[("Ap", b2)], stream=f"p{b2}{ri}")
            for cc in range(2):
                for kb in range(4):
                    def grp(cc=cc, kb=kb):
                        bank = 4 + (cc * 4 + kb) % 4
                        pv = C.psb[bank][:, :].rearrange("p (k2 j) -> p k2 j", j=8)
                        for jx in range(8):
                            k1l = kb * 8 + jx
                            k1 = q * 32 + k1l
                            P.op("pe", lambda e, jx=jx, k1l=k1l, k1=k1: e.matmul(
                                pv[:, :, jx], lhsT=Ap[b2][:, k1l, cc * 128:(cc + 1) * 128], rhs=Wt[:, k1, :],
                                start=True, stop=True), r=[("Ap", b2), "Wt"], w=[PS(bank)])
                        k1b = q * 32 + kb * 8
                        yv = Yt[:, cc, :].rearrange("p (k2 k1) -> p k2 k1", k1=128)[:, :, k1b:k1b + 8]
                        if kb % 2 == 0:
                            P.op("act", lambda e: e.activation(out=yv, in_=pv, func=AF.Identity, scale=scl),
                                 w=[PS(bank), "Yt"])
                        else:
                            P.op("dve", lambda e: e.tensor_scalar(out=yv, in0=pv, scalar1=scl, scalar2=None,
                                                                  op0=ALU.mult), w=[PS(bank), "Yt"])
                    grp()
        stC()
    for cc in range(2):
        P.dma("sp", C.CAT[768 + cc * 128:768 + (cc + 1) * 128, 256:T], Yt[:, cc, :], r=["Yt"], stream=f"y{cc}")
    if need_ctx:
        Gc_ = P.sb("Gctx", [128, 2, 512], BF16)
        Ytc = P.sb("Ytc", [128, 2, 256], BF16)
        sclc = 1.0 / float(np.sqrt(256.0 * 64.0))
        P.dma("sp", Gc_[:], C.G[0:256, :].rearrange("(a p) c -> p a c", p=128), w=["Gctx"], stream="g0")
        for cc in range(2):
            def cx(cc=cc):
                bank = cc
                i = 0
                for nchk in range(2):
                    for part in range(2):
                        P.op("pe", lambda e, nchk=nchk, part=part, i=i: e.matmul(
                            C.psb[bank][:, 0:256], lhsT=Gc_[:, nchk, part * 256 + cc * 128: part * 256 + (cc + 1) * 128],
                            rhs=t256[:, nchk, part, :], start=(i == 0), stop=(i == 3)),
                            r=["Gctx", "t256"], w=[PS(bank)])
                        i += 1
                P.op("act", lambda e: e.activation(out=Ytc[:, cc, :], in_=C.psb[bank][:, 0:256], func=AF.Identity,
                                                   scale=sclc), w=[PS(bank), "Ytc"])
            cx()
        P.dma("sp", C.CAT.rearrange("(c p) t -> p c t", p=128)[:, 6:8, 0:256], Ytc[:], r=["Ytc"], stream="y2")
    P.release(mk)


def outproj_phase(P, C, l, tiles):
    j = 1
    P.barrier()
    mk = P.mark()
    wo = P.sb("wo", [128, 8, D], BF16)
    Wv = C.w_out[l].rearrange("(k p) n -> p k n", p=128)
    for k in range(8):
        P.dma("pool", wo[:, k, :], Wv[:, k, :], w=[("wo", k)], stream=f"w{k % 4}")
    xb = [P.sb(f"xb{i}", [128, 8, 512], F32) for i in range(3)]
    cb = [P.sb(f"cb{i}", [128, 8, 512], BF16) for i in range(2)]
    zb = [P.sb(f"zb{i}", [128, 512], BF16) for i in range(2)]
    zq = [P.sb(f"zq{i}", [128, 512], BF16) for i in range(2)]
    msq = [P.sb(f"msq{i}", [128, 512], F32) for i in range(2)]
    srcv = C.XT.rearrange("(m p) t -> p m t", p=128)
    catv = C.CAT.rearrange("(m p) t -> p m t", p=128)
    nt = len(tiles)

    def load(ti):
        t0, n, s = tiles[ti]
        P.dma("sp", xb[ti % 3][:, :, :n], srcv[:, :, t0:t0 + n], w=[(f"xb{ti % 3}", m) for m in range(8)],
              stream=f"xl{ti % 3}")
        P.dma("sp", cb[ti % 2][:, :, :n], catv[:, :, t0:t0 + n], w=[("cb", ti % 2)], stream=f"cl{ti % 2}")

    def resid_piece(ti, m):
        t0, n, s = tiles[ti]
        X = xb[ti % 3]
        xk = f"xb{ti % 3}"
        cbt = cb[ti % 2]
        bm, be = (6, 7) if ti % 2 == 0 else (2, 3)
        py = 4 + m % 2
        s1p, sh, gs = mod_aps(C, l, j, m, s)

        def stats(mm):
            P.op("pe", lambda e: e.matmul(C.psb[bm][:, :n], lhsT=C.onesb[:, :], rhs=zb[mm % 2][:, :n],
                                          start=(mm == 0), stop=(mm == 7)), r=["onesb", ("zb", mm % 2)], w=[PS(bm)])
            P.op("pe", lambda e: e.matmul(C.psb[be][:, :n], lhsT=C.onesb[:, :], rhs=zq[mm % 2][:, :n],
                                          start=(mm == 0), stop=(mm == 7)), r=["onesb", ("zq", mm % 2)], w=[PS(be)])
        for k in range(8):
            P.op("pe", lambda e, k=k: e.matmul(C.psb[py][:, :n], lhsT=wo[:, k, m * 128:(m + 1) * 128],
                                               rhs=cbt[:, k, :n], start=(k == 0), stop=(k == 7)),
                 r=[("wo", k), ("cb", ti % 2)], w=[PS(py)])
        P.op("dve", lambda e: e.scalar_tensor_tensor(
            out=X[:, m, :n], in0=C.psb[py][:, :n], scalar=gs, in1=X[:, m, :n], op0=ALU.mult, op1=ALU.add),
            r=["GS"], w=[PS(py), (xk, m)])
        P.op("act", lambda e: e.activation(out=zb[m % 2][:, :n], in_=X[:, m, :n], func=AF.Identity),
             r=[(xk, m)], w=[("zb", m % 2)])
        P.op("act", lambda e: e.activation(out=zq[m % 2][:, :n], in_=X[:, m, :n], func=AF.Square),
             r=[(xk, m)], w=[("zq", m % 2)])
        if m > 0:
            stats(m - 1)
        if m == 7:
            stats(7)

    def finish(ti, inter):
        t0, n, s = tiles[ti]
        X = xb[ti % 3]
        xk = f"xb{ti % 3}"
        bm, be = (6, 7) if ti % 2 == 0 else (2, 3)
        ln_tail(P, C, l, j, X, xk, n, bm, be, msq[ti % 2], f"o{ti % 2}", inter, every=True)
        P.dma("sp", srcv[:, :, t0:t0 + n], X[:, :, :n], r=[(xk, m) for m in range(8)], stream=f"xs{ti % 3}")

    load(0)
    if nt > 1:
        load(1)
    for m in range(8):
        resid_piece(0, m)
    for ti in range(nt):
        if ti + 2 < nt:
            load(ti + 2)
        inter = []
        if ti + 1 < nt:
            inter = [(lambda m=m: resid_piece(ti + 1, m)) for m in range(8)]
        finish(ti, inter)
    P.release(mk)


def build(n_layers=DEPTH, stop_after=None, dbg=False, tiles=None, only=None):
    nc = bass.Bass("TRN2", target_bir_lowering=False)
    C = Ctx()
    P = Prog(nc)
    declare(nc, C, n_layers, dbg)
    prologue(P, C, n_layers)
    tl = tiles if tiles is not None else tiles_all()
    XTv = C.XT.rearrange("(m p) t -> p m t", p=128)

    def to_xt(tiles):
        return lambda ti: XTv[:, :, tiles[ti][0]:tiles[ti][0] + tiles[ti][1]]

    stages = ["ffn1", "inproj", "attn", "lru", "fno", "mix", "ffn2"]
    outv = C.out.rearrange("(m p) t -> p m t", p=128)

    def run_layers():
        for l in range(n_layers):
            last = (l == DEPTH - 1)
            tl2 = tl if not last else tl[1:]

            def dst_last(ti, tl2=tl2):
                t0, n, s_ = tl2[ti]
                return outv[:, :, t0 - NCTX:t0 - NCTX + n]
            seq = [
                ("ffn1", lambda: ffn_phase(P, C, l, 0, C.xin if l == 0 else C.XT, to_xt(tl), tl)),
                ("inproj", lambda: inproj_phase(P, C, l, tl, n_layers)),
                ("attn", lambda: attn_phase(P, C, l, not last)),
                ("lru", lambda: lru_phase(P, C, l)),
                ("fno", lambda: fno_phase(P, C, l, not last)),
                ("mix", lambda: outproj_phase(P, C, l, tl2)),
                ("ffn2", lambda: ffn_phase(P, C, l, 1, C.XT, dst_last if last else to_xt(tl2), tl2)),
            ]
            if l == 0 and only is not None and "ffn1" not in only:
                P.dma("sp", C.XT, C.xin, stream="cp")
            for name, fn in seq:
                if only is None or name in only:
                    fn()
                if stop_after == (l, name):
                    return
    run_layers()
    if dbg:
        P.barrier()
        P.dma("sp", C.dbg, C.XT, stream="dbg")
        P.dma("sp", C.dbg2, C.M[:], stream="dbg2")
    P.emit()
    C.P = P
    return nc, C


def host_inputs(inp, b):
    f = np.float32
    x, ctx = inp["x"], inp["ctx"]
    xin = np.ascontiguousarray(np.concatenate([ctx[b].T, x[b].T], axis=1), dtype=f)
    cv = np.stack([np.asarray(inp["c"][b]), np.asarray(inp["c_ctx"])], axis=-1)
    cvec = np.ascontiguousarray(cv.reshape(8, 128, 2).transpose(1, 0, 2).reshape(128, 16), dtype=f)
    return {"xin": xin, "cvec": cvec}


def host_shared(inp):
    f = np.float32
    ba = np.asarray(inp["b_ada"]).reshape(DEPTH, 72, 128).transpose(2, 0, 1)
    bada = np.ascontiguousarray(np.repeat(ba[:, :, :, None], 2, axis=3).reshape(128, DEPTH * 144), dtype=f)
    lng = np.ascontiguousarray(np.asarray(inp["ln_g"]).reshape(DEPTH * 3 * 8, 128).T, dtype=f)
    lnb = np.ascontiguousarray(np.asarray(inp["ln_b"]).reshape(DEPTH * 3 * 8, 128).T, dtype=f)
    sh = {"bada": bada, "lng": lng, "lnb": lnb, "ident": np.eye(128, dtype=f)}
    for k in ("w_ada", "ff1_gate", "ff1_up", "ff1_down", "ff2_gate", "ff2_up", "ff2_down", "w_in", "w_out"):
        sh[k] = np.ascontiguousarray(inp[k], dtype=f)
    sh.update(host_consts(inp))
    return sh


def host_consts(inp):
    f = np.float32
    k64 = np.arange(64)
    a64 = 2 * np.pi * ((np.outer(k64, k64)) % 64) / 64.0
    C64, S64 = np.cos(a64), np.sin(a64)
    z = np.zeros((64, 64))
    c64bd = np.block([[C64, z], [z, C64]])
    s64bd = np.block([[S64, z], [z, S64]])
    n1 = np.arange(128)
    a128 = 2 * np.pi * ((np.outer(n1, n1)) % 128) / 128.0
    n2 = np.arange(64)
    kk = n1[:, None] + 128 * k64[None, :]
    ang = 2 * np.pi * ((n2[:, None, None] * kk[None]) % 8192) / 8192.0
    wtC = np.concatenate([np.cos(ang), -np.sin(ang)], axis=0).reshape(128, 128 * 64)
    n256 = np.arange(256)
    a256 = 2 * np.pi * ((np.outer(n256, n256)) % 256) / 256.0
    t256 = np.stack([np.cos(a256), -np.sin(a256)], axis=1)
    t256 = t256.reshape(2, 128, 2, 256).transpose(1, 0, 2, 3).reshape(128, 2 * 2 * 256)
    p = np.arange(128)
    krl, kc = p // 64, p % 64
    e = np.arange(30)
    qc = np.arange(64)
    d = krl[:, None] - (e[None, :] - 14)
    dr = d + 7
    dc = kc[:, None] - qc[None, :] + 15
    col0 = np.clip(qc - 8, 0, 48)
    colv = (kc[:, None] >= col0[None, :]) & (kc[:, None] < col0[None, :] + 16)
    drv = (dr >= 0) & (dr <= 14)
    rpb = np.asarray(inp["na_rpb"])
    dri = np.clip(dr, 0, 14)
    dci = np.clip(dc, 0, 30)
    gat = rpb[:, :, dri[:, :, None], dci[:, None, :]]
    okf = drv[:, :, None] & colv[:, None, :]
    rpbT = np.where(okf[None, None], gat, 0.0).reshape(DEPTH, 8, 128, 1920)
    oki = okf & ((d >= -4) & (d <= 3))[:, :, None]
    cmask = np.stack([np.where(oki, 0.0, NEG), np.where(okf, 0.0, NEG)], 0).reshape(2, 128, 1920)
    rm = np.zeros((2, 128, 8, 8, 64))
    for ri, (kr0, qr0) in enumerate(((0, 0), (112, 120))):
        for ai in range(8):
            for i in range(8):
                qr = qr0 + i
                rs = min(max(qr - 4, 0), 120)
                kr = kr0 + 2 * ai + krl
                ok = (kr >= rs) & (kr < rs + 8)
                rm[ri, :, ai, i, :] = np.where(ok, 0.0, NEG)[:, None]
    rmask = rm.reshape(2, 128, 8 * 512)
    L = DEPTH
    lv = np.zeros((L, 256, 11))
    lv[:, :, 0:4] = np.asarray(inp["lru_conv_w"]).transpose(0, 2, 1)
    lv[:, :, 4] = np.asarray(inp["lru_conv_b"])
    for dd in range(2):
        lv[:, :, 5 + 3 * dd] = np.asarray(inp["lru_ba"])[:, dd]
        lv[:, :, 6 + 3 * dd] = np.asarray(inp["lru_bx"])[:, dd]
        lv[:, :, 7 + 3 * dd] = np.asarray(inp["lru_lambda"])[:, dd]
    lruv = lv.reshape(L, 2, 128, 11).transpose(2, 0, 1, 3).reshape(128, L * 2 * 11)
    out = {"c64bd": c64bd, "s64bd": s64bd, "cw128": np.cos(a128), "sw128": np.sin(a128), "nsw128": -np.sin(a128),
           "wtC": wtC, "t256": t256, "rpbT": rpbT, "cmask": cmask, "rmask": rmask, "lruv": lruv}
    out = {k: np.ascontiguousarray(v, dtype=f) for k, v in out.items()}
    for k in ("lru_wa", "lru_wx", "fno_w"):
        out[k] = np.ascontiguousarray(inp[k], dtype=f)
    return out


_CACHE = {}


def kernel(**inputs):
    if "nc" not in _CACHE:
        _CACHE["nc"] = build()[0]
    nc = _CACHE["nc"]
    sh = host_shared(inputs)
    in_maps = []
    for b in range(8):
        d = dict(sh)
        d.update(host_inputs(inputs, b))
        in_maps.append(d)
    res = run_bass_kernel_spmd(nc, in_maps, core_ids=list(range(8)))
    out = np.stack([np.ascontiguousarray(r["out"].T) for r in res.results], axis=0)
    return out.astype(np.float32)
```
